# Optimizing a Trainium2 kernel written in Bass

```python
import jax, jax.numpy as jnp
from jax import lax
import numpy as np

D_MODEL = 4096
BATCH = 4
SEQ = 2048
DEPTH = 4

MEM_LEN = 256
HEAD_DIM = 128
ROPE_THETA = 10000.0
EPS = 1e-6
NEG = -1e30

SC_WIDTH = D_MODEL // 2
SC_KERNEL = 3

NSA_HEADS = 16
NSA_KV_GROUPS = 4
NSA_WIDTH = NSA_HEADS * HEAD_DIM
NSA_BRANCHES = 3
CMP_BLOCK = 32
CMP_STRIDE = 16
SLC_BLOCK = 64
SLC_TOP_N = 16
SLC_Q_BLOCK = 64
WINDOW = 512
Q_BLOCK = 128

X_HEADS = 4
X_WIDTH = X_HEADS * HEAD_DIM

N_BRANCHES = 3
IN_SPLITS = (SC_WIDTH, SC_WIDTH, SC_WIDTH, SC_WIDTH,
             NSA_WIDTH,
             NSA_BRANCHES * 2 * NSA_KV_GROUPS * HEAD_DIM,
             NSA_BRANCHES * NSA_HEADS,
             NSA_WIDTH,
             X_WIDTH, X_WIDTH,
             N_BRANCHES * D_MODEL)
IN_WIDTH = sum(IN_SPLITS)

kernel_name = 'hybrid_conv_nsa_mem_gated_merge'


def rmsnorm(x, g):
    xf = x.astype(jnp.float32)
    y = xf * lax.rsqrt(jnp.mean(xf * xf, axis=-1, keepdims=True) + EPS)
    return (y * g.astype(jnp.float32)).astype(x.dtype)


def rope(x, pos):
    half = HEAD_DIM // 2
    inv_freq = ROPE_THETA ** (-jnp.arange(half, dtype=jnp.float32) / half)
    ang = pos.astype(jnp.float32)[:, None] * inv_freq[None, :]
    cos = jnp.cos(ang)[None, :, None, :]
    sin = jnp.sin(ang)[None, :, None, :]
    xf = x.astype(jnp.float32)
    x1, x2 = xf[..., :half], xf[..., half:]
    return jnp.concatenate([x1 * cos - x2 * sin, x2 * cos + x1 * sin], axis=-1).astype(x.dtype)


def masked_softmax(s, mask):
    p = jax.nn.softmax(jnp.where(mask, s, NEG), axis=-1)
    return jnp.where(mask, p, 0.0)


def short_conv_mixer(a_h, a_b, a_c, conv_w, conv_b):
    seq = a_h.shape[1]
    u = a_c * a_h
    up = jnp.pad(u, ((0, 0), (SC_KERNEL - 1, 0), (0, 0)))
    y = conv_b
    for k in range(SC_KERNEL):
        y = y + conv_w[k] * up[:, k:k + seq]
    return a_b * y


def nsa_mixer(q, kv, gate_logits, cmp_pos, cmp_w1, cmp_w2):
    bsz, seq = q.shape[0], q.shape[1]
    G, R, Dh = NSA_KV_GROUPS, NSA_HEADS // NSA_KV_GROUPS, HEAD_DIM
    scale = Dh ** -0.5
    pos = jnp.arange(seq, dtype=jnp.int32)
    kv = kv.reshape(bsz, seq, NSA_BRANCHES, 2, G, Dh)
    qg = q.reshape(bsz, seq, G, R, Dh)
    qg_rot = rope(q, pos).reshape(bsz, seq, G, R, Dh)

    n_cmp = (seq - CMP_BLOCK) // CMP_STRIDE + 1
    cmp_start = jnp.arange(n_cmp, dtype=jnp.int32) * CMP_STRIDE
    tok = cmp_start[:, None] + jnp.arange(CMP_BLOCK, dtype=jnp.int32)[None, :]
    blk = kv[:, :, 0][:, tok]
    blk = blk + jnp.transpose(cmp_pos, (1, 0, 2))[:, :, None, :]
    blk = jnp.transpose(blk, (0, 1, 3, 4, 2, 5)).reshape(bsz, n_cmp, 2, G, CMP_BLOCK * Dh)
    hid = jax.nn.silu(jnp.einsum('bncgf,cfd->bncgd', blk, cmp_w1))
    kvc = jnp.einsum('bncgd,cde->bncge', hid, cmp_w2)
    k_cmp, v_cmp = kvc[:, :, 0], kvc[:, :, 1]
    s_cmp = jnp.einsum('bsgrd,bngd->bgrsn', qg, k_cmp).astype(jnp.float32) * scale
    cmp_mask = (cmp_start + CMP_BLOCK - 1)[None, :] <= pos[:, None]
    p_cmp = masked_softmax(s_cmp, cmp_mask)
    o_cmp = jnp.einsum('bgrsn,bngd->bsgrd', p_cmp.astype(v_cmp.dtype), v_cmp)

    n_slc = seq // SLC_BLOCK
    slc_start = jnp.arange(n_slc, dtype=jnp.int32) * SLC_BLOCK
    overlap = ((cmp_start[:, None] < slc_start[None, :] + SLC_BLOCK)
               & (cmp_start[:, None] + CMP_BLOCK > slc_start[None, :])).astype(jnp.float32)
    imp = jnp.einsum('bgrsn,nj->bgsj', p_cmp, overlap)
    blk_id = jnp.arange(n_slc, dtype=jnp.int32)[None, :]
    cur = (pos // SLC_BLOCK)[:, None]
    future = slc_start[None, :] > pos[:, None]
    forced = (blk_id == 0) | (blk_id == cur) | (blk_id == cur - 1)
    imp = jnp.where(future, -jnp.inf, jnp.where(forced, jnp.inf, imp))
    n_top = min(SLC_TOP_N, n_slc)
    _, sel = lax.top_k(imp, n_top)

    k_slc = rope(kv[:, :, 1, 0], pos)
    v_slc = kv[:, :, 1, 1]
    ks = k_slc.reshape(bsz, n_slc, SLC_BLOCK, G, Dh).transpose(0, 3, 1, 2, 4)
    vs = v_slc.reshape(bsz, n_slc, SLC_BLOCK, G, Dh).transpose(0, 3, 1, 2, 4)
    nq = seq // SLC_Q_BLOCK
    q_chunks = qg_rot.reshape(bsz, nq, SLC_Q_BLOCK, G, R, Dh).transpose(1, 0, 2, 3, 4, 5)
    sel_chunks = sel.reshape(bsz, G, nq, SLC_Q_BLOCK, n_top).transpose(2, 0, 1, 3, 4)
    pos_chunks = pos.reshape(nq, SLC_Q_BLOCK)
    bi = jnp.arange(bsz)[:, None, None, None]
    gi = jnp.arange(G)[None, :, None, None]
    n_keys = n_top * SLC_BLOCK

    def slc_block(args):
        qc, selc, tc = args
        kc = ks[bi, gi, selc].reshape(bsz, G, SLC_Q_BLOCK, n_keys, Dh)
        vc = vs[bi, gi, selc].reshape(bsz, G, SLC_Q_BLOCK, n_keys, Dh)
        key_pos = (selc[..., None] * SLC_BLOCK + jnp.arange(SLC_BLOCK, dtype=jnp.int32)).reshape(
            bsz, G, SLC_Q_BLOCK, n_keys)
        mask = (key_pos <= tc[None, None, :, None])[:, :, None]
        s = jnp.einsum('bqgrd,bgqmd->bgrqm', qc, kc).astype(jnp.float32) * scale
        p = masked_softmax(s, mask)
        return jnp.einsum('bgrqm,bgqmd->bqgrd', p.astype(vc.dtype), vc)

    o_slc = lax.map(slc_block, (q_chunks, sel_chunks, pos_chunks))
    o_slc = o_slc.transpose(1, 0, 2, 3, 4, 5).reshape(bsz, seq, G, R, Dh)

    k_win = rope(kv[:, :, 2, 0], pos)
    v_win = kv[:, :, 2, 1]
    kp = jnp.pad(k_win, ((0, 0), (WINDOW, 0), (0, 0), (0, 0)))
    vp = jnp.pad(v_win, ((0, 0), (WINDOW, 0), (0, 0), (0, 0)))
    nw = seq // Q_BLOCK
    span = WINDOW + Q_BLOCK
    qw = qg_rot.reshape(bsz, nw, Q_BLOCK, G, R, Dh).transpose(1, 0, 2, 3, 4, 5)
    starts = jnp.arange(nw, dtype=jnp.int32) * Q_BLOCK

    def win_block(args):
        qc, c0 = args
        kc = lax.dynamic_slice_in_dim(kp, c0, span, axis=1)
        vc = lax.dynamic_slice_in_dim(vp, c0, span, axis=1)
        q_pos = c0 + jnp.arange(Q_BLOCK, dtype=jnp.int32)
        k_pos = c0 - WINDOW + jnp.arange(span, dtype=jnp.int32)
        d = q_pos[:, None] - k_pos[None, :]
        mask = (d >= 0) & (d < WINDOW) & (k_pos[None, :] >= 0)
        s = jnp.einsum('bqgrd,bkgd->bgrqk', qc, kc).astype(jnp.float32) * scale
        p = masked_softmax(s, mask)
        return jnp.einsum('bgrqk,bkgd->bqgrd', p.astype(vc.dtype), vc)

    o_win = lax.map(win_block, (qw, starts))
    o_win = o_win.transpose(1, 0, 2, 3, 4, 5).reshape(bsz, seq, G, R, Dh)

    g = jax.nn.sigmoid(gate_logits.astype(jnp.float32)).astype(q.dtype).reshape(
        bsz, seq, NSA_HEADS, NSA_BRANCHES, 1)
    o = (g[:, :, :, 0] * o_cmp.reshape(bsz, seq, NSA_HEADS, Dh)
         + g[:, :, :, 1] * o_slc.reshape(bsz, seq, NSA_HEADS, Dh)
         + g[:, :, :, 2] * o_win.reshape(bsz, seq, NSA_HEADS, Dh))
    return o.reshape(bsz, seq, NSA_WIDTH)


def memory_attention(q, mem_kv):
    k, v = mem_kv[:, :, 0], mem_kv[:, :, 1]
    s = jnp.einsum('bshd,bmhd->bhsm', q, k).astype(jnp.float32) * (HEAD_DIM ** -0.5)
    p = jax.nn.softmax(s, axis=-1)
    return jnp.einsum('bhsm,bmhd->bshd', p.astype(v.dtype), v)


def hybrid_layer(x, mem, norm_g, w_in, conv_w, conv_b, cmp_pos, cmp_w1, cmp_w2,
                 mem_norm_g, w_mem_kv, w_up_a, w_up_b, w_up_x, w_out):
    bsz, seq, _ = x.shape
    h = rmsnorm(x, norm_g)
    proj = h @ w_in
    offs = []
    acc = 0
    for sz in IN_SPLITS[:-1]:
        acc += sz
        offs.append(acc)
    (a_h, a_b, a_c, a_z, n_q, n_kv, n_g, n_z, x_q, x_z, m_g) = jnp.split(proj, offs, axis=-1)

    y_a = short_conv_mixer(a_h, a_b, a_c, conv_w, conv_b) * jax.nn.silu(a_z)

    y_b = nsa_mixer(n_q.reshape(bsz, seq, NSA_HEADS, HEAD_DIM), n_kv, n_g,
                    cmp_pos, cmp_w1, cmp_w2) * jax.nn.silu(n_z)

    mem_kv = (rmsnorm(mem, mem_norm_g) @ w_mem_kv).reshape(bsz, mem.shape[1], 2, X_HEADS, HEAD_DIM)
    y_x = memory_attention(x_q.reshape(bsz, seq, X_HEADS, HEAD_DIM), mem_kv).reshape(
        bsz, seq, X_WIDTH) * jax.nn.silu(x_z)

    gates = jax.nn.sigmoid(m_g.astype(jnp.float32)).astype(x.dtype).reshape(bsz, seq, N_BRANCHES, D_MODEL)
    u = (gates[:, :, 0] * (y_a @ w_up_a)
         + gates[:, :, 1] * (y_b @ w_up_b)
         + gates[:, :, 2] * (y_x @ w_up_x))
    return x + u @ w_out


def setup_inputs(seed: int = 0) -> dict:
    key = jax.random.key(seed)
    ks = jax.random.split(key, 16)
    f32 = jnp.float32
    nrm = lambda k, shape, sc: jax.random.normal(k, shape, f32) * sc
    return {
        'x': nrm(ks[0], (BATCH, SEQ, D_MODEL), 1.0),
        'mem': nrm(ks[1], (BATCH, MEM_LEN, D_MODEL), 1.0),
        'norm_g': 1.0 + nrm(ks[2], (DEPTH, D_MODEL), 0.02),
        'w_in': nrm(ks[3], (DEPTH, D_MODEL, IN_WIDTH), D_MODEL ** -0.5),
        'conv_w': nrm(ks[4], (DEPTH, SC_KERNEL, SC_WIDTH), SC_KERNEL ** -0.5),
        'conv_b': nrm(ks[5], (DEPTH, SC_WIDTH), 0.02),
        'cmp_pos': nrm(ks[6], (DEPTH, 2, CMP_BLOCK, HEAD_DIM), 0.1),
        'cmp_w1': nrm(ks[7], (DEPTH, 2, CMP_BLOCK * HEAD_DIM, HEAD_DIM), (CMP_BLOCK * HEAD_DIM) ** -0.5),
        'cmp_w2': nrm(ks[8], (DEPTH, 2, HEAD_DIM, HEAD_DIM), HEAD_DIM ** -0.5),
        'mem_norm_g': 1.0 + nrm(ks[9], (DEPTH, D_MODEL), 0.02),
        'w_mem_kv': nrm(ks[10], (DEPTH, D_MODEL, 2 * X_WIDTH), D_MODEL ** -0.5),
        'w_up_a': nrm(ks[11], (DEPTH, SC_WIDTH, D_MODEL), SC_WIDTH ** -0.5),
        'w_up_b': nrm(ks[12], (DEPTH, NSA_WIDTH, D_MODEL), NSA_WIDTH ** -0.5),
        'w_up_x': nrm(ks[13], (DEPTH, X_WIDTH, D_MODEL), X_WIDTH ** -0.5),
        'w_out': nrm(ks[14], (DEPTH, D_MODEL, D_MODEL), D_MODEL ** -0.5),
        'final_g': 1.0 + nrm(ks[15], (D_MODEL,), 0.02),
    }


def reference(x, mem, norm_g, w_in, conv_w, conv_b, cmp_pos, cmp_w1, cmp_w2,
              mem_norm_g, w_mem_kv, w_up_a, w_up_b, w_up_x, w_out, final_g):
    for l in range(DEPTH):
        x = hybrid_layer(x, mem, norm_g[l], w_in[l], conv_w[l], conv_b[l], cmp_pos[l],
                         cmp_w1[l], cmp_w2[l], mem_norm_g[l], w_mem_kv[l],
                         w_up_a[l], w_up_b[l], w_up_x[l], w_out[l])
    return rmsnorm(x, final_g)
```

```python
import numpy as np
import ml_dtypes
import concourse.bass as bass
import concourse.mybir as mybir
from concourse.bass_utils import run_bass_kernel_spmd

F32 = mybir.dt.float32
BF16 = mybir.dt.bfloat16
AF = mybir.ActivationFunctionType
ALU = mybir.AluOpType

D = 4096
KC = 32
S = 2048
DEPTH = 4
MEM = 256
DH = 128
INW = 28720
TT = 1024
EPS = 1e-6
SCALE = DH ** -0.5
NEGB = -30000.0
O_AH, O_AB, O_AC, O_AZ, O_Q, O_KV, O_NG, O_NZ, O_XQ, O_XZ, O_MG = (
    0, 2048, 4096, 6144, 8192, 10240, 13312, 13360, 15408, 15920, 16432)


class Tok:
    __slots__ = ("w", "r", "multi", "dsem", "dcount", "name")

    def __init__(self, name="", multi=False):
        self.w = {}
        self.r = {}
        self.multi = multi
        self.dsem = None
        self.dcount = 0
        self.name = name


class Eng:
    def __init__(self, name, h, sem):
        self.name = name
        self.h = h
        self.sem = sem
        self.count = 0
        self.known = {}


class Scope:
    def __init__(self, fw):
        self.fw = fw

    def __enter__(self):
        self.mark = len(self.fw._ctx)
        self.fw._scope_toks.append([])
        return self

    def __exit__(self, *a):
        self.fw.barrier()
        while len(self.fw._ctx) > self.mark:
            self.fw._ctx.pop().__exit__(None, None, None)
        for t in self.fw._scope_toks.pop():
            self.fw._sem_pool.append((t.dsem, t.dcount))
        return False


class FW:
    def __init__(self, nc):
        self.nc = nc
        self._ctx = []
        self._sem_ctx = []
        self._sem_pool = []
        self._scope_toks = []
        self.all_handles = {}
        self.pe = self._mk("pe", nc.tensor)
        self.act = self._mk("act", nc.scalar)
        self.dve = self._mk("dve", nc.vector)
        self.pool = self._mk("pool", nc.gpsimd)
        self.sp = self._mk("sp", nc.sync)
        self.engs = [self.pe, self.act, self.dve, self.pool, self.sp]
        self.n_instr = 0
        self._uid = 0

    def _enter(self, cm):
        v = cm.__enter__()
        self._ctx.append(cm)
        return v

    def close(self):
        while self._ctx:
            self._ctx.pop().__exit__(None, None, None)
        while self._sem_ctx:
            self._sem_ctx.pop().__exit__(None, None, None)

    def scope(self):
        return Scope(self)

    def _mk(self, name, h):
        cm = self.nc.semaphore("p_" + name)
        sem = cm.__enter__()
        self._sem_ctx.append(cm)
        return Eng(name, h, sem)

    def _nm(self, name):
        self._uid += 1
        return f"{name}_{self._uid}"

    def sbuf(self, name, shape, dt):
        return self._enter(self.nc.sbuf_tensor(self._nm(name), shape, dt))

    def psum(self, name, shape, dt):
        return self._enter(self.nc.psum_tensor(self._nm(name), shape, dt))

    def tok(self, name="", dma=False, multi=False):
        t = Tok(name, multi)
        if dma:
            if self._sem_pool:
                t.dsem, t.dcount = self._sem_pool.pop()
            else:
                cm = self.nc.semaphore(self._nm("d_" + name))
                t.dsem = cm.__enter__()
                self._sem_ctx.append(cm)
                t.dcount = 0
            if self._scope_toks:
                self._scope_toks[-1].append(t)
        return t

    def _needs(self, reads, writes):
        needs = {}

        def add(d):
            for s, v in d.items():
                if needs.get(s, 0) < v:
                    needs[s] = v
        for t in reads:
            add(t.w)
        for t in writes:
            add(t.r)
            if not t.multi:
                add(t.w)
        return needs

    def _emit_waits(self, eng, needs):
        for s, v in needs.items():
            if eng.name == "pe" and s is eng.sem:
                continue
            if eng.known.get(s, 0) < v:
                eng.h.wait_ge(s, v)
                eng.known[s] = v

    def _commit(self, handle, reads, writes):
        s, v = handle
        if self.all_handles.get(s, 0) < v:
            self.all_handles[s] = v
        for t in writes:
            if t.multi:
                if t.r:
                    t.w = {}
                    t.r = {}
                if t.w.get(s, 0) < v:
                    t.w[s] = v
            else:
                t.w = {s: v}
                t.r = {}
        for t in reads:
            if t.r.get(s, 0) < v:
                t.r[s] = v

    def op(self, eng, fn, reads=(), writes=()):
        self._emit_waits(eng, self._needs(reads, writes))
        ins = fn()
        eng.count += 1
        ins.then_inc(eng.sem, 1)
        self._commit((eng.sem, eng.count), reads, writes)
        self.n_instr += 1
        return ins

    def dma(self, q, out, in_, semtok, reads=(), writes=(), **kw):
        self._emit_waits(q, self._needs(reads, writes))
        ins = q.h.dma_start(out=out, in_=in_, **kw)
        semtok.dcount += 16
        ins.then_inc(semtok.dsem, 16)
        self._commit((semtok.dsem, semtok.dcount), reads, writes)
        self.n_instr += 1
        return ins

    def coll(self, kind, src, dst, rg, semtok, reads=(), writes=()):
        q = self.pool
        self._emit_waits(q, self._needs(reads, writes))
        ins = self.nc.gpsimd.collective_compute(kind, ALU.bypass, replica_groups=rg, ins=[src.opt()], outs=[dst.opt()])
        semtok.dcount += 16
        ins.then_inc(semtok.dsem, 16)
        self._commit((semtok.dsem, semtok.dcount), reads, writes)
        self.n_instr += 1
        return ins

    def barrier(self):
        for e in self.engs:
            self._emit_waits(e, dict(self.all_handles))


def build(depth=DEPTH, dbg=False):
    nc = bass.Bass("TRN2", target_bir_lowering=False)
    fw = FW(nc)

    def din(name, shape, dt=F32):
        return nc.dram_tensor(name, list(shape), dt, kind="ExternalInput").ap()

    def dscr(name, shape, dt):
        kind = "ExternalOutput" if dbg else "Internal"
        return nc.dram_tensor(name, list(shape), dt, kind=kind).ap()

    xT_in = din("xT", [KC, 128, S])
    memT_in = din("memT", [KC, 128, MEM])
    gT_in = din("gT", [depth, 128, KC])
    mgT_in = din("mgnT", [depth, 128, KC])
    fgT_in = din("fgT", [128, KC])
    w_in = din("w_in", [depth, D, INW])
    cw_in = din("cw", [depth, 128, 16, 3])
    cb_in = din("cb", [depth, 128, 16])
    posT_in = din("posT", [depth, 2, 128, 32])
    w1_in = din("w1", [depth, 2, D, 128])
    w2_in = din("w2", [depth, 2, 128, 128])
    wmk_in = din("wmk", [depth, D, 1024])
    wua_in = din("wua", [depth, 2048, D])
    wub_in = din("wub", [depth, 2048, D])
    wux_in = din("wux", [depth, 512, D])
    wo_in = din("wo", [depth, D, D])
    cos_in = din("c_cos", [128, S])
    sin_in = din("c_sin", [128, S])
    ident_in = din("c_ident", [128, 128], BF16)
    caus_in = din("c_caus", [128, 512], BF16)
    wedge_in = din("c_wedge", [128, 512], BF16)
    cbias_in = din("c_cbias", [128, 16, 512], BF16)
    expand_in = din("c_expand", [128, 16, 128], BF16)
    selc_in = din("c_selc", [128, 16, 32])
    ovl_in = din("c_ovl", [128, 32], BF16)
    out = nc.dram_tensor("outT", [KC, 128, S], F32, kind="ExternalOutput").ap()

    xT_s = dscr("xT_s", [KC, 128, S], F32)
    yaT = dscr("yaT", [2048, S], BF16)
    qpT = dscr("qpT", [16, 128, S], BF16)
    qrT = dscr("qrT", [16, 128, S], BF16)
    kcmpT = dscr("kcmpT", [4, 128, S], BF16)
    vcmpT = dscr("vcmpT", [4, 128, S], BF16)
    kslcT = dscr("kslcT", [4, 128, S], BF16)
    kwinT = dscr("kwinT", [4, 128, S], BF16)
    vslc = dscr("vslc", [S, 512], BF16)
    vwin = dscr("vwin", [S, 512], BF16)
    ngs = dscr("ngs", [S, 48], F32)
    nzT = dscr("nzT", [2048, S], BF16)
    xqT = dscr("xqT", [4, 128, S], BF16)
    xzT = dscr("xzT", [512, S], BF16)
    mgT = dscr("mgT", [3, D, S], BF16)
    ybT = dscr("ybT", [2048, S], BF16)
    yxT = dscr("yxT", [512, S], BF16)
    uT_s = dscr("uT_s", [D, S], BF16)

    T = {n: fw.tok(n, multi=True) for n in
         ["xT_s", "yaT", "qpT", "qrT", "kcmpT", "vcmpT", "kslcT", "kwinT", "vslc", "vwin", "ngs", "nzT",
          "xqT", "xzT", "mgT", "ybT", "yxT", "uT_s", "out"]}
    t_ext = fw.tok("ext", multi=True)

    ident = fw.sbuf("ident", [128, 128], BF16)
    caus = fw.sbuf("caus", [128, 512], BF16)
    wedge = fw.sbuf("wedge", [128, 512], BF16)
    ones = fw.sbuf("ones", [128, 128], BF16)
    t_const = fw.tok("const", dma=True)
    for dst, src in ((ident, ident_in), (caus, caus_in), (wedge, wedge_in)):
        fw.dma(fw.sp, dst[:], src, t_const, writes=[t_const])
    t_ones = fw.tok("ones")
    fw.op(fw.dve, lambda: nc.vector.memset(ones[:], 1.0), writes=[t_ones])
    t_cp = fw.tok("cp", dma=True)
    for k in range(0, KC, 8):
        fw.dma(fw.sp, xT_s[k:k + 8], xT_in[k:k + 8], t_cp, reads=[t_ext], writes=[T["xT_s"]])
    fw.barrier()

    def norm_phase(src, src_tok, g_dram, ntok, dst_hT=None, t_hT=None, hoff=0, dst_dram=None, dst_tok=None):
        with fw.scope():
            gsb = fw.sbuf("gsb", [128, KC], F32)
            t_g = fw.tok("g", dma=True)
            fw.dma(fw.sp, gsb[:], g_dram, t_g, reads=[t_ext], writes=[t_g])
            xs = [fw.sbuf("xs", [128, KC, 128], F32) for _ in range(2)]
            t_xs = [fw.tok("xs", dma=True) for _ in range(2)]
            sq = fw.sbuf("sq", [128, KC, 128], BF16)
            t_sq = fw.tok("sq")
            psn = fw.psum("psn", [128, 512], F32)
            t_psn = fw.tok("psn")
            sd = fw.sbuf("sd", [128, 128], F32)
            t_sd = fw.tok("sd")
            rstd = [fw.sbuf("rstd", [128, 128], F32) for _ in range(2)]
            t_rstd = [fw.tok("rstd") for _ in range(2)]
            if dst_dram is not None:
                ob = [fw.sbuf("ob", [128, KC, 128], F32) for _ in range(2)]
                t_ob = [fw.tok("ob", dma=True, multi=True) for _ in range(2)]
            for i in range(ntok // 128):
                b = i % 2
                t0 = i * 128
                fw.dma(fw.sp, xs[b][:], src[:, :, t0:t0 + 128].rearrange("k p t -> p k t"), t_xs[b],
                       reads=[src_tok], writes=[t_xs[b]])
                fw.op(fw.act, lambda: nc.scalar.activation(out=sq[:], in_=xs[b][:], func=AF.Square),
                      reads=[t_xs[b]], writes=[t_sq])
                for kc in range(KC):
                    fw.op(fw.pe, lambda: nc.tensor.matmul(psn[:, 0:128], lhsT=ones[:], rhs=sq[:, kc, :],
                                                          start=(kc == 0), stop=(kc == KC - 1)),
                          reads=[t_sq, t_ones], writes=[t_psn])
                fw.op(fw.dve, lambda: nc.vector.tensor_scalar(out=sd[:], in0=psn[:, 0:128], scalar1=1.0 / D,
                                                              scalar2=EPS, op0=ALU.mult, op1=ALU.add),
                      reads=[t_psn], writes=[t_sd])
                fw.op(fw.act, lambda: nc.scalar.activation(out=sd[:], in_=sd[:], func=AF.Sqrt),
                      reads=[t_sd], writes=[t_sd])
                fw.op(fw.dve, lambda: nc.vector.reciprocal(out=rstd[b][:], in_=sd[:]),
                      reads=[t_sd], writes=[t_rstd[b]])
                for kc in range(KC):
                    if dst_dram is None:
                        o_ap, o_tok = dst_hT[:, kc, hoff + t0:hoff + t0 + 128], t_hT
                    else:
                        o_ap, o_tok = ob[b][:, kc, :], t_ob[b]
                    fw.op(fw.dve, lambda: nc.vector.scalar_tensor_tensor(
                        out=o_ap, in0=xs[b][:, kc, :], scalar=gsb[:, kc:kc + 1], in1=rstd[b][:],
                        op0=ALU.mult, op1=ALU.mult), reads=[t_xs[b], t_rstd[b], t_g], writes=[o_tok])
                if dst_dram is not None:
                    fw.dma(fw.sp, dst_dram[:, :, t0:t0 + 128].rearrange("k p t -> p k t"), ob[b][:], t_ob[b],
                           reads=[t_ob[b]], writes=[dst_tok])

    class WStream:
        def __init__(self):
            self.wt = [fw.sbuf("wt", [128, KC, 512], BF16) for _ in range(2)]
            self.t_wt = [fw.tok("wt", dma=True) for _ in range(2)]

        def run(self, groups):
            def load(gi):
                b = gi % 2
                for (src, kc0, nkc, col0, ncols) in groups[gi][0]:
                    fw.dma(fw.pool, self.wt[b][:, kc0:kc0 + nkc, col0:col0 + ncols],
                           src.rearrange("(k p) c -> p k c", p=128), self.t_wt[b],
                           reads=[t_ext], writes=[self.t_wt[b]])
            load(0)
            for gi in range(len(groups)):
                if gi + 1 < len(groups):
                    load(gi + 1)
                groups[gi][1](self.wt[gi % 2], self.t_wt[gi % 2])

    for l in range(depth):
        with fw.scope():
            halo = fw.sbuf("halo", [128, 16, 2], F32)
            t_halo = fw.tok("halo")
            fw.op(fw.dve, lambda: nc.vector.memset(halo[:], 0.0), writes=[t_halo])
            mK = fw.sbuf("mK", [128, 4, MEM], BF16)
            t_mK = fw.tok("mK")
            mV = fw.sbuf("mV", [128, 2, 4, 129], BF16)
            t_mV = fw.tok("mV")
            fw.op(fw.dve, lambda: nc.vector.memset(mV[:], 1.0), writes=[t_mV])

            with fw.scope():
                mhT = fw.sbuf("mhT", [128, KC, MEM], BF16)
                t_mhT = fw.tok("mhT", multi=True)
                norm_phase(memT_in, t_ext, mgT_in[l], MEM, dst_hT=mhT, t_hT=t_mhT)
                ws = WStream()
                pa = [fw.psum("pa", [128, 512], F32) for _ in range(2)]
                t_pa = [fw.tok("pa") for _ in range(2)]

                def hK(wtb, t_wtb):
                    for hx in range(4):
                        p = hx % 2
                        for kc in range(KC):
                            fw.op(fw.pe, lambda: nc.tensor.matmul(pa[p][:, 0:MEM], lhsT=wtb[:, kc, hx * 128:(hx + 1) * 128],
                                                                  rhs=mhT[:, kc, :], start=(kc == 0), stop=(kc == KC - 1)),
                                  reads=[t_wtb, t_mhT], writes=[t_pa[p]])
                        fw.op(fw.act, lambda: nc.scalar.copy(out=mK[:, hx, :], in_=pa[p][:, 0:MEM]),
                              reads=[t_pa[p]], writes=[t_mK])

                def hV(wtb, t_wtb):
                    for mt in range(2):
                        p = mt % 2
                        for kc in range(KC):
                            fw.op(fw.pe, lambda: nc.tensor.matmul(pa[p][:], lhsT=mhT[:, kc, mt * 128:(mt + 1) * 128],
                                                                  rhs=wtb[:, kc, :], start=(kc == 0), stop=(kc == KC - 1)),
                                  reads=[t_wtb, t_mhT], writes=[t_pa[p]])
                        fw.op(fw.act, lambda: nc.scalar.copy(out=mV[:, mt, :, 0:128],
                                                             in_=pa[p][:].rearrange("p (h d) -> p h d", h=4)),
                              reads=[t_pa[p]], writes=[t_mV])
                ws.run([([(wmk_in[l][:, 0:512], 0, KC, 0, 512)], hK),
                        ([(wmk_in[l][:, 512:1024], 0, KC, 0, 512)], hV)])

            for Tt in range(S // TT):
                tk0 = Tt * TT
                with fw.scope():
                    hT = fw.sbuf("hT", [128, KC, TT], BF16)
                    t_hT = fw.tok("hT", multi=True)
                    norm_phase(xT_s[:, :, tk0:tk0 + TT], T["xT_s"], gT_in[l], TT, dst_hT=hT, t_hT=t_hT)
                    ws = WStream()
                    pa = [fw.psum("pa", [128, 512], F32) for _ in range(4)]
                    t_pa = [fw.tok("pa") for _ in range(4)]
                    stg = [fw.sbuf("stg", [128, TT], BF16) for _ in range(4)]
                    t_stg = [fw.tok("stg", dma=True) for _ in range(4)]
                    cnt = {"pa": 0, "stg": 0, "ng": 0}
                    cosb = fw.sbuf("cosb", [128, TT], F32)
                    sinb = fw.sbuf("sinb", [128, TT], F32)
                    cwb = fw.sbuf("cwb", [128, 16, 3], F32)
                    cbb = fw.sbuf("cbb", [128, 16], F32)
                    t_tab = fw.tok("tab", dma=True)
                    fw.dma(fw.sp, cosb[:], cos_in[:, tk0:tk0 + TT], t_tab, reads=[t_ext], writes=[t_tab])
                    fw.dma(fw.sp, sinb[:], sin_in[:, tk0:tk0 + TT], t_tab, reads=[t_ext], writes=[t_tab])
                    fw.dma(fw.sp, cwb[:], cw_in[l], t_tab, reads=[t_ext], writes=[t_tab])
                    fw.dma(fw.sp, cbb[:], cb_in[l], t_tab, reads=[t_ext], writes=[t_tab])
                    x32 = fw.sbuf("x32", [128, 512], F32)
                    xsw = fw.sbuf("xsw", [128, 512], F32)
                    r1 = fw.sbuf("r1", [128, 512], F32)
                    r2 = fw.sbuf("r2", [128, 512], F32)
                    t_x32, t_xsw, t_r1, t_r2 = fw.tok(), fw.tok(), fw.tok(), fw.tok()
                    ah = fw.sbuf("ah", [128, 512], F32)
                    aB = fw.sbuf("aB", [128, 512], F32)
                    sz = fw.sbuf("sz", [128, 512], F32)
                    yv = fw.sbuf("yv", [128, 512], F32)
                    ub = fw.sbuf("ub", [128, 514], F32)
                    t_ah, t_aB, t_sz, t_yv, t_ub = fw.tok(), fw.tok(), fw.tok(), fw.tok(), fw.tok()
                    ngst = [fw.sbuf("ngst", [128, 48], F32) for _ in range(2)]
                    t_ngst = [fw.tok("ngst", dma=True) for _ in range(2)]

                    def accum_fm(wtb, t_wtb, m, th):
                        p = cnt["pa"] % 4
                        cnt["pa"] += 1
                        for kc in range(KC):
                            fw.op(fw.pe, lambda: nc.tensor.matmul(pa[p][:], lhsT=wtb[:, kc, m * 128:(m + 1) * 128],
                                                                  rhs=hT[:, kc, th * 512:(th + 1) * 512],
                                                                  start=(kc == 0), stop=(kc == KC - 1)),
                                  reads=[t_wtb, t_hT], writes=[t_pa[p]])
                        return p

                    def next_stg():
                        s_ = cnt["stg"] % 4
                        cnt["stg"] += 1
                        return s_

                    def rope_evac(p, th, sdst):
                        fw.op(fw.act, lambda: nc.scalar.copy(out=x32[:], in_=pa[p][:]), reads=[t_pa[p]], writes=[t_x32])
                        fw.op(fw.act, lambda: nc.scalar.copy(out=xsw[64:128, :], in_=pa[p][0:64, :]),
                              reads=[t_pa[p]], writes=[t_xsw])
                        fw.op(fw.dve, lambda: nc.vector.tensor_copy(out=xsw[0:64, :], in_=pa[p][64:128, :]),
                              reads=[t_pa[p]], writes=[t_xsw])
                        fw.op(fw.dve, lambda: nc.vector.tensor_tensor(out=r1[:], in0=x32[:], in1=cosb[:, th * 512:(th + 1) * 512],
                                                                      op=ALU.mult), reads=[t_x32, t_tab], writes=[t_r1])
                        fw.op(fw.dve, lambda: nc.vector.tensor_tensor(out=r2[:], in0=xsw[:], in1=sinb[:, th * 512:(th + 1) * 512],
                                                                      op=ALU.mult), reads=[t_xsw, t_tab], writes=[t_r2])
                        fw.op(fw.dve, lambda: nc.vector.tensor_tensor(out=stg[sdst][:, th * 512:(th + 1) * 512], in0=r1[:],
                                                                      in1=r2[:], op=ALU.add),
                              reads=[t_r1, t_r2], writes=[t_stg[sdst]])

                    def fm_handler(chunks):
                        def h(wtb, t_wtb):
                            for m, ch in enumerate(chunks):
                                kind = ch[0]
                                s1 = next_stg()
                                s2 = next_stg() if kind == "qrope" else None
                                for th in range(2):
                                    p = accum_fm(wtb, t_wtb, m, th)
                                    sl = slice(th * 512, (th + 1) * 512)
                                    if kind in ("copy", "silu", "sigmoid"):
                                        func = {"copy": AF.Copy, "silu": AF.Silu, "sigmoid": AF.Sigmoid}[kind]
                                        fw.op(fw.act, lambda: nc.scalar.activation(out=stg[s1][:, sl], in_=pa[p][:], func=func),
                                              reads=[t_pa[p]], writes=[t_stg[s1]])
                                    elif kind == "rope":
                                        rope_evac(p, th, s1)
                                    elif kind == "qrope":
                                        fw.op(fw.act, lambda: nc.scalar.copy(out=stg[s2][:, sl], in_=pa[p][:]),
                                              reads=[t_pa[p]], writes=[t_stg[s2]])
                                        rope_evac(p, th, s1)
                                fw.dma(fw.sp, ch[1][:, tk0:tk0 + TT], stg[s1][:], t_stg[s1], reads=[t_stg[s1]], writes=[ch[2]])
                                if kind == "qrope":
                                    fw.dma(fw.sp, ch[3][:, tk0:tk0 + TT], stg[s2][:], t_stg[s2], reads=[t_stg[s2]], writes=[ch[4]])
                        return h

                    def conv_handler(cb):
                        def h(wtb, t_wtb):
                            s1 = next_stg()
                            for th in range(2):
                                sl = slice(th * 512, (th + 1) * 512)
                                if th == 0:
                                    fw.op(fw.dve, lambda: nc.vector.tensor_copy(out=ub[:, 0:2], in_=halo[:, cb, :]),
                                          reads=[t_halo], writes=[t_ub])
                                else:
                                    fw.op(fw.dve, lambda: nc.vector.tensor_copy(out=ub[:, 0:2], in_=ub[:, 512:514]),
                                          reads=[t_ub], writes=[t_ub])
                                p = accum_fm(wtb, t_wtb, 0, th)
                                fw.op(fw.act, lambda: nc.scalar.copy(out=ah[:], in_=pa[p][:]), reads=[t_pa[p]], writes=[t_ah])
                                p = accum_fm(wtb, t_wtb, 1, th)
                                fw.op(fw.act, lambda: nc.scalar.copy(out=aB[:], in_=pa[p][:]), reads=[t_pa[p]], writes=[t_aB])
                                p = accum_fm(wtb, t_wtb, 2, th)
                                fw.op(fw.dve, lambda: nc.vector.tensor_tensor(out=ub[:, 2:514], in0=pa[p][:], in1=ah[:], op=ALU.mult),
                                      reads=[t_pa[p], t_ah], writes=[t_ub])
                                p = accum_fm(wtb, t_wtb, 3, th)
                                fw.op(fw.act, lambda: nc.scalar.activation(out=sz[:], in_=pa[p][:], func=AF.Silu),
                                      reads=[t_pa[p]], writes=[t_sz])
                                fw.op(fw.dve, lambda: nc.vector.tensor_scalar(out=yv[:], in0=ub[:, 2:514], scalar1=cwb[:, cb, 2:3],
                                                                              scalar2=cbb[:, cb:cb + 1], op0=ALU.mult, op1=ALU.add),
                                      reads=[t_ub, t_tab], writes=[t_yv])
                                fw.op(fw.dve, lambda: nc.vector.scalar_tensor_tensor(out=yv[:], in0=ub[:, 1:513], scalar=cwb[:, cb, 1:2],
                                                                                     in1=yv[:], op0=ALU.mult, op1=ALU.add),
                                      reads=[t_ub, t_tab, t_yv], writes=[t_yv])
                                fw.op(fw.dve, lambda: nc.vector.scalar_tensor_tensor(out=yv[:], in0=ub[:, 0:512], scalar=cwb[:, cb, 0:1],
                                                                                     in1=yv[:], op0=ALU.mult, op1=ALU.add),
                                      reads=[t_ub, t_tab, t_yv], writes=[t_yv])
                                fw.op(fw.dve, lambda: nc.vector.tensor_tensor(out=yv[:], in0=yv[:], in1=aB[:], op=ALU.mult),
                                      reads=[t_yv, t_aB], writes=[t_yv])
                                fw.op(fw.dve, lambda: nc.vector.tensor_tensor(out=stg[s1][:, sl], in0=yv[:], in1=sz[:], op=ALU.mult),
                                      reads=[t_yv, t_sz], writes=[t_stg[s1]])
                            fw.op(fw.dve, lambda: nc.vector.tensor_copy(out=halo[:, cb, :], in_=ub[:, 512:514]),
                                  reads=[t_ub], writes=[t_halo])
                            fw.dma(fw.sp, yaT[cb * 128:(cb + 1) * 128, tk0:tk0 + TT], stg[s1][:], t_stg[s1],
                                   reads=[t_stg[s1]], writes=[T["yaT"]])
                        return h

                    def tm_handler(ncols, dst, dst_tok, kind):
                        def h(wtb, t_wtb):
                            for tb in range(TT // 128):
                                p = cnt["pa"] % 4
                                cnt["pa"] += 1
                                for kc in range(KC):
                                    fw.op(fw.pe, lambda: nc.tensor.matmul(pa[p][:, 0:ncols], lhsT=hT[:, kc, tb * 128:(tb + 1) * 128],
                                                                          rhs=wtb[:, kc, 0:ncols], start=(kc == 0), stop=(kc == KC - 1)),
                                          reads=[t_wtb, t_hT], writes=[t_pa[p]])
                                r0 = tk0 + tb * 128
                                if kind == "v":
                                    s1 = next_stg()
                                    fw.op(fw.act, lambda: nc.scalar.copy(out=stg[s1][:, 0:512], in_=pa[p][:]),
                                          reads=[t_pa[p]], writes=[t_stg[s1]])
                                    fw.dma(fw.sp, dst[r0:r0 + 128, :], stg[s1][:, 0:512], t_stg[s1], reads=[t_stg[s1]], writes=[dst_tok])
                                else:
                                    n_ = cnt["ng"] % 2
                                    cnt["ng"] += 1
                                    fw.op(fw.act, lambda: nc.scalar.activation(out=ngst[n_][:], in_=pa[p][:, 0:48], func=AF.Sigmoid),
                                          reads=[t_pa[p]], writes=[t_ngst[n_]])
                                    fw.dma(fw.sp, dst[r0:r0 + 128, :], ngst[n_][:], t_ngst[n_], reads=[t_ngst[n_]], writes=[dst_tok])
                        return h

                    W = w_in[l]
                    groups = []
                    for cb in range(16):
                        segs = [(W[:, o + cb * 128:o + cb * 128 + 128], 0, KC, j * 128, 128)
                                for j, o in enumerate((O_AH, O_AB, O_AC, O_AZ))]
                        groups.append((segs, conv_handler(cb)))
                    for hg in range(4):
                        segs = [(W[:, O_Q + hg * 512:O_Q + hg * 512 + 512], 0, KC, 0, 512)]
                        groups.append((segs, fm_handler([("qrope", qrT[hg * 4 + r], T["qrT"], qpT[hg * 4 + r], T["qpT"])
                                                         for r in range(4)])))
                    kvo = lambda br, kvi: O_KV + (br * 2 + kvi) * 512
                    for (br, dstT, nm) in ((1, kslcT, "kslcT"), (2, kwinT, "kwinT")):
                        groups.append(([(W[:, kvo(br, 0):kvo(br, 0) + 512], 0, KC, 0, 512)],
                                       fm_handler([("rope", dstT[g], T[nm]) for g in range(4)])))
                    for (kvi, dstT, nm) in ((0, kcmpT, "kcmpT"), (1, vcmpT, "vcmpT")):
                        groups.append(([(W[:, kvo(0, kvi):kvo(0, kvi) + 512], 0, KC, 0, 512)],
                                       fm_handler([("copy", dstT[g], T[nm]) for g in range(4)])))
                    for (br, dst, nm) in ((1, vslc, "vslc"), (2, vwin, "vwin")):
                        groups.append(([(W[:, kvo(br, 1):kvo(br, 1) + 512], 0, KC, 0, 512)], tm_handler(512, dst, T[nm], "v")))
                    groups.append(([(W[:, O_NG:O_NG + 48], 0, KC, 0, 48)], tm_handler(48, ngs, T["ngs"], "g")))
                    for k in range(4):
                        groups.append(([(W[:, O_NZ + k * 512:O_NZ + k * 512 + 512], 0, KC, 0, 512)],
                                       fm_handler([("silu", nzT[(k * 4 + r) * 128:(k * 4 + r + 1) * 128], T["nzT"]) for r in range(4)])))
                    groups.append(([(W[:, O_XQ:O_XQ + 512], 0, KC, 0, 512)],
                                   fm_handler([("copy", xqT[r], T["xqT"]) for r in range(4)])))
                    groups.append(([(W[:, O_XZ:O_XZ + 512], 0, KC, 0, 512)],
                                   fm_handler([("silu", xzT[r * 128:(r + 1) * 128], T["xzT"]) for r in range(4)])))
                    for k in range(24):
                        gi_, dc0 = divmod(k * 4, 32)
                        groups.append(([(W[:, O_MG + k * 512:O_MG + k * 512 + 512], 0, KC, 0, 512)],
                                       fm_handler([("sigmoid", mgT[gi_][(dc0 + r) * 128:(dc0 + r + 1) * 128], T["mgT"])
                                                   for r in range(4)])))
                    ws.run(groups)

            with fw.scope():
                kcA = fw.sbuf("kcA", [128, 4, 128], BF16)
                vcA = fw.sbuf("vcA", [128, 4, 161], BF16)
                t_kcA, t_vcA = fw.tok("kcA"), fw.tok("vcA", dma=True)
                fw.op(fw.dve, lambda: nc.vector.memset(kcA[:], 0.0), writes=[t_kcA])
                fw.op(fw.dve, lambda: nc.vector.memset(vcA[:], 1.0), writes=[t_vcA])
                for g in range(4):
                    fw.dma(fw.sp, vcA[:, g, 129:161], ovl_in, t_vcA, reads=[t_ext], writes=[t_vcA])
                cbias = fw.sbuf("cbias", [128, 16, 512], BF16)
                expand = fw.sbuf("expand", [128, 16, 128], BF16)
                selc = fw.sbuf("selc", [128, 16, 32], F32)
                t_c3 = fw.tok("c3", dma=True)
                fw.dma(fw.sp, cbias[:], cbias_in, t_c3, reads=[t_ext], writes=[t_c3])
                fw.dma(fw.sp, expand[:], expand_in, t_c3, reads=[t_ext], writes=[t_c3])
                fw.dma(fw.sp, selc[:], selc_in, t_c3, reads=[t_ext], writes=[t_c3])

                with fw.scope():
                    w1b = fw.sbuf("w1b", [128, 32, 128], BF16)
                    w2b = fw.sbuf("w2b", [128, 128], BF16)
                    posb = fw.sbuf("posb", [128, 32], BF16)
                    t_w1 = fw.tok("w1", dma=True)
                    kcs = fw.sbuf("kcs", [128, S], BF16)
                    t_kcs = fw.tok("kcs", dma=True)
                    pp = fw.psum("pp", [128, 512], F32)
                    t_pp = fw.tok("pp")
                    pq = fw.psum("pq", [128, 512], F32)
                    t_pq = fw.tok("pq")
                    pbias = fw.sbuf("pbias", [128, 1], F32)
                    t_pbias = fw.tok("pbias")
                    hid = fw.sbuf("hid", [128, 128], BF16)
                    t_hid = fw.tok("hid")
                    fw.op(fw.dve, lambda: nc.vector.memset(hid[:], 0.0), writes=[t_hid])
                    for c in range(2):
                        fw.dma(fw.pool, w1b[:], w1_in[l, c].rearrange("(l d) o -> d l o", d=128), t_w1, reads=[t_ext], writes=[t_w1])
                        fw.dma(fw.pool, w2b[:], w2_in[l, c], t_w1, reads=[t_ext], writes=[t_w1])
                        fw.dma(fw.pool, posb[:], posT_in[l, c], t_w1, reads=[t_ext], writes=[t_w1])
                        for ll in range(32):
                            fw.op(fw.pe, lambda: nc.tensor.matmul(pq[:, 0:1], lhsT=w1b[:, ll, :], rhs=posb[:, ll:ll + 1],
                                                                  start=(ll == 0), stop=(ll == 31)),
                                  reads=[t_w1], writes=[t_pq])
                        fw.op(fw.act, lambda: nc.scalar.copy(out=pbias[:], in_=pq[:, 0:1]), reads=[t_pq], writes=[t_pbias])
                        srcT = kcmpT if c == 0 else vcmpT
                        for g in range(4):
                            fw.dma(fw.sp, kcs[:], srcT[g], t_kcs, reads=[T["kcmpT"], T["vcmpT"]], writes=[t_kcs])
                            for ll in range(32):
                                fw.op(fw.pe, lambda: nc.tensor.matmul(pp[:, 0:127], lhsT=w1b[:, ll, :],
                                                                      rhs=kcs[:, ll:ll + 16 * 126 + 1:16],
                                                                      start=(ll == 0), stop=(ll == 31)),
                                      reads=[t_w1, t_kcs], writes=[t_pp])
                            fw.op(fw.act, lambda: nc.scalar.activation(out=hid[:, 0:127], in_=pp[:, 0:127], func=AF.Silu,
                                                                       bias=pbias[:, 0:1]),
                                  reads=[t_pp, t_pbias], writes=[t_hid])
                            if c == 0:
                                fw.op(fw.pe, lambda: nc.tensor.matmul(pq[:, 0:127], lhsT=w2b[:], rhs=hid[:, 0:127], start=True, stop=True),
                                      reads=[t_w1, t_hid], writes=[t_pq])
                                fw.op(fw.act, lambda: nc.scalar.copy(out=kcA[:, g, 0:127], in_=pq[:, 0:127]),
                                      reads=[t_pq], writes=[t_kcA])
                            else:
                                fw.op(fw.pe, lambda: nc.tensor.matmul(pq[:, 0:128], lhsT=hid[:, 0:128], rhs=w2b[:], start=True, stop=True),
                                      reads=[t_w1, t_hid], writes=[t_pq])
                                fw.op(fw.act, lambda: nc.scalar.copy(out=vcA[:, g, 0:128], in_=pq[:, 0:128]),
                                      reads=[t_pq], writes=[t_vcA])

                ks2 = [fw.sbuf("ks", [128, S], BF16) for _ in range(2)]
                kw2 = [fw.sbuf("kw", [128, S], BF16) for _ in range(2)]
                vsA2 = [fw.sbuf("vsA", [128, 16, 129], BF16) for _ in range(2)]
                vwA2 = [fw.sbuf("vwA", [128, 16, 129], BF16) for _ in range(2)]
                t_kv2 = [fw.tok("kv", dma=True, multi=True) for _ in range(2)]
                for p_ in range(2):
                    fw.op(fw.dve, lambda: nc.vector.memset(vsA2[p_][:], 1.0), writes=[t_kv2[p_]])
                    fw.op(fw.dve, lambda: nc.vector.memset(vwA2[p_][:], 1.0), writes=[t_kv2[p_]])
                qp = [fw.sbuf("qp", [128, 4, 128], BF16) for _ in range(2)]
                qr = [fw.sbuf("qr", [128, 4, 128], BF16) for _ in range(2)]
                gs = [fw.sbuf("gs", [128, 48], F32) for _ in range(2)]
                nz = [fw.sbuf("nz", [128, 4, 128], BF16) for _ in range(2)]
                t_q = [fw.tok("q", dma=True, multi=True) for _ in range(2)]
                NE = 24
                Eb = fw.sbuf("Eb", [128, NE, 512], BF16)
                t_E = [fw.tok("E") for _ in range(NE)]
                pss = [fw.psum("pss", [128, 512], F32) for _ in range(3)]
                t_pss = [fw.tok("pss") for _ in range(3)]
                po = [fw.psum("po", [128, 512], F32) for _ in range(4)]
                t_po = [fw.tok("po") for _ in range(4)]
                pst = fw.psum("pst", [128, 1024], BF16)
                t_pst = fw.tok("pst")
                den = fw.sbuf("den", [128, 4], F32)
                rden = fw.sbuf("rden", [128, 4], F32)
                cc = fw.sbuf("cc", [128, 4], F32)
                imp = fw.sbuf("imp", [128, 32], F32)
                imp3 = fw.sbuf("imp3", [128, 32], F32)
                m1 = fw.sbuf("m1", [128, 8], F32)
                m2 = fw.sbuf("m2", [128, 8], F32)
                selb = fw.sbuf("selb", [128, 128], BF16)
                selbT = fw.sbuf("selbT", [128, 512], BF16)
                oacc = fw.sbuf("oacc", [128, 4, 128], F32)
                cc3 = [fw.sbuf("cc3", [128, 4], F32) for _ in range(3)]
                t_cc3 = [fw.tok("cc3") for _ in range(3)]
                o3 = [[fw.sbuf("o3", [128, 4, 128], F32) for _ in range(3)] for _ in range(2)]
                t_o3 = [[fw.tok("o3", multi=True) for _ in range(3)] for _ in range(2)]
                osum = fw.sbuf("osum", [128, 512], F32)
                t_osum = fw.tok("osum")
                obf = fw.sbuf("obf", [128, 4, 128], BF16)
                yst = [fw.sbuf("yst", [128, 4, 128], BF16) for _ in range(2)]
                t_yst = [fw.tok("yst", dma=True) for _ in range(2)]
                t_den, t_rden, t_cc, t_imp, t_imp3, t_m1, t_m2 = (fw.tok() for _ in range(7))
                t_selb, t_selbT, t_oacc, t_obf = (fw.tok() for _ in range(4))
                fw.op(fw.dve, lambda: nc.vector.memset(selb[:], 0.0), writes=[t_selb])
                cn = {"e": 0, "s": 0, "po": 0, "it": 0}

                def po_ap(pair, r, c0, c1, w):
                    bank = po[pair * 2 + r // 2]
                    off = (r % 2) * w
                    return bank[:, off + c0:off + c1], t_po[pair * 2 + r // 2]

                def score_tile(lhsT, lhs_toks, rhs, rhs_toks, biases):
                    sb_ = cn["s"] % 3
                    cn["s"] += 1
                    e_ = cn["e"] % NE
                    cn["e"] += 1
                    fw.op(fw.pe, lambda: nc.tensor.matmul(pss[sb_][:], lhsT=lhsT, rhs=rhs, start=True, stop=(len(biases) == 0)),
                          reads=lhs_toks + rhs_toks, writes=[t_pss[sb_]])
                    nb = len(biases)
                    for bi, (bl, br_, btoks) in enumerate(biases):
                        fw.op(fw.pe, lambda: nc.tensor.matmul(pss[sb_][:], lhsT=bl, rhs=br_, start=False, stop=(bi == nb - 1)),
                              reads=btoks, writes=[t_pss[sb_]])
                    fw.op(fw.act, lambda: nc.scalar.activation(out=Eb[:, e_, :], in_=pss[sb_][:], func=AF.Exp, scale=SCALE),
                          reads=[t_pss[sb_]], writes=[t_E[e_]])
                    return e_

                def pv_and_combine(eslots, vA, v_toks, vsel, width, br, g, b, first, last):
                    pair = cn["po"] % 2
                    cn["po"] += 1
                    n = len(eslots)
                    for r in range(4):
                        oap, otok = po_ap(pair, r, 0, width, width)
                        for j, (e_, kt) in enumerate(eslots):
                            fw.op(fw.pe, lambda: nc.tensor.matmul(oap, lhsT=Eb[:, e_, r * 128:(r + 1) * 128], rhs=vsel(kt),
                                                                  start=(j == 0), stop=(j == n - 1)),
                                  reads=[t_E[e_]] + v_toks, writes=[otok])
                    for bk in range(2):
                        bank, tokb = po[pair * 2 + bk], t_po[pair * 2 + bk]
                        fw.op(fw.dve, lambda: nc.vector.tensor_scalar_max(out=den[:, 2 * bk:2 * bk + 2],
                                                                          in0=bank[:, 128:128 + width + 1:width], scalar1=1e-30),
                              reads=[tokb], writes=[t_den])
                    fw.op(fw.dve, lambda: nc.vector.reciprocal(out=rden[:], in_=den[:]), reads=[t_den], writes=[t_rden])
                    if br == 0:
                        for r in range(4):
                            iap, otok = po_ap(pair, r, 129, 161, width)
                            if r == 0:
                                fw.op(fw.dve, lambda: nc.vector.tensor_scalar_mul(out=imp[:], in0=iap, scalar1=rden[:, 0:1]),
                                      reads=[otok, t_rden], writes=[t_imp])
                            else:
                                fw.op(fw.dve, lambda: nc.vector.scalar_tensor_tensor(out=imp[:], in0=iap, scalar=rden[:, r:r + 1],
                                                                                     in1=imp[:], op0=ALU.mult, op1=ALU.add),
                                      reads=[otok, t_rden, t_imp], writes=[t_imp])
                    gview = gs[b][:, 12 * g:12 * g + 12].rearrange("p (r c) -> p r c", c=3)[:, :, br]
                    fw.op(fw.dve, lambda: nc.vector.tensor_tensor(out=cc3[br][:], in0=rden[:], in1=gview, op=ALU.mult),
                          reads=[t_rden, t_q[b]], writes=[t_cc3[br]])
                    for r in range(4):
                        vap, otok = po_ap(pair, r, 0, 128, width)
                        fw.op(fw.act, lambda: nc.scalar.activation(out=o3[b][br][:, r, :], in_=vap, func=AF.Copy,
                                                                   scale=cc3[br][:, r:r + 1]),
                              reads=[otok, t_cc3[br]], writes=[t_o3[b][br]])
                    if last:
                        fl = lambda t_: t_[:].rearrange("p r t -> p (r t)")
                        fw.op(fw.pool, lambda: nc.gpsimd.tensor_tensor(out=osum[:], in0=fl(o3[b][0]), in1=fl(o3[b][2]), op=ALU.add),
                              reads=[t_o3[b][0], t_o3[b][2]], writes=[t_osum])
                        fw.op(fw.pool, lambda: nc.gpsimd.tensor_tensor(out=fl(obf), in0=osum[:], in1=fl(o3[b][1]), op=ALU.add),
                              reads=[t_osum, t_o3[b][1]], writes=[t_obf])

                def load_kv(g):
                    p_ = g % 2
                    fw.dma(fw.sp, ks2[p_][:], kslcT[g], t_kv2[p_], reads=[T["kslcT"]], writes=[t_kv2[p_]])
                    fw.dma(fw.sp, kw2[p_][:], kwinT[g], t_kv2[p_], reads=[T["kwinT"]], writes=[t_kv2[p_]])
                    fw.dma(fw.sp, vsA2[p_][:, :, 0:128], vslc[:, g * 128:(g + 1) * 128].rearrange("(k p) d -> p k d", p=128), t_kv2[p_],
                           reads=[T["vslc"]], writes=[t_kv2[p_]])
                    fw.dma(fw.sp, vwA2[p_][:, :, 0:128], vwin[:, g * 128:(g + 1) * 128].rearrange("(k p) d -> p k d", p=128), t_kv2[p_],
                           reads=[T["vwin"]], writes=[t_kv2[p_]])

                def load_q(g, i, b):
                    q0 = i * 128
                    fw.dma(fw.sp, qp[b][:], qpT[4 * g:4 * g + 4, :, q0:q0 + 128].rearrange("r p t -> p r t"), t_q[b],
                           reads=[T["qpT"]], writes=[t_q[b]])
                    fw.dma(fw.sp, qr[b][:], qrT[4 * g:4 * g + 4, :, q0:q0 + 128].rearrange("r p t -> p r t"), t_q[b],
                           reads=[T["qrT"]], writes=[t_q[b]])
                    fw.dma(fw.sp, gs[b][:], ngs[q0:q0 + 128, :], t_q[b], reads=[T["ngs"]], writes=[t_q[b]])
                    fw.dma(fw.sp, nz[b][:], nzT[g * 512:(g + 1) * 512, q0:q0 + 128].rearrange("(r p) t -> p r t", p=128), t_q[b],
                           reads=[T["nzT"]], writes=[t_q[b]])

                iters = [(g_, i_) for g_ in range(4) for i_ in range(16)]
                NIT = len(iters)

                def stage_A(n_it):
                    if True:
                        g, i = iters[n_it]
                        b = n_it % 2
                        q0 = i * 128
                        ks, kw, vsA, vwA, t_kv = ks2[g % 2], kw2[g % 2], vsA2[g % 2], vwA2[g % 2], t_kv2[g % 2]
                        qpf = qp[b][:].rearrange("p r t -> p (r t)")
                        qrf = qr[b][:].rearrange("p r t -> p (r t)")
                        e_ = score_tile(kcA[:, g, :], [t_kcA], qpf, [t_q[b]], [(ident[:], cbias[:, i, :], [t_const, t_c3])])
                        pv_and_combine([(e_, 0)], vcA, [t_vcA], lambda kt: vcA[:, g, :], 161, 0, g, b, True, False)
                        sel_bias = []
                        if i >= 8:
                            fw.op(fw.dve, lambda: nc.vector.tensor_tensor(out=imp[:], in0=imp[:], in1=selc[:, i, :], op=ALU.add),
                                  reads=[t_imp, t_c3], writes=[t_imp])
                            fw.op(fw.dve, lambda: nc.vector.max(out=m1[:], in_=imp[:]), reads=[t_imp], writes=[t_m1])
                            fw.op(fw.dve, lambda: nc.vector.match_replace(out=imp3[:], in_to_replace=m1[:], in_values=imp[:],
                                                                          imm_value=-1e9),
                                  reads=[t_imp, t_m1], writes=[t_imp3])
                            fw.op(fw.dve, lambda: nc.vector.max(out=m2[:], in_=imp3[:]), reads=[t_imp3], writes=[t_m2])
                            fw.op(fw.dve, lambda: nc.vector.tensor_scalar(out=selb[:, 0:32], in0=imp[:], scalar1=m2[:, 7:8], scalar2=-1.0,
                                                                          op0=ALU.is_ge, op1=ALU.add),
                                  reads=[t_imp, t_m2], writes=[t_selb])
                        es = []
                        for kt in range(max(0, i - 4), i + 1):
                            biases = []
                            if kt == i:
                                biases.append((ident[:], caus[:], [t_const]))
                            if kt == i - 4:
                                biases.append((ident[:], wedge[:], [t_const]))
                            es.append((score_tile(kw[:, kt * 128:(kt + 1) * 128], [t_kv], qrf, [t_q[b]], biases), kt))
                        pv_and_combine(es, vwA, [t_kv], lambda kt: vwA[:, kt, :], 129, 2, g, b, False, False)

                def stage_B(n_it):
                    if True:
                        g, i = iters[n_it]
                        b = n_it % 2
                        q0 = i * 128
                        ks, kw, vsA, vwA, t_kv = ks2[g % 2], kw2[g % 2], vsA2[g % 2], vwA2[g % 2], t_kv2[g % 2]
                        qpf = qp[b][:].rearrange("p r t -> p (r t)")
                        qrf = qr[b][:].rearrange("p r t -> p (r t)")
                        if i >= 8:
                            for r in range(4):
                                fw.op(fw.pe, lambda: nc.tensor.transpose(pst[:, r * 128:(r + 1) * 128], selb[:], ident[:]),
                                      reads=[t_selb, t_const], writes=[t_pst])
                            fw.op(fw.act, lambda: nc.scalar.copy(out=selbT[:], in_=pst[:, 0:512]), reads=[t_pst], writes=[t_selbT])
                        es = []
                        for kt in range(i + 1):
                            biases = []
                            if i >= 8:
                                biases.append((expand[:, kt, :], selbT[:], [t_c3, t_selbT]))
                            if kt == i:
                                biases.append((ident[:], caus[:], [t_const]))
                            es.append((score_tile(ks[:, kt * 128:(kt + 1) * 128], [t_kv], qrf, [t_q[b]], biases), kt))
                        pv_and_combine(es, vsA, [t_kv], lambda kt: vsA[:, kt, :], 129, 1, g, b, False, True)

                def stage_C(n_it):
                    if True:
                        g, i = iters[n_it]
                        b = n_it % 2
                        q0 = i * 128
                        ks, kw, vsA, vwA, t_kv = ks2[g % 2], kw2[g % 2], vsA2[g % 2], vwA2[g % 2], t_kv2[g % 2]
                        qpf = qp[b][:].rearrange("p r t -> p (r t)")
                        qrf = qr[b][:].rearrange("p r t -> p (r t)")
                        for r in range(4):
                            fw.op(fw.pe, lambda: nc.tensor.transpose(pst[:, r * 128:(r + 1) * 128], obf[:, r, :], ident[:]),
                                  reads=[t_obf, t_const], writes=[t_pst])
                        fw.op(fw.dve, lambda: nc.vector.tensor_tensor(out=yst[b][:].rearrange("p r t -> p (r t)"), in0=pst[:, 0:512],
                                                                      in1=nz[b][:].rearrange("p r t -> p (r t)"), op=ALU.mult),
                              reads=[t_pst, t_q[b]], writes=[t_yst[b]])
                        fw.dma(fw.sp, ybT[g * 512:(g + 1) * 512, q0:q0 + 128].rearrange("(r p) t -> p r t", p=128), yst[b][:], t_yst[b],
                               reads=[t_yst[b]], writes=[T["ybT"]])

                load_kv(0)
                load_q(0, 0, 0)
                stage_A(0)
                if NIT > 1:
                    load_q(iters[1][0], iters[1][1], 1)
                for n_it in range(NIT):
                    g, i = iters[n_it]
                    if i == 0 and g + 1 < 4:
                        load_kv(g + 1)
                    stage_B(n_it)
                    if n_it + 1 < NIT:
                        stage_A(n_it + 1)
                    stage_C(n_it)
                    if n_it + 2 < NIT:
                        load_q(iters[n_it + 2][0], iters[n_it + 2][1], n_it % 2)

            with fw.scope():
                xq = [fw.sbuf("xq", [128, 512], BF16) for _ in range(2)]
                xz = [fw.sbuf("xz", [128, 512], BF16) for _ in range(2)]
                t_xq = [fw.tok("xq", dma=True, multi=True) for _ in range(2)]
                Em = [fw.sbuf("Em", [128, 512], BF16) for _ in range(4)]
                t_Em = [fw.tok("Em") for _ in range(4)]
                pss = [fw.psum("pss", [128, 512], F32) for _ in range(2)]
                t_pss = [fw.tok("pss") for _ in range(2)]
                po = [fw.psum("po", [128, 512], F32) for _ in range(4)]
                t_po = [fw.tok("po") for _ in range(4)]
                pst = fw.psum("pst", [128, 1024], BF16)
                t_pst = fw.tok("pst")
                den = fw.sbuf("den", [128, 4], F32)
                rden = fw.sbuf("rden", [128, 4], F32)
                obf = fw.sbuf("obf", [128, 4, 128], BF16)
                yst = [fw.sbuf("yst", [128, 512], BF16) for _ in range(2)]
                t_yst = [fw.tok("yst", dma=True) for _ in range(2)]
                t_den, t_rden, t_obf = fw.tok(), fw.tok(), fw.tok()
                def load_x(tq_, hx_, b_):
                    fw.dma(fw.sp, xq[b_][:], xqT[hx_][:, tq_ * 512:tq_ * 512 + 512], t_xq[b_], reads=[T["xqT"]], writes=[t_xq[b_]])
                    fw.dma(fw.sp, xz[b_][:], xzT[hx_ * 128:(hx_ + 1) * 128, tq_ * 512:tq_ * 512 + 512], t_xq[b_],
                           reads=[T["xzT"]], writes=[t_xq[b_]])

                iters4 = [(tq_, hx_) for tq_ in range(4) for hx_ in range(4)]
                load_x(0, 0, 0)
                for it, (tq, hx) in enumerate(iters4):
                    if True:
                        t0 = tq * 512
                        b = it % 2
                        if it + 1 < len(iters4):
                            load_x(iters4[it + 1][0], iters4[it + 1][1], (it + 1) % 2)
                        for mt in range(2):
                            fw.op(fw.pe, lambda: nc.tensor.matmul(pss[mt][:], lhsT=mK[:, hx, mt * 128:(mt + 1) * 128], rhs=xq[b][:],
                                                                  start=True, stop=True),
                                  reads=[t_mK, t_xq[b]], writes=[t_pss[mt]])
                            e_ = b * 2 + mt
                            fw.op(fw.act, lambda: nc.scalar.activation(out=Em[e_][:], in_=pss[mt][:], func=AF.Exp, scale=SCALE),
                                  reads=[t_pss[mt]], writes=[t_Em[e_]])
                        pair = b
                        for qs in range(4):
                            bank = pair * 2 + qs // 2
                            off = (qs % 2) * 129
                            for mt in range(2):
                                e_ = b * 2 + mt
                                fw.op(fw.pe, lambda: nc.tensor.matmul(po[bank][:, off:off + 129], lhsT=Em[e_][:, qs * 128:(qs + 1) * 128],
                                                                      rhs=mV[:, mt, hx, :], start=(mt == 0), stop=(mt == 1)),
                                      reads=[t_Em[e_], t_mV], writes=[t_po[bank]])
                        for qs in range(4):
                            bank = pair * 2 + qs // 2
                            off = (qs % 2) * 129
                            fw.op(fw.dve, lambda: nc.vector.tensor_scalar_max(out=den[:, qs:qs + 1], in0=po[bank][:, off + 128:off + 129],
                                                                              scalar1=1e-30), reads=[t_po[bank]], writes=[t_den])
                        fw.op(fw.dve, lambda: nc.vector.reciprocal(out=rden[:], in_=den[:]), reads=[t_den], writes=[t_rden])
                        for qs in range(4):
                            bank = pair * 2 + qs // 2
                            off = (qs % 2) * 129
                            fw.op(fw.dve, lambda: nc.vector.tensor_scalar_mul(out=obf[:, qs, :], in0=po[bank][:, off:off + 128],
                                                                              scalar1=rden[:, qs:qs + 1]),
                                  reads=[t_po[bank], t_rden], writes=[t_obf])
                        for qs in range(4):
                            fw.op(fw.pe, lambda: nc.tensor.transpose(pst[:, qs * 128:(qs + 1) * 128], obf[:, qs, :], ident[:]),
                                  reads=[t_obf, t_const], writes=[t_pst])
                        fw.op(fw.dve, lambda: nc.vector.tensor_tensor(out=yst[b][:], in0=pst[:, 0:512], in1=xz[b][:], op=ALU.mult),
                              reads=[t_pst, t_xq[b]], writes=[t_yst[b]])
                        fw.dma(fw.sp, yxT[hx * 128:(hx + 1) * 128, t0:t0 + 512], yst[b][:], t_yst[b], reads=[t_yst[b]], writes=[T["yxT"]])

            for tq in range(S // TT):
                t0 = tq * TT
                with fw.scope():
                    ya = fw.sbuf("ya", [128, 16, TT], BF16)
                    yb = fw.sbuf("yb", [128, 16, TT], BF16)
                    yx = fw.sbuf("yx", [128, 4, TT], BF16)
                    t_y = fw.tok("y", dma=True, multi=True)
                    fw.dma(fw.sp, ya[:], yaT[:, t0:t0 + TT].rearrange("(k p) t -> p k t", p=128), t_y, reads=[T["yaT"]], writes=[t_y])
                    fw.dma(fw.sp, yb[:], ybT[:, t0:t0 + TT].rearrange("(k p) t -> p k t", p=128), t_y, reads=[T["ybT"]], writes=[t_y])
                    fw.dma(fw.sp, yx[:], yxT[:, t0:t0 + TT].rearrange("(k p) t -> p k t", p=128), t_y, reads=[T["yxT"]], writes=[t_y])
                    ws = WStream()
                    pa = [fw.psum("pa", [128, 512], F32) for _ in range(6)]
                    t_pa = [fw.tok("pa") for _ in range(6)]
                    NG = 5
                    gt = [fw.sbuf("gt", [128, 3, TT], BF16) for _ in range(NG)]
                    t_gt = [fw.tok("gt", dma=True) for _ in range(NG)]
                    ut = fw.sbuf("ut", [128, 4, TT], F32)
                    t_ut = [fw.tok("ut") for _ in range(4)]
                    u2 = fw.sbuf("u2", [128, 512], F32)
                    t_u2 = fw.tok("u2")
                    ust = [fw.sbuf("ust", [128, TT], BF16) for _ in range(2)]
                    t_ust = [fw.tok("ust", dma=True) for _ in range(2)]
                    cn5 = {"pa": 0, "gt": 0, "us": 0}
                    gslot = {}

                    def npa():
                        p = cn5["pa"] % 6
                        cn5["pa"] += 1
                        return p

                    def hA(dg):
                        def h(wtb, t_wtb):
                            for m in range(4):
                                dc = dg * 4 + m
                                gsl = cn5["gt"] % NG
                                cn5["gt"] += 1
                                gslot[dc] = gsl
                                fw.dma(fw.sp, gt[gsl][:], mgT[:, dc * 128:(dc + 1) * 128, t0:t0 + TT].rearrange("i p t -> p i t"),
                                       t_gt[gsl], reads=[T["mgT"]], writes=[t_gt[gsl]])
                                for th in range(2):
                                    sl = slice(th * 512, (th + 1) * 512)
                                    pA = npa()
                                    for kc in range(16):
                                        fw.op(fw.pe, lambda: nc.tensor.matmul(pa[pA][:], lhsT=wtb[:, kc, m * 128:(m + 1) * 128], rhs=ya[:, kc, sl],
                                                                              start=(kc == 0), stop=(kc == 15)),
                                              reads=[t_wtb, t_y], writes=[t_pa[pA]])
                                    pX = npa()
                                    for kc in range(4):
                                        fw.op(fw.pe, lambda: nc.tensor.matmul(pa[pX][:], lhsT=wtb[:, 16 + kc, m * 128:(m + 1) * 128], rhs=yx[:, kc, sl],
                                                                              start=(kc == 0), stop=(kc == 3)),
                                              reads=[t_wtb, t_y], writes=[t_pa[pX]])
                                    fw.op(fw.dve, lambda: nc.vector.tensor_tensor(out=ut[:, m, sl], in0=pa[pA][:], in1=gt[gsl][:, 0, sl], op=ALU.mult),
                                          reads=[t_pa[pA], t_gt[gsl]], writes=[t_ut[m]])
                                    fw.op(fw.dve, lambda: nc.vector.tensor_tensor(out=u2[:], in0=pa[pX][:], in1=gt[gsl][:, 2, sl], op=ALU.mult),
                                          reads=[t_pa[pX], t_gt[gsl]], writes=[t_u2])
                                    fw.op(fw.dve, lambda: nc.vector.tensor_tensor(out=ut[:, m, sl], in0=ut[:, m, sl], in1=u2[:], op=ALU.add),
                                          reads=[t_ut[m], t_u2], writes=[t_ut[m]])
                        return h

                    def hB(dg):
                        def h(wtb, t_wtb):
                            for m in range(4):
                                dc = dg * 4 + m
                                gsl = gslot[dc]
                                us = cn5["us"] % 2
                                cn5["us"] += 1
                                for th in range(2):
                                    sl = slice(th * 512, (th + 1) * 512)
                                    pB = npa()
                                    for kc in range(16):
                                        fw.op(fw.pe, lambda: nc.tensor.matmul(pa[pB][:], lhsT=wtb[:, kc, m * 128:(m + 1) * 128], rhs=yb[:, kc, sl],
                                                                              start=(kc == 0), stop=(kc == 15)),
                                              reads=[t_wtb, t_y], writes=[t_pa[pB]])
                                    fw.op(fw.dve, lambda: nc.vector.tensor_tensor(out=u2[:], in0=pa[pB][:], in1=gt[gsl][:, 1, sl], op=ALU.mult),
                                          reads=[t_pa[pB], t_gt[gsl]], writes=[t_u2])
                                    fw.op(fw.dve, lambda: nc.vector.tensor_tensor(out=ust[us][:, sl], in0=ut[:, m, sl], in1=u2[:], op=ALU.add),
                                          reads=[t_ut[m], t_u2], writes=[t_ust[us]])
                                fw.dma(fw.sp, uT_s[dc * 128:(dc + 1) * 128, t0:t0 + TT], ust[us][:], t_ust[us],
                                       reads=[t_ust[us]], writes=[T["uT_s"]])
                        return h

                    groups = []
                    for dg in range(8):
                        c0 = dg * 512
                        groups.append(([(wua_in[l][:, c0:c0 + 512], 0, 16, 0, 512), (wux_in[l][:, c0:c0 + 512], 16, 4, 0, 512)], hA(dg)))
                        groups.append(([(wub_in[l][:, c0:c0 + 512], 0, 16, 0, 512)], hB(dg)))
                    ws.run(groups)

            for tq in range(S // TT):
                t0 = tq * TT
                with fw.scope():
                    uT = fw.sbuf("uT", [128, KC, TT], BF16)
                    t_uT = fw.tok("uT", dma=True, multi=True)
                    for k4 in range(0, KC, 8):
                        fw.dma(fw.sp, uT[:, k4:k4 + 8, :], uT_s[k4 * 128:(k4 + 8) * 128, t0:t0 + TT].rearrange("(k p) t -> p k t", p=128),
                               t_uT, reads=[T["uT_s"]], writes=[t_uT])
                    ws = WStream()
                    pa = [fw.psum("pa", [128, 512], F32) for _ in range(4)]
                    t_pa = [fw.tok("pa") for _ in range(4)]
                    xres = [fw.sbuf("xres", [128, TT], F32) for _ in range(2)]
                    t_xres = [fw.tok("xres", dma=True) for _ in range(2)]
                    xo = [fw.sbuf("xo", [128, TT], F32) for _ in range(2)]
                    t_xo = [fw.tok("xo", dma=True) for _ in range(2)]
                    cn6 = {"pa": 0, "x": 0}

                    def hO(dg):
                        def h(wtb, t_wtb):
                            for m in range(4):
                                dc = dg * 4 + m
                                xb_ = cn6["x"] % 2
                                cn6["x"] += 1
                                fw.dma(fw.sp, xres[xb_][:], xT_s[dc, :, t0:t0 + TT], t_xres[xb_], reads=[T["xT_s"]], writes=[t_xres[xb_]])
                                for th in range(2):
                                    sl = slice(th * 512, (th + 1) * 512)
                                    pD = cn6["pa"] % 4
                                    cn6["pa"] += 1
                                    for kc in range(KC):
                                        fw.op(fw.pe, lambda: nc.tensor.matmul(pa[pD][:], lhsT=wtb[:, kc, m * 128:(m + 1) * 128], rhs=uT[:, kc, sl],
                                                                              start=(kc == 0), stop=(kc == KC - 1)),
                                              reads=[t_wtb, t_uT], writes=[t_pa[pD]])
                                    fw.op(fw.dve, lambda: nc.vector.tensor_tensor(out=xo[xb_][:, sl], in0=pa[pD][:], in1=xres[xb_][:, sl], op=ALU.add),
                                          reads=[t_pa[pD], t_xres[xb_]], writes=[t_xo[xb_]])
                                fw.dma(fw.sp, xT_s[dc, :, t0:t0 + TT], xo[xb_][:], t_xo[xb_], reads=[t_xo[xb_]], writes=[T["xT_s"]])
                        return h

                    ws.run([([(wo_in[l][:, dg * 512:dg * 512 + 512], 0, KC, 0, 512)], hO(dg)) for dg in range(8)])

    norm_phase(xT_s, T["xT_s"], fgT_in, S, dst_dram=out, dst_tok=T["out"])
    fw.barrier()
    n_instr = fw.n_instr
    fw.close()
    return nc, n_instr


def _consts():
    bf = ml_dtypes.bfloat16
    half = DH // 2
    inv_freq = (np.float32(10000.0) ** (-np.arange(half, dtype=np.float32) / np.float32(half))).astype(np.float32)
    ang = np.arange(S, dtype=np.float32)[None, :] * inv_freq[:, None]
    cos = np.cos(ang).astype(np.float32)
    sin = np.sin(ang).astype(np.float32)
    c = {}
    c["c_cos"] = np.ascontiguousarray(np.concatenate([cos, cos], 0))
    c["c_sin"] = np.ascontiguousarray(np.concatenate([-sin, sin], 0))
    c["c_ident"] = np.eye(128, dtype=np.float32).astype(bf)
    k = np.arange(128)[:, None]
    q = np.arange(128)[None, :]
    c["c_caus"] = np.ascontiguousarray(np.tile(np.where(k <= q, 0.0, NEGB).astype(np.float32).astype(bf), (1, 4)))
    c["c_wedge"] = np.ascontiguousarray(np.tile(np.where(k > q, 0.0, NEGB).astype(np.float32).astype(bf), (1, 4)))
    n = np.arange(128)[:, None, None]
    i = np.arange(16)[None, :, None]
    qq = np.arange(128)[None, None, :]
    t = 128 * i + qq
    c["c_cbias"] = np.ascontiguousarray(np.tile(np.where((16 * n + 31 <= t) & (n < 127), 0.0, NEGB).astype(np.float32).astype(bf), (1, 1, 4)))
    j = np.arange(128)[:, None, None]
    kt = np.arange(16)[None, :, None]
    key = np.arange(128)[None, None, :]
    c["c_expand"] = np.where((j == 2 * kt + key // 64) & (j < 32), -NEGB, 0.0).astype(np.float32).astype(bf)
    qv = np.arange(128)[:, None, None]
    iv = np.arange(16)[None, :, None]
    jv = np.arange(32)[None, None, :]
    tt = 128 * iv + qv
    cur = tt // 64
    future = jv > cur
    forced = (jv == 0) | (jv == cur) | (jv == cur - 1)
    c["c_selc"] = np.where(future, -10.0, np.where(forced, 10.0, 0.0)).astype(np.float32)
    nn = np.arange(128)[:, None]
    jj = np.arange(32)[None, :]
    ovl = ((16 * nn < 64 * jj + 64) & (16 * nn + 32 > 64 * jj) & (nn < 127))
    c["c_ovl"] = ovl.astype(np.float32).astype(bf)
    return c


def _prep_shared(inp, depth):
    f = lambda a: np.ascontiguousarray(np.asarray(a, dtype=np.float32))
    sh = {}
    sh["gT"] = f(np.asarray(inp["norm_g"])[:depth].reshape(depth, KC, 128).transpose(0, 2, 1))
    sh["mgnT"] = f(np.asarray(inp["mem_norm_g"])[:depth].reshape(depth, KC, 128).transpose(0, 2, 1))
    sh["fgT"] = f(np.asarray(inp["final_g"]).reshape(KC, 128).T)
    sh["w_in"] = f(np.asarray(inp["w_in"])[:depth])
    sh["cw"] = f(np.asarray(inp["conv_w"])[:depth].reshape(depth, 3, 16, 128).transpose(0, 3, 2, 1))
    sh["cb"] = f(np.asarray(inp["conv_b"])[:depth].reshape(depth, 16, 128).transpose(0, 2, 1))
    sh["posT"] = f(np.asarray(inp["cmp_pos"])[:depth].transpose(0, 1, 3, 2))
    sh["w1"] = f(np.asarray(inp["cmp_w1"])[:depth])
    sh["w2"] = f(np.asarray(inp["cmp_w2"])[:depth])
    sh["wmk"] = f(np.asarray(inp["w_mem_kv"])[:depth])
    sh["wua"] = f(np.asarray(inp["w_up_a"])[:depth])
    sh["wub"] = f(np.asarray(inp["w_up_b"])[:depth])
    sh["wux"] = f(np.asarray(inp["w_up_x"])[:depth])
    sh["wo"] = f(np.asarray(inp["w_out"])[:depth])
    sh.update(_consts())
    return sh


_CACHE = {}


def run(inp, depth=DEPTH, dbg=False, ncores=8, trace=False):
    key = (depth, dbg)
    if key not in _CACHE:
        _CACHE[key] = build(depth, dbg)
    nc, _ = _CACHE[key]
    sh = _prep_shared(inp, depth)
    x = np.asarray(inp["x"], dtype=np.float32)
    mem = np.asarray(inp["mem"], dtype=np.float32)
    B = x.shape[0]
    in_maps = []
    for c in range(ncores):
        b = c % B
        m = dict(sh)
        m["xT"] = np.ascontiguousarray(x[b].T).reshape(KC, 128, S)
        m["memT"] = np.ascontiguousarray(mem[b].T).reshape(KC, 128, MEM)
        in_maps.append(m)
    res = run_bass_kernel_spmd(nc, in_maps, core_ids=list(range(ncores)), trace=trace)
    return res


def kernel(**inputs):
    res = run(inputs)
    B = np.asarray(inputs["x"]).shape[0]
    outs = []
    for b in range(B):
        oT = np.asarray(res.results[b]["outT"]).reshape(D, S)
        outs.append(np.ascontiguousarray(oT.T))
    return np.stack(outs, 0).astype(np.float32)
```

```python
import numpy as np
import ml_dtypes
import concourse.bass as bass
import concourse.mybir as mybir
from concourse.bass_utils import run_bass_kernel_spmd

F32 = mybir.dt.float32
BF16 = mybir.dt.bfloat16
AF = mybir.ActivationFunctionType
ALU = mybir.AluOpType

D = 4096
KC = 32
S = 2048
DEPTH = 4
MEM = 256
DH = 128
INW = 28720
TT = 1024
EPS = 1e-6
SCALE = DH ** -0.5
NEGB = -30000.0
O_AH, O_AB, O_AC, O_AZ, O_Q, O_KV, O_NG, O_NZ, O_XQ, O_XZ, O_MG = (
    0, 2048, 4096, 6144, 8192, 10240, 13312, 13360, 15408, 15920, 16432)


class Tok:
    __slots__ = ("w", "r", "multi", "dsem", "dcount", "name")

    def __init__(self, name="", multi=False):
        self.w = {}
        self.r = {}
        self.multi = multi
        self.dsem = None
        self.dcount = 0
        self.name = name


class Eng:
    def __init__(self, name, h, sem):
        self.name = name
        self.h = h
        self.sem = sem
        self.count = 0
        self.known = {}


class Scope:
    def __init__(self, fw):
        self.fw = fw

    def __enter__(self):
        self.mark = len(self.fw._ctx)
        self.fw._scope_toks.append([])
        return self

    def __exit__(self, *a):
        self.fw.barrier()
        while len(self.fw._ctx) > self.mark:
            self.fw._ctx.pop().__exit__(None, None, None)
        for t in self.fw._scope_toks.pop():
            self.fw._sem_pool.append((t.dsem, t.dcount))
        return False


class FW:
    def __init__(self, nc):
        self.nc = nc
        self._ctx = []
        self._sem_ctx = []
        self._sem_pool = []
        self._scope_toks = []
        self.all_handles = {}
        self.pe = self._mk("pe", nc.tensor)
        self.act = self._mk("act", nc.scalar)
        self.dve = self._mk("dve", nc.vector)
        self.pool = self._mk("pool", nc.gpsimd)
        self.sp = self._mk("sp", nc.sync)
        self.engs = [self.pe, self.act, self.dve, self.pool, self.sp]
        self.n_instr = 0
        self._uid = 0

    def _enter(self, cm):
        v = cm.__enter__()
        self._ctx.append(cm)
        return v

    def close(self):
        while self._ctx:
            self._ctx.pop().__exit__(None, None, None)
        while self._sem_ctx:
            self._sem_ctx.pop().__exit__(None, None, None)

    def scope(self):
        return Scope(self)

    def _mk(self, name, h):
        cm = self.nc.semaphore("p_" + name)
        sem = cm.__enter__()
        self._sem_ctx.append(cm)
        return Eng(name, h, sem)

    def _nm(self, name):
        self._uid += 1
        return f"{name}_{self._uid}"

    def sbuf(self, name, shape, dt):
        return self._enter(self.nc.sbuf_tensor(self._nm(name), shape, dt))

    def psum(self, name, shape, dt):
        return self._enter(self.nc.psum_tensor(self._nm(name), shape, dt))

    def tok(self, name="", dma=False, multi=False):
        t = Tok(name, multi)
        if dma:
            if self._sem_pool:
                t.dsem, t.dcount = self._sem_pool.pop()
            else:
                cm = self.nc.semaphore(self._nm("d_" + name))
                t.dsem = cm.__enter__()
                self._sem_ctx.append(cm)
                t.dcount = 0
            if self._scope_toks:
                self._scope_toks[-1].append(t)
        return t

    def _needs(self, reads, writes):
        needs = {}

        def add(d):
            for s, v in d.items():
                if needs.get(s, 0) < v:
                    needs[s] = v
        for t in reads:
            add(t.w)
        for t in writes:
            add(t.r)
            if not t.multi:
                add(t.w)
        return needs

    def _emit_waits(self, eng, needs):
        for s, v in needs.items():
            if eng.name == "pe" and s is eng.sem:
                continue
            if eng.known.get(s, 0) < v:
                eng.h.wait_ge(s, v)
                eng.known[s] = v

    def _commit(self, handle, reads, writes):
        s, v = handle
        if self.all_handles.get(s, 0) < v:
            self.all_handles[s] = v
        for t in writes:
            if t.multi:
                if t.r:
                    t.w = {}
                    t.r = {}
                if t.w.get(s, 0) < v:
                    t.w[s] = v
            else:
                t.w = {s: v}
                t.r = {}
        for t in reads:
            if t.r.get(s, 0) < v:
                t.r[s] = v

    def op(self, eng, fn, reads=(), writes=()):
        self._emit_waits(eng, self._needs(reads, writes))
        ins = fn()
        eng.count += 1
        ins.then_inc(eng.sem, 1)
        self._commit((eng.sem, eng.count), reads, writes)
        self.n_instr += 1
        return ins

    def dma(self, q, out, in_, semtok, reads=(), writes=(), **kw):
        self._emit_waits(q, self._needs(reads, writes))
        ins = q.h.dma_start(out=out, in_=in_, **kw)
        semtok.dcount += 16
        ins.then_inc(semtok.dsem, 16)
        self._commit((semtok.dsem, semtok.dcount), reads, writes)
        self.n_instr += 1
        return ins

    def coll(self, kind, src, dst, rg, semtok, reads=(), writes=()):
        q = self.pool
        self._emit_waits(q, self._needs(reads, writes))
        ins = self.nc.gpsimd.collective_compute(kind, ALU.bypass, replica_groups=rg, ins=[src.opt()], outs=[dst.opt()])
        semtok.dcount += 16
        ins.then_inc(semtok.dsem, 16)
        self._commit((semtok.dsem, semtok.dcount), reads, writes)
        self.n_instr += 1
        return ins

    def barrier(self):
        for e in self.engs:
            self._emit_waits(e, dict(self.all_handles))


def build(depth=DEPTH, dbg=False):
    nc = bass.Bass("TRN2", target_bir_lowering=False)
    fw = FW(nc)

    def din(name, shape, dt=F32):
        return nc.dram_tensor(name, list(shape), dt, kind="ExternalInput").ap()

    def dscr(name, shape, dt):
        kind = "ExternalOutput" if dbg else "Internal"
        return nc.dram_tensor(name, list(shape), dt, kind=kind).ap()

    xT_in = din("xT", [KC, 128, S])
    memT_in = din("memT", [KC, 128, MEM])
    gT_in = din("gT", [depth, 128, KC])
    mgT_in = din("mgnT", [depth, 128, KC])
    fgT_in = din("fgT", [128, KC])
    w_in = din("w_in", [depth, D, INW])
    cw_in = din("cw", [depth, 128, 16, 3])
    cb_in = din("cb", [depth, 128, 16])
    posT_in = din("posT", [depth, 2, 128, 32])
    w1_in = din("w1", [depth, 2, D, 128])
    w2_in = din("w2", [depth, 2, 128, 128])
    wmk_in = din("wmk", [depth, D, 1024])
    wua_in = din("wua", [depth, 2048, D])
    wub_in = din("wub", [depth, 2048, D])
    wux_in = din("wux", [depth, 512, D])
    wo_in = din("wo", [depth, D, D])
    cos_in = din("c_cos", [128, S])
    sin_in = din("c_sin", [128, S])
    ident_in = din("c_ident", [128, 128], BF16)
    caus_in = din("c_caus", [128, 512], BF16)
    wedge_in = din("c_wedge", [128, 512], BF16)
    cbias_in = din("c_cbias", [128, 16, 512], BF16)
    expand_in = din("c_expand", [128, 16, 128], BF16)
    selc_in = din("c_selc", [128, 16, 32])
    ovl_in = din("c_ovl", [128, 32], BF16)
    out = nc.dram_tensor("outT", [KC, 128, S], F32, kind="ExternalOutput").ap()

    xT_s = dscr("xT_s", [KC, 128, S], F32)
    yaT = dscr("yaT", [2048, S], BF16)
    qpT = dscr("qpT", [16, 128, S], BF16)
    qrT = dscr("qrT", [16, 128, S], BF16)
    kcmpT = dscr("kcmpT", [4, 128, S], BF16)
    vcmpT = dscr("vcmpT", [4, 128, S], BF16)
    kslcT = dscr("kslcT", [4, 128, S], BF16)
    kwinT = dscr("kwinT", [4, 128, S], BF16)
    vslc = dscr("vslc", [S, 512], BF16)
    vwin = dscr("vwin", [S, 512], BF16)
    ngs = dscr("ngs", [S, 48], F32)
    nzT = dscr("nzT", [2048, S], BF16)
    xqT = dscr("xqT", [4, 128, S], BF16)
    xzT = dscr("xzT", [512, S], BF16)
    mgT = dscr("mgT", [3, D, S], BF16)
    ybT = dscr("ybT", [2048, S], BF16)
    yxT = dscr("yxT", [512, S], BF16)
    uT_s = dscr("uT_s", [D, S], BF16)

    T = {n: fw.tok(n, multi=True) for n in
         ["xT_s", "yaT", "qpT", "qrT", "kcmpT", "vcmpT", "kslcT", "kwinT", "vslc", "vwin", "ngs", "nzT",
          "xqT", "xzT", "mgT", "ybT", "yxT", "uT_s", "out"]}
    t_ext = fw.tok("ext", multi=True)

    ident = fw.sbuf("ident", [128, 128], BF16)
    caus = fw.sbuf("caus", [128, 512], BF16)
    wedge = fw.sbuf("wedge", [128, 512], BF16)
    ones = fw.sbuf("ones", [128, 128], BF16)
    t_const = fw.tok("const", dma=True)
    for dst, src in ((ident, ident_in), (caus, caus_in), (wedge, wedge_in)):
        fw.dma(fw.sp, dst[:], src, t_const, writes=[t_const])
    t_ones = fw.tok("ones")
    fw.op(fw.dve, lambda: nc.vector.memset(ones[:], 1.0), writes=[t_ones])
    t_cp = fw.tok("cp", dma=True)
    for k in range(0, KC, 8):
        fw.dma(fw.sp, xT_s[k:k + 8], xT_in[k:k + 8], t_cp, reads=[t_ext], writes=[T["xT_s"]])
    fw.barrier()

    def norm_phase(src, src_tok, g_dram, ntok, dst_hT=None, t_hT=None, hoff=0, dst_dram=None, dst_tok=None):
        with fw.scope():
            gsb = fw.sbuf("gsb", [128, KC], F32)
            t_g = fw.tok("g", dma=True)
            fw.dma(fw.sp, gsb[:], g_dram, t_g, reads=[t_ext], writes=[t_g])
            xs = [fw.sbuf("xs", [128, KC, 128], F32) for _ in range(2)]
            t_xs = [fw.tok("xs", dma=True) for _ in range(2)]
            sq = fw.sbuf("sq", [128, KC, 128], BF16)
            t_sq = fw.tok("sq")
            psn = fw.psum("psn", [128, 512], F32)
            t_psn = fw.tok("psn")
            sd = fw.sbuf("sd", [128, 128], F32)
            t_sd = fw.tok("sd")
            rstd = [fw.sbuf("rstd", [128, 128], F32) for _ in range(2)]
            t_rstd = [fw.tok("rstd") for _ in range(2)]
            if dst_dram is not None:
                ob = [fw.sbuf("ob", [128, KC, 128], F32) for _ in range(2)]
                t_ob = [fw.tok("ob", dma=True, multi=True) for _ in range(2)]
            for i in range(ntok // 128):
                b = i % 2
                t0 = i * 128
                fw.dma(fw.sp, xs[b][:], src[:, :, t0:t0 + 128].rearrange("k p t -> p k t"), t_xs[b],
                       reads=[src_tok], writes=[t_xs[b]])
                fw.op(fw.act, lambda: nc.scalar.activation(out=sq[:], in_=xs[b][:], func=AF.Square),
                      reads=[t_xs[b]], writes=[t_sq])
                for kc in range(KC):
                    fw.op(fw.pe, lambda: nc.tensor.matmul(psn[:, 0:128], lhsT=ones[:], rhs=sq[:, kc, :],
                                                          start=(kc == 0), stop=(kc == KC - 1)),
                          reads=[t_sq, t_ones], writes=[t_psn])
                fw.op(fw.dve, lambda: nc.vector.tensor_scalar(out=sd[:], in0=psn[:, 0:128], scalar1=1.0 / D,
                                                              scalar2=EPS, op0=ALU.mult, op1=ALU.add),
                      reads=[t_psn], writes=[t_sd])
                fw.op(fw.act, lambda: nc.scalar.activation(out=sd[:], in_=sd[:], func=AF.Sqrt),
                      reads=[t_sd], writes=[t_sd])
                fw.op(fw.dve, lambda: nc.vector.reciprocal(out=rstd[b][:], in_=sd[:]),
                      reads=[t_sd], writes=[t_rstd[b]])
                for kc in range(KC):
                    if dst_dram is None:
                        o_ap, o_tok = dst_hT[:, kc, hoff + t0:hoff + t0 + 128], t_hT
                    else:
                        o_ap, o_tok = ob[b][:, kc, :], t_ob[b]
                    fw.op(fw.dve, lambda: nc.vector.scalar_tensor_tensor(
                        out=o_ap, in0=xs[b][:, kc, :], scalar=gsb[:, kc:kc + 1], in1=rstd[b][:],
                        op0=ALU.mult, op1=ALU.mult), reads=[t_xs[b], t_rstd[b], t_g], writes=[o_tok])
                if dst_dram is not None:
                    fw.dma(fw.sp, dst_dram[:, :, t0:t0 + 128].rearrange("k p t -> p k t"), ob[b][:], t_ob[b],
                           reads=[t_ob[b]], writes=[dst_tok])

    class WStream:
        def __init__(self):
            self.wt = [fw.sbuf("wt", [128, KC, 512], BF16) for _ in range(2)]
            self.t_wt = [fw.tok("wt", dma=True) for _ in range(2)]

        def run(self, groups):
            def load(gi):
                b = gi % 2
                for (src, kc0, nkc, col0, ncols) in groups[gi][0]:
                    fw.dma(fw.pool, self.wt[b][:, kc0:kc0 + nkc, col0:col0 + ncols],
                           src.rearrange("(k p) c -> p k c", p=128), self.t_wt[b],
                           reads=[t_ext], writes=[self.t_wt[b]])
            load(0)
            for gi in range(len(groups)):
                if gi + 1 < len(groups):
                    load(gi + 1)
                groups[gi][1](self.wt[gi % 2], self.t_wt[gi % 2])

    for l in range(depth):
        with fw.scope():
            halo = fw.sbuf("halo", [128, 16, 2], F32)
            t_halo = fw.tok("halo")
            fw.op(fw.dve, lambda: nc.vector.memset(halo[:], 0.0), writes=[t_halo])
            mK = fw.sbuf("mK", [128, 4, MEM], BF16)
            t_mK = fw.tok("mK")
            mV = fw.sbuf("mV", [128, 2, 4, 129], BF16)
            t_mV = fw.tok("mV")
            fw.op(fw.dve, lambda: nc.vector.memset(mV[:], 1.0), writes=[t_mV])

            with fw.scope():
                mhT = fw.sbuf("mhT", [128, KC, MEM], BF16)
                t_mhT = fw.tok("mhT", multi=True)
                norm_phase(memT_in, t_ext, mgT_in[l], MEM, dst_hT=mhT, t_hT=t_mhT)
                ws = WStream()
                pa = [fw.psum("pa", [128, 512], F32) for _ in range(2)]
                t_pa = [fw.tok("pa") for _ in range(2)]

                def hK(wtb, t_wtb):
                    for hx in range(4):
                        p = hx % 2
                        for kc in range(KC):
                            fw.op(fw.pe, lambda: nc.tensor.matmul(pa[p][:, 0:MEM], lhsT=wtb[:, kc, hx * 128:(hx + 1) * 128],
                                                                  rhs=mhT[:, kc, :], start=(kc == 0), stop=(kc == KC - 1)),
                                  reads=[t_wtb, t_mhT], writes=[t_pa[p]])
                        fw.op(fw.act, lambda: nc.scalar.copy(out=mK[:, hx, :], in_=pa[p][:, 0:MEM]),
                              reads=[t_pa[p]], writes=[t_mK])

                def hV(wtb, t_wtb):
                    for mt in range(2):
                        p = mt % 2
                        for kc in range(KC):
                            fw.op(fw.pe, lambda: nc.tensor.matmul(pa[p][:], lhsT=mhT[:, kc, mt * 128:(mt + 1) * 128],
                                                                  rhs=wtb[:, kc, :], start=(kc == 0), stop=(kc == KC - 1)),
                                  reads=[t_wtb, t_mhT], writes=[t_pa[p]])
                        fw.op(fw.act, lambda: nc.scalar.copy(out=mV[:, mt, :, 0:128],
                                                             in_=pa[p][:].rearrange("p (h d) -> p h d", h=4)),
                              reads=[t_pa[p]], writes=[t_mV])
                ws.run([([(wmk_in[l][:, 0:512], 0, KC, 0, 512)], hK),
                        ([(wmk_in[l][:, 512:1024], 0, KC, 0, 512)], hV)])

            for Tt in range(S // TT):
                tk0 = Tt * TT
                with fw.scope():
                    hT = fw.sbuf("hT", [128, KC, TT], BF16)
                    t_hT = fw.tok("hT", multi=True)
                    norm_phase(xT_s[:, :, tk0:tk0 + TT], T["xT_s"], gT_in[l], TT, dst_hT=hT, t_hT=t_hT)
                    ws = WStream()
                    pa = [fw.psum("pa", [128, 512], F32) for _ in range(4)]
                    t_pa = [fw.tok("pa") for _ in range(4)]
                    stg = [fw.sbuf("stg", [128, TT], BF16) for _ in range(4)]
                    t_stg = [fw.tok("stg", dma=True) for _ in range(4)]
                    cnt = {"pa": 0, "stg": 0, "ng": 0}
                    cosb = fw.sbuf("cosb", [128, TT], F32)
                    sinb = fw.sbuf("sinb", [128, TT], F32)
                    cwb = fw.sbuf("cwb", [128, 16, 3], F32)
                    cbb = fw.sbuf("cbb", [128, 16], F32)
                    t_tab = fw.tok("tab", dma=True)
                    fw.dma(fw.sp, cosb[:], cos_in[:, tk0:tk0 + TT], t_tab, reads=[t_ext], writes=[t_tab])
                    fw.dma(fw.sp, sinb[:], sin_in[:, tk0:tk0 + TT], t_tab, reads=[t_ext], writes=[t_tab])
                    fw.dma(fw.sp, cwb[:], cw_in[l], t_tab, reads=[t_ext], writes=[t_tab])
                    fw.dma(fw.sp, cbb[:], cb_in[l], t_tab, reads=[t_ext], writes=[t_tab])
                    x32 = fw.sbuf("x32", [128, 512], F32)
                    xsw = fw.sbuf("xsw", [128, 512], F32)
                    r1 = fw.sbuf("r1", [128, 512], F32)
                    r2 = fw.sbuf("r2", [128, 512], F32)
                    t_x32, t_xsw, t_r1, t_r2 = fw.tok(), fw.tok(), fw.tok(), fw.tok()
                    ah = fw.sbuf("ah", [128, 512], F32)
                    aB = fw.sbuf("aB", [128, 512], F32)
                    sz = fw.sbuf("sz", [128, 512], F32)
                    yv = fw.sbuf("yv", [128, 512], F32)
                    ub = fw.sbuf("ub", [128, 514], F32)
                    t_ah, t_aB, t_sz, t_yv, t_ub = fw.tok(), fw.tok(), fw.tok(), fw.tok(), fw.tok()
                    ngst = [fw.sbuf("ngst", [128, 48], F32) for _ in range(2)]
                    t_ngst = [fw.tok("ngst", dma=True) for _ in range(2)]

                    def accum_fm(wtb, t_wtb, m, th):
                        p = cnt["pa"] % 4
                        cnt["pa"] += 1
                        for kc in range(KC):
                            fw.op(fw.pe, lambda: nc.tensor.matmul(pa[p][:], lhsT=wtb[:, kc, m * 128:(m + 1) * 128],
                                                                  rhs=hT[:, kc, th * 512:(th + 1) * 512],
                                                                  start=(kc == 0), stop=(kc == KC - 1)),
                                  reads=[t_wtb, t_hT], writes=[t_pa[p]])
                        return p

                    def next_stg():
                        s_ = cnt["stg"] % 4
                        cnt["stg"] += 1
                        return s_

                    def rope_evac(p, th, sdst):
                        fw.op(fw.act, lambda: nc.scalar.copy(out=x32[:], in_=pa[p][:]), reads=[t_pa[p]], writes=[t_x32])
                        fw.op(fw.act, lambda: nc.scalar.copy(out=xsw[64:128, :], in_=pa[p][0:64, :]),
                              reads=[t_pa[p]], writes=[t_xsw])
                        fw.op(fw.dve, lambda: nc.vector.tensor_copy(out=xsw[0:64, :], in_=pa[p][64:128, :]),
                              reads=[t_pa[p]], writes=[t_xsw])
                        fw.op(fw.dve, lambda: nc.vector.tensor_tensor(out=r1[:], in0=x32[:], in1=cosb[:, th * 512:(th + 1) * 512],
                                                                      op=ALU.mult), reads=[t_x32, t_tab], writes=[t_r1])
                        fw.op(fw.dve, lambda: nc.vector.tensor_tensor(out=r2[:], in0=xsw[:], in1=sinb[:, th * 512:(th + 1) * 512],
                                                                      op=ALU.mult), reads=[t_xsw, t_tab], writes=[t_r2])
                        fw.op(fw.dve, lambda: nc.vector.tensor_tensor(out=stg[sdst][:, th * 512:(th + 1) * 512], in0=r1[:],
                                                                      in1=r2[:], op=ALU.add),
                              reads=[t_r1, t_r2], writes=[t_stg[sdst]])

                    def fm_handler(chunks):
                        def h(wtb, t_wtb):
                            for m, ch in enumerate(chunks):
                                kind = ch[0]
                                s1 = next_stg()
                                s2 = next_stg() if kind == "qrope" else None
                                for th in range(2):
                                    p = accum_fm(wtb, t_wtb, m, th)
                                    sl = slice(th * 512, (th + 1) * 512)
                                    if kind in ("copy", "silu", "sigmoid"):
                                        func = {"copy": AF.Copy, "silu": AF.Silu, "sigmoid": AF.Sigmoid}[kind]
                                        fw.op(fw.act, lambda: nc.scalar.activation(out=stg[s1][:, sl], in_=pa[p][:], func=func),
                                              reads=[t_pa[p]], writes=[t_stg[s1]])
                                    elif kind == "rope":
                                        rope_evac(p, th, s1)
                                    elif kind == "qrope":
                                        fw.op(fw.act, lambda: nc.scalar.copy(out=stg[s2][:, sl], in_=pa[p][:]),
                                              reads=[t_pa[p]], writes=[t_stg[s2]])
                                        rope_evac(p, th, s1)
                                fw.dma(fw.sp, ch[1][:, tk0:tk0 + TT], stg[s1][:], t_stg[s1], reads=[t_stg[s1]], writes=[ch[2]])
                                if kind == "qrope":
                                    fw.dma(fw.sp, ch[3][:, tk0:tk0 + TT], stg[s2][:], t_stg[s2], reads=[t_stg[s2]], writes=[ch[4]])
                        return h

                    def conv_handler(cb):
                        def h(wtb, t_wtb):
                            s1 = next_stg()
                            for th in range(2):
                                sl = slice(th * 512, (th + 1) * 512)
                                if th == 0:
                                    fw.op(fw.dve, lambda: nc.vector.tensor_copy(out=ub[:, 0:2], in_=halo[:, cb, :]),
                                          reads=[t_halo], writes=[t_ub])
                                else:
                                    fw.op(fw.dve, lambda: nc.vector.tensor_copy(out=ub[:, 0:2], in_=ub[:, 512:514]),
                                          reads=[t_ub], writes=[t_ub])
                                p = accum_fm(wtb, t_wtb, 0, th)
                                fw.op(fw.act, lambda: nc.scalar.copy(out=ah[:], in_=pa[p][:]), reads=[t_pa[p]], writes=[t_ah])
                                p = accum_fm(wtb, t_wtb, 1, th)
                                fw.op(fw.act, lambda: nc.scalar.copy(out=aB[:], in_=pa[p][:]), reads=[t_pa[p]], writes=[t_aB])
                                p = accum_fm(wtb, t_wtb, 2, th)
                                fw.op(fw.dve, lambda: nc.vector.tensor_tensor(out=ub[:, 2:514], in0=pa[p][:], in1=ah[:], op=ALU.mult),
                                      reads=[t_pa[p], t_ah], writes=[t_ub])
                                p = accum_fm(wtb, t_wtb, 3, th)
                                fw.op(fw.act, lambda: nc.scalar.activation(out=sz[:], in_=pa[p][:], func=AF.Silu),
                                      reads=[t_pa[p]], writes=[t_sz])
                                fw.op(fw.dve, lambda: nc.vector.tensor_scalar(out=yv[:], in0=ub[:, 2:514], scalar1=cwb[:, cb, 2:3],
                                                                              scalar2=cbb[:, cb:cb + 1], op0=ALU.mult, op1=ALU.add),
                                      reads=[t_ub, t_tab], writes=[t_yv])
                                fw.op(fw.dve, lambda: nc.vector.scalar_tensor_tensor(out=yv[:], in0=ub[:, 1:513], scalar=cwb[:, cb, 1:2],
                                                                                     in1=yv[:], op0=ALU.mult, op1=ALU.add),
                                      reads=[t_ub, t_tab, t_yv], writes=[t_yv])
                                fw.op(fw.dve, lambda: nc.vector.scalar_tensor_tensor(out=yv[:], in0=ub[:, 0:512], scalar=cwb[:, cb, 0:1],
                                                                                     in1=yv[:], op0=ALU.mult, op1=ALU.add),
                                      reads=[t_ub, t_tab, t_yv], writes=[t_yv])
                                fw.op(fw.dve, lambda: nc.vector.tensor_tensor(out=yv[:], in0=yv[:], in1=aB[:], op=ALU.mult),
                                      reads=[t_yv, t_aB], writes=[t_yv])
                                fw.op(fw.dve, lambda: nc.vector.tensor_tensor(out=stg[s1][:, sl], in0=yv[:], in1=sz[:], op=ALU.mult),
                                      reads=[t_yv, t_sz], writes=[t_stg[s1]])
                            fw.op(fw.dve, lambda: nc.vector.tensor_copy(out=halo[:, cb, :], in_=ub[:, 512:514]),
                                  reads=[t_ub], writes=[t_halo])
                            fw.dma(fw.sp, yaT[cb * 128:(cb + 1) * 128, tk0:tk0 + TT], stg[s1][:], t_stg[s1],
                                   reads=[t_stg[s1]], writes=[T["yaT"]])
                        return h

                    def tm_handler(ncols, dst, dst_tok, kind):
                        def h(wtb, t_wtb):
                            for tb in range(TT // 128):
                                p = cnt["pa"] % 4
                                cnt["pa"] += 1
                                for kc in range(KC):
                                    fw.op(fw.pe, lambda: nc.tensor.matmul(pa[p][:, 0:ncols], lhsT=hT[:, kc, tb * 128:(tb + 1) * 128],
                                                                          rhs=wtb[:, kc, 0:ncols], start=(kc == 0), stop=(kc == KC - 1)),
                                          reads=[t_wtb, t_hT], writes=[t_pa[p]])
                                r0 = tk0 + tb * 128
                                if kind == "v":
                                    s1 = next_stg()
                                    fw.op(fw.act, lambda: nc.scalar.copy(out=stg[s1][:, 0:512], in_=pa[p][:]),
                                          reads=[t_pa[p]], writes=[t_stg[s1]])
                                    fw.dma(fw.sp, dst[r0:r0 + 128, :], stg[s1][:, 0:512], t_stg[s1], reads=[t_stg[s1]], writes=[dst_tok])
                                else:
                                    n_ = cnt["ng"] % 2
                                    cnt["ng"] += 1
                                    fw.op(fw.act, lambda: nc.scalar.activation(out=ngst[n_][:], in_=pa[p][:, 0:48], func=AF.Sigmoid),
                                          reads=[t_pa[p]], writes=[t_ngst[n_]])
                                    fw.dma(fw.sp, dst[r0:r0 + 128, :], ngst[n_][:], t_ngst[n_], reads=[t_ngst[n_]], writes=[dst_tok])
                        return h

                    W = w_in[l]
                    groups = []
                    for cb in range(16):
                        segs = [(W[:, o + cb * 128:o + cb * 128 + 128], 0, KC, j * 128, 128)
                                for j, o in enumerate((O_AH, O_AB, O_AC, O_AZ))]
                        groups.append((segs, conv_handler(cb)))
                    for hg in range(4):
                        segs = [(W[:, O_Q + hg * 512:O_Q + hg * 512 + 512], 0, KC, 0, 512)]
                        groups.append((segs, fm_handler([("qrope", qrT[hg * 4 + r], T["qrT"], qpT[hg * 4 + r], T["qpT"])
                                                         for r in range(4)])))
                    kvo = lambda br, kvi: O_KV + (br * 2 + kvi) * 512
                    for (br, dstT, nm) in ((1, kslcT, "kslcT"), (2, kwinT, "kwinT")):
                        groups.append(([(W[:, kvo(br, 0):kvo(br, 0) + 512], 0, KC, 0, 512)],
                                       fm_handler([("rope", dstT[g], T[nm]) for g in range(4)])))
                    for (kvi, dstT, nm) in ((0, kcmpT, "kcmpT"), (1, vcmpT, "vcmpT")):
                        groups.append(([(W[:, kvo(0, kvi):kvo(0, kvi) + 512], 0, KC, 0, 512)],
                                       fm_handler([("copy", dstT[g], T[nm]) for g in range(4)])))
                    for (br, dst, nm) in ((1, vslc, "vslc"), (2, vwin, "vwin")):
                        groups.append(([(W[:, kvo(br, 1):kvo(br, 1) + 512], 0, KC, 0, 512)], tm_handler(512, dst, T[nm], "v")))
                    groups.append(([(W[:, O_NG:O_NG + 48], 0, KC, 0, 48)], tm_handler(48, ngs, T["ngs"], "g")))
                    for k in range(4):
                        groups.append(([(W[:, O_NZ + k * 512:O_NZ + k * 512 + 512], 0, KC, 0, 512)],
                                       fm_handler([("silu", nzT[(k * 4 + r) * 128:(k * 4 + r + 1) * 128], T["nzT"]) for r in range(4)])))
                    groups.append(([(W[:, O_XQ:O_XQ + 512], 0, KC, 0, 512)],
                                   fm_handler([("copy", xqT[r], T["xqT"]) for r in range(4)])))
                    groups.append(([(W[:, O_XZ:O_XZ + 512], 0, KC, 0, 512)],
                                   fm_handler([("silu", xzT[r * 128:(r + 1) * 128], T["xzT"]) for r in range(4)])))
                    for k in range(24):
                        gi_, dc0 = divmod(k * 4, 32)
                        groups.append(([(W[:, O_MG + k * 512:O_MG + k * 512 + 512], 0, KC, 0, 512)],
                                       fm_handler([("sigmoid", mgT[gi_][(dc0 + r) * 128:(dc0 + r + 1) * 128], T["mgT"])
                                                   for r in range(4)])))
                    ws.run(groups)

            with fw.scope():
                kcA = fw.sbuf("kcA", [128, 4, 128], BF16)
                vcA = fw.sbuf("vcA", [128, 4, 161], BF16)
                t_kcA, t_vcA = fw.tok("kcA"), fw.tok("vcA", dma=True)
                fw.op(fw.dve, lambda: nc.vector.memset(kcA[:], 0.0), writes=[t_kcA])
                fw.op(fw.dve, lambda: nc.vector.memset(vcA[:], 1.0), writes=[t_vcA])
                for g in range(4):
                    fw.dma(fw.sp, vcA[:, g, 129:161], ovl_in, t_vcA, reads=[t_ext], writes=[t_vcA])
                cbias = fw.sbuf("cbias", [128, 16, 512], BF16)
                expand = fw.sbuf("expand", [128, 16, 128], BF16)
                selc = fw.sbuf("selc", [128, 16, 32], F32)
                t_c3 = fw.tok("c3", dma=True)
                fw.dma(fw.sp, cbias[:], cbias_in, t_c3, reads=[t_ext], writes=[t_c3])
                fw.dma(fw.sp, expand[:], expand_in, t_c3, reads=[t_ext], writes=[t_c3])
                fw.dma(fw.sp, selc[:], selc_in, t_c3, reads=[t_ext], writes=[t_c3])

                with fw.scope():
                    w1b = fw.sbuf("w1b", [128, 32, 128], BF16)
                    w2b = fw.sbuf("w2b", [128, 128], BF16)
                    posb = fw.sbuf("posb", [128, 32], BF16)
                    t_w1 = fw.tok("w1", dma=True)
                    kcs = fw.sbuf("kcs", [128, S], BF16)
                    t_kcs = fw.tok("kcs", dma=True)
                    pp = fw.psum("pp", [128, 512], F32)
                    t_pp = fw.tok("pp")
                    pq = fw.psum("pq", [128, 512], F32)
                    t_pq = fw.tok("pq")
                    pbias = fw.sbuf("pbias", [128, 1], F32)
                    t_pbias = fw.tok("pbias")
                    hid = fw.sbuf("hid", [128, 128], BF16)
                    t_hid = fw.tok("hid")
                    fw.op(fw.dve, lambda: nc.vector.memset(hid[:], 0.0), writes=[t_hid])
                    for c in range(2):
                        fw.dma(fw.pool, w1b[:], w1_in[l, c].rearrange("(l d) o -> d l o", d=128), t_w1, reads=[t_ext], writes=[t_w1])
                        fw.dma(fw.pool, w2b[:], w2_in[l, c], t_w1, reads=[t_ext], writes=[t_w1])
                        fw.dma(fw.pool, posb[:], posT_in[l, c], t_w1, reads=[t_ext], writes=[t_w1])
                        for ll in range(32):
                            fw.op(fw.pe, lambda: nc.tensor.matmul(pq[:, 0:1], lhsT=w1b[:, ll, :], rhs=posb[:, ll:ll + 1],
                                                                  start=(ll == 0), stop=(ll == 31)),
                                  reads=[t_w1], writes=[t_pq])
                        fw.op(fw.act, lambda: nc.scalar.copy(out=pbias[:], in_=pq[:, 0:1]), reads=[t_pq], writes=[t_pbias])
                        srcT = kcmpT if c == 0 else vcmpT
                        for g in range(4):
                            fw.dma(fw.sp, kcs[:], srcT[g], t_kcs, reads=[T["kcmpT"], T["vcmpT"]], writes=[t_kcs])
                            for ll in range(32):
                                fw.op(fw.pe, lambda: nc.tensor.matmul(pp[:, 0:127], lhsT=w1b[:, ll, :],
                                                                      rhs=kcs[:, ll:ll + 16 * 126 + 1:16],
                                                                      start=(ll == 0), stop=(ll == 31)),
                                      reads=[t_w1, t_kcs], writes=[t_pp])
                            fw.op(fw.act, lambda: nc.scalar.activation(out=hid[:, 0:127], in_=pp[:, 0:127], func=AF.Silu,
                                                                       bias=pbias[:, 0:1]),
                                  reads=[t_pp, t_pbias], writes=[t_hid])
                            if c == 0:
                                fw.op(fw.pe, lambda: nc.tensor.matmul(pq[:, 0:127], lhsT=w2b[:], rhs=hid[:, 0:127], start=True, stop=True),
                                      reads=[t_w1, t_hid], writes=[t_pq])
                                fw.op(fw.act, lambda: nc.scalar.copy(out=kcA[:, g, 0:127], in_=pq[:, 0:127]),
                                      reads=[t_pq], writes=[t_kcA])
                            else:
                                fw.op(fw.pe, lambda: nc.tensor.matmul(pq[:, 0:128], lhsT=hid[:, 0:128], rhs=w2b[:], start=True, stop=True),
                                      reads=[t_w1, t_hid], writes=[t_pq])
                                fw.op(fw.act, lambda: nc.scalar.copy(out=vcA[:, g, 0:128], in_=pq[:, 0:128]),
                                      reads=[t_pq], writes=[t_vcA])

                ks2 = [fw.sbuf("ks", [128, S], BF16) for _ in range(2)]
                kw2 = [fw.sbuf("kw", [128, S], BF16) for _ in range(2)]
                vsA2 = [fw.sbuf("vsA", [128, 16, 129], BF16) for _ in range(2)]
                vwA2 = [fw.sbuf("vwA", [128, 16, 129], BF16) for _ in range(2)]
                t_kv2 = [fw.tok("kv", dma=True, multi=True) for _ in range(2)]
                for p_ in range(2):
                    fw.op(fw.dve, lambda: nc.vector.memset(vsA2[p_][:], 1.0), writes=[t_kv2[p_]])
                    fw.op(fw.dve, lambda: nc.vector.memset(vwA2[p_][:], 1.0), writes=[t_kv2[p_]])
                qp = [fw.sbuf("qp", [128, 4, 128], BF16) for _ in range(2)]
                qr = [fw.sbuf("qr", [128, 4, 128], BF16) for _ in range(2)]
                gs = [fw.sbuf("gs", [128, 48], F32) for _ in range(2)]
                nz = [fw.sbuf("nz", [128, 4, 128], BF16) for _ in range(2)]
                t_q = [fw.tok("q", dma=True, multi=True) for _ in range(2)]
                NE = 24
                Eb = fw.sbuf("Eb", [128, NE, 512], BF16)
                t_E = [fw.tok("E") for _ in range(NE)]
                pss = [fw.psum("pss", [128, 512], F32) for _ in range(3)]
                t_pss = [fw.tok("pss") for _ in range(3)]
                po = [fw.psum("po", [128, 512], F32) for _ in range(4)]
                t_po = [fw.tok("po") for _ in range(4)]
                pst = fw.psum("pst", [128, 1024], BF16)
                t_pst = fw.tok("pst")
                den = fw.sbuf("den", [128, 4], F32)
                rden = fw.sbuf("rden", [128, 4], F32)
                cc = fw.sbuf("cc", [128, 4], F32)
                imp = fw.sbuf("imp", [128, 32], F32)
                imp3 = fw.sbuf("imp3", [128, 32], F32)
                m1 = fw.sbuf("m1", [128, 8], F32)
                m2 = fw.sbuf("m2", [128, 8], F32)
                selb = fw.sbuf("selb", [128, 128], BF16)
                selbT = fw.sbuf("selbT", [128, 512], BF16)
                oacc = fw.sbuf("oacc", [128, 4, 128], F32)
                t_oaccr = [fw.tok("oaccr") for _ in range(4)]
                t_obfr = [fw.tok("obfr") for _ in range(4)]
                obf = fw.sbuf("obf", [128, 4, 128], BF16)
                yst = [fw.sbuf("yst", [128, 4, 128], BF16) for _ in range(2)]
                t_yst = [fw.tok("yst", dma=True) for _ in range(2)]
                t_den, t_rden, t_cc, t_imp, t_imp3, t_m1, t_m2 = (fw.tok() for _ in range(7))
                t_selb, t_selbT, t_oacc, t_obf = (fw.tok() for _ in range(4))
                fw.op(fw.dve, lambda: nc.vector.memset(selb[:], 0.0), writes=[t_selb])
                cn = {"e": 0, "s": 0, "po": 0, "it": 0}

                def po_ap(pair, r, c0, c1, w):
                    bank = po[pair * 2 + r // 2]
                    off = (r % 2) * w
                    return bank[:, off + c0:off + c1], t_po[pair * 2 + r // 2]

                def score_tile(lhsT, lhs_toks, rhs, rhs_toks, biases):
                    sb_ = cn["s"] % 3
                    cn["s"] += 1
                    e_ = cn["e"] % NE
                    cn["e"] += 1
                    fw.op(fw.pe, lambda: nc.tensor.matmul(pss[sb_][:], lhsT=lhsT, rhs=rhs, start=True, stop=(len(biases) == 0)),
                          reads=lhs_toks + rhs_toks, writes=[t_pss[sb_]])
                    nb = len(biases)
                    for bi, (bl, br_, btoks) in enumerate(biases):
                        fw.op(fw.pe, lambda: nc.tensor.matmul(pss[sb_][:], lhsT=bl, rhs=br_, start=False, stop=(bi == nb - 1)),
                              reads=btoks, writes=[t_pss[sb_]])
                    fw.op(fw.act, lambda: nc.scalar.activation(out=Eb[:, e_, :], in_=pss[sb_][:], func=AF.Exp, scale=SCALE),
                          reads=[t_pss[sb_]], writes=[t_E[e_]])
                    return e_

                def pv_and_combine(eslots, vA, v_toks, vsel, width, br, g, b, first, last):
                    pair = cn["po"] % 2
                    cn["po"] += 1
                    n = len(eslots)
                    for r in range(4):
                        oap, otok = po_ap(pair, r, 0, width, width)
                        for j, (e_, kt) in enumerate(eslots):
                            fw.op(fw.pe, lambda: nc.tensor.matmul(oap, lhsT=Eb[:, e_, r * 128:(r + 1) * 128], rhs=vsel(kt),
                                                                  start=(j == 0), stop=(j == n - 1)),
                                  reads=[t_E[e_]] + v_toks, writes=[otok])
                    for bk in range(2):
                        bank, tokb = po[pair * 2 + bk], t_po[pair * 2 + bk]
                        fw.op(fw.dve, lambda: nc.vector.tensor_scalar_max(out=den[:, 2 * bk:2 * bk + 2],
                                                                          in0=bank[:, 128:128 + width + 1:width], scalar1=1e-30),
                              reads=[tokb], writes=[t_den])
                    fw.op(fw.dve, lambda: nc.vector.reciprocal(out=rden[:], in_=den[:]), reads=[t_den], writes=[t_rden])
                    if br == 0:
                        for r in range(4):
                            iap, otok = po_ap(pair, r, 129, 161, width)
                            if r == 0:
                                fw.op(fw.dve, lambda: nc.vector.tensor_scalar_mul(out=imp[:], in0=iap, scalar1=rden[:, 0:1]),
                                      reads=[otok, t_rden], writes=[t_imp])
                            else:
                                fw.op(fw.dve, lambda: nc.vector.scalar_tensor_tensor(out=imp[:], in0=iap, scalar=rden[:, r:r + 1],
                                                                                     in1=imp[:], op0=ALU.mult, op1=ALU.add),
                                      reads=[otok, t_rden, t_imp], writes=[t_imp])
                    gview = gs[b][:, 12 * g:12 * g + 12].rearrange("p (r c) -> p r c", c=3)[:, :, br]
                    fw.op(fw.dve, lambda: nc.vector.tensor_tensor(out=cc[:], in0=rden[:], in1=gview, op=ALU.mult),
                          reads=[t_rden, t_q[b]], writes=[t_cc])
                    for r in range(4):
                        vap, otok = po_ap(pair, r, 0, 128, width)
                        if first:
                            fw.op(fw.dve, lambda: nc.vector.tensor_scalar_mul(out=oacc[:, r, :], in0=vap, scalar1=cc[:, r:r + 1]),
                                  reads=[otok, t_cc], writes=[t_oaccr[r]])
                        else:
                            dst_ap, dst_t = (obf[:, r, :], t_obfr[r]) if last else (oacc[:, r, :], t_oaccr[r])
                            fw.op(fw.dve, lambda: nc.vector.scalar_tensor_tensor(out=dst_ap, in0=vap, scalar=cc[:, r:r + 1],
                                                                                 in1=oacc[:, r, :], op0=ALU.mult, op1=ALU.add),
                                  reads=[otok, t_cc, t_oaccr[r]], writes=[dst_t])

                def load_kv(g):
                    p_ = g % 2
                    fw.dma(fw.sp, ks2[p_][:], kslcT[g], t_kv2[p_], reads=[T["kslcT"]], writes=[t_kv2[p_]])
                    fw.dma(fw.sp, kw2[p_][:], kwinT[g], t_kv2[p_], reads=[T["kwinT"]], writes=[t_kv2[p_]])
                    fw.dma(fw.sp, vsA2[p_][:, :, 0:128], vslc[:, g * 128:(g + 1) * 128].rearrange("(k p) d -> p k d", p=128), t_kv2[p_],
                           reads=[T["vslc"]], writes=[t_kv2[p_]])
                    fw.dma(fw.sp, vwA2[p_][:, :, 0:128], vwin[:, g * 128:(g + 1) * 128].rearrange("(k p) d -> p k d", p=128), t_kv2[p_],
                           reads=[T["vwin"]], writes=[t_kv2[p_]])

                def load_q(g, i, b):
                    q0 = i * 128
                    fw.dma(fw.sp, qp[b][:], qpT[4 * g:4 * g + 4, :, q0:q0 + 128].rearrange("r p t -> p r t"), t_q[b],
                           reads=[T["qpT"]], writes=[t_q[b]])
                    fw.dma(fw.sp, qr[b][:], qrT[4 * g:4 * g + 4, :, q0:q0 + 128].rearrange("r p t -> p r t"), t_q[b],
                           reads=[T["qrT"]], writes=[t_q[b]])
                    fw.dma(fw.sp, gs[b][:], ngs[q0:q0 + 128, :], t_q[b], reads=[T["ngs"]], writes=[t_q[b]])
                    fw.dma(fw.sp, nz[b][:], nzT[g * 512:(g + 1) * 512, q0:q0 + 128].rearrange("(r p) t -> p r t", p=128), t_q[b],
                           reads=[T["nzT"]], writes=[t_q[b]])

                iters = [(g_, i_) for g_ in range(4) for i_ in range(16)]
                NIT = len(iters)

                def stage_A(n_it):
                    if True:
                        g, i = iters[n_it]
                        b = n_it % 2
                        q0 = i * 128
                        ks, kw, vsA, vwA, t_kv = ks2[g % 2], kw2[g % 2], vsA2[g % 2], vwA2[g % 2], t_kv2[g % 2]
                        qpf = qp[b][:].rearrange("p r t -> p (r t)")
                        qrf = qr[b][:].rearrange("p r t -> p (r t)")
                        e_ = score_tile(kcA[:, g, :], [t_kcA], qpf, [t_q[b]], [(ident[:], cbias[:, i, :], [t_const, t_c3])])
                        pv_and_combine([(e_, 0)], vcA, [t_vcA], lambda kt: vcA[:, g, :], 161, 0, g, b, True, False)
                        sel_bias = []
                        if i >= 8:
                            fw.op(fw.dve, lambda: nc.vector.tensor_tensor(out=imp[:], in0=imp[:], in1=selc[:, i, :], op=ALU.add),
                                  reads=[t_imp, t_c3], writes=[t_imp])
                            fw.op(fw.dve, lambda: nc.vector.max(out=m1[:], in_=imp[:]), reads=[t_imp], writes=[t_m1])
                            fw.op(fw.dve, lambda: nc.vector.match_replace(out=imp3[:], in_to_replace=m1[:], in_values=imp[:],
                                                                          imm_value=-1e9),
                                  reads=[t_imp, t_m1], writes=[t_imp3])
                            fw.op(fw.dve, lambda: nc.vector.max(out=m2[:], in_=imp3[:]), reads=[t_imp3], writes=[t_m2])
                            fw.op(fw.dve, lambda: nc.vector.tensor_scalar(out=selb[:, 0:32], in0=imp[:], scalar1=m2[:, 7:8], scalar2=-1.0,
                                                                          op0=ALU.is_ge, op1=ALU.add),
                                  reads=[t_imp, t_m2], writes=[t_selb])
                        es = []
                        for kt in range(max(0, i - 4), i + 1):
                            biases = []
                            if kt == i:
                                biases.append((ident[:], caus[:], [t_const]))
                            if kt == i - 4:
                                biases.append((ident[:], wedge[:], [t_const]))
                            es.append((score_tile(kw[:, kt * 128:(kt + 1) * 128], [t_kv], qrf, [t_q[b]], biases), kt))
                        pv_and_combine(es, vwA, [t_kv], lambda kt: vwA[:, kt, :], 129, 2, g, b, False, False)

                def stage_B(n_it):
                    if True:
                        g, i = iters[n_it]
                        b = n_it % 2
                        q0 = i * 128
                        ks, kw, vsA, vwA, t_kv = ks2[g % 2], kw2[g % 2], vsA2[g % 2], vwA2[g % 2], t_kv2[g % 2]
                        qpf = qp[b][:].rearrange("p r t -> p (r t)")
                        qrf = qr[b][:].rearrange("p r t -> p (r t)")
                        if i >= 8:
                            for r in range(4):
                                fw.op(fw.pe, lambda: nc.tensor.transpose(pst[:, r * 128:(r + 1) * 128], selb[:], ident[:]),
                                      reads=[t_selb, t_const], writes=[t_pst])
                            fw.op(fw.act, lambda: nc.scalar.copy(out=selbT[:], in_=pst[:, 0:512]), reads=[t_pst], writes=[t_selbT])
                        es = []
                        for kt in range(i + 1):
                            biases = []
                            if i >= 8:
                                biases.append((expand[:, kt, :], selbT[:], [t_c3, t_selbT]))
                            if kt == i:
                                biases.append((ident[:], caus[:], [t_const]))
                            es.append((score_tile(ks[:, kt * 128:(kt + 1) * 128], [t_kv], qrf, [t_q[b]], biases), kt))
                        pv_and_combine(es, vsA, [t_kv], lambda kt: vsA[:, kt, :], 129, 1, g, b, False, True)

                def stage_C(n_it):
                    if True:
                        g, i = iters[n_it]
                        b = n_it % 2
                        q0 = i * 128
                        ks, kw, vsA, vwA, t_kv = ks2[g % 2], kw2[g % 2], vsA2[g % 2], vwA2[g % 2], t_kv2[g % 2]
                        qpf = qp[b][:].rearrange("p r t -> p (r t)")
                        qrf = qr[b][:].rearrange("p r t -> p (r t)")
                        for r in range(4):
                            fw.op(fw.pe, lambda: nc.tensor.transpose(pst[:, r * 128:(r + 1) * 128], obf[:, r, :], ident[:]),
                                  reads=[t_obfr[r], t_const], writes=[t_pst])
                        fw.op(fw.dve, lambda: nc.vector.tensor_tensor(out=yst[b][:].rearrange("p r t -> p (r t)"), in0=pst[:, 0:512],
                                                                      in1=nz[b][:].rearrange("p r t -> p (r t)"), op=ALU.mult),
                              reads=[t_pst, t_q[b]], writes=[t_yst[b]])
                        fw.dma(fw.sp, ybT[g * 512:(g + 1) * 512, q0:q0 + 128].rearrange("(r p) t -> p r t", p=128), yst[b][:], t_yst[b],
                               reads=[t_yst[b]], writes=[T["ybT"]])

                load_kv(0)
                load_q(0, 0, 0)
                stage_A(0)
                if NIT > 1:
                    load_q(iters[1][0], iters[1][1], 1)
                for n_it in range(NIT):
                    g, i = iters[n_it]
                    if i == 0 and g + 1 < 4:
                        load_kv(g + 1)
                    stage_B(n_it)
                    if n_it + 1 < NIT:
                        stage_A(n_it + 1)
                    stage_C(n_it)
                    if n_it + 2 < NIT:
                        load_q(iters[n_it + 2][0], iters[n_it + 2][1], n_it % 2)

            with fw.scope():
                xq = [fw.sbuf("xq", [128, 512], BF16) for _ in range(2)]
                xz = [fw.sbuf("xz", [128, 512], BF16) for _ in range(2)]
                t_xq = [fw.tok("xq", dma=True, multi=True) for _ in range(2)]
                Em = [fw.sbuf("Em", [128, 512], BF16) for _ in range(4)]
                t_Em = [fw.tok("Em") for _ in range(4)]
                pss = [fw.psum("pss", [128, 512], F32) for _ in range(2)]
                t_pss = [fw.tok("pss") for _ in range(2)]
                po = [fw.psum("po", [128, 512], F32) for _ in range(4)]
                t_po = [fw.tok("po") for _ in range(4)]
                pst = fw.psum("pst", [128, 1024], BF16)
                t_pst = fw.tok("pst")
                den = fw.sbuf("den", [128, 4], F32)
                rden = fw.sbuf("rden", [128, 4], F32)
                obf = fw.sbuf("obf", [128, 4, 128], BF16)
                yst = [fw.sbuf("yst", [128, 512], BF16) for _ in range(2)]
                t_yst = [fw.tok("yst", dma=True) for _ in range(2)]
                t_den, t_rden, t_obf = fw.tok(), fw.tok(), fw.tok()
                def load_x(tq_, hx_, b_):
                    fw.dma(fw.sp, xq[b_][:], xqT[hx_][:, tq_ * 512:tq_ * 512 + 512], t_xq[b_], reads=[T["xqT"]], writes=[t_xq[b_]])
                    fw.dma(fw.sp, xz[b_][:], xzT[hx_ * 128:(hx_ + 1) * 128, tq_ * 512:tq_ * 512 + 512], t_xq[b_],
                           reads=[T["xzT"]], writes=[t_xq[b_]])

                iters4 = [(tq_, hx_) for tq_ in range(4) for hx_ in range(4)]
                load_x(0, 0, 0)
                for it, (tq, hx) in enumerate(iters4):
                    if True:
                        t0 = tq * 512
                        b = it % 2
                        if it + 1 < len(iters4):
                            load_x(iters4[it + 1][0], iters4[it + 1][1], (it + 1) % 2)
                        for mt in range(2):
                            fw.op(fw.pe, lambda: nc.tensor.matmul(pss[mt][:], lhsT=mK[:, hx, mt * 128:(mt + 1) * 128], rhs=xq[b][:],
                                                                  start=True, stop=True),
                                  reads=[t_mK, t_xq[b]], writes=[t_pss[mt]])
                            e_ = b * 2 + mt
                            fw.op(fw.act, lambda: nc.scalar.activation(out=Em[e_][:], in_=pss[mt][:], func=AF.Exp, scale=SCALE),
                                  reads=[t_pss[mt]], writes=[t_Em[e_]])
                        pair = b
                        for qs in range(4):
                            bank = pair * 2 + qs // 2
                            off = (qs % 2) * 129
                            for mt in range(2):
                                e_ = b * 2 + mt
                                fw.op(fw.pe, lambda: nc.tensor.matmul(po[bank][:, off:off + 129], lhsT=Em[e_][:, qs * 128:(qs + 1) * 128],
                                                                      rhs=mV[:, mt, hx, :], start=(mt == 0), stop=(mt == 1)),
                                      reads=[t_Em[e_], t_mV], writes=[t_po[bank]])
                        for qs in range(4):
                            bank = pair * 2 + qs // 2
                            off = (qs % 2) * 129
                            fw.op(fw.dve, lambda: nc.vector.tensor_scalar_max(out=den[:, qs:qs + 1], in0=po[bank][:, off + 128:off + 129],
                                                                              scalar1=1e-30), reads=[t_po[bank]], writes=[t_den])
                        fw.op(fw.dve, lambda: nc.vector.reciprocal(out=rden[:], in_=den[:]), reads=[t_den], writes=[t_rden])
                        for qs in range(4):
                            bank = pair * 2 + qs // 2
                            off = (qs % 2) * 129
                            fw.op(fw.dve, lambda: nc.vector.tensor_scalar_mul(out=obf[:, qs, :], in0=po[bank][:, off:off + 128],
                                                                              scalar1=rden[:, qs:qs + 1]),
                                  reads=[t_po[bank], t_rden], writes=[t_obf])
                        for qs in range(4):
                            fw.op(fw.pe, lambda: nc.tensor.transpose(pst[:, qs * 128:(qs + 1) * 128], obf[:, qs, :], ident[:]),
                                  reads=[t_obf, t_const], writes=[t_pst])
                        fw.op(fw.dve, lambda: nc.vector.tensor_tensor(out=yst[b][:], in0=pst[:, 0:512], in1=xz[b][:], op=ALU.mult),
                              reads=[t_pst, t_xq[b]], writes=[t_yst[b]])
                        fw.dma(fw.sp, yxT[hx * 128:(hx + 1) * 128, t0:t0 + 512], yst[b][:], t_yst[b], reads=[t_yst[b]], writes=[T["yxT"]])

            for tq in range(S // TT):
                t0 = tq * TT
                with fw.scope():
                    ya = fw.sbuf("ya", [128, 16, TT], BF16)
                    yb = fw.sbuf("yb", [128, 16, TT], BF16)
                    yx = fw.sbuf("yx", [128, 4, TT], BF16)
                    t_y = fw.tok("y", dma=True, multi=True)
                    fw.dma(fw.sp, ya[:], yaT[:, t0:t0 + TT].rearrange("(k p) t -> p k t", p=128), t_y, reads=[T["yaT"]], writes=[t_y])
                    fw.dma(fw.sp, yb[:], ybT[:, t0:t0 + TT].rearrange("(k p) t -> p k t", p=128), t_y, reads=[T["ybT"]], writes=[t_y])
                    fw.dma(fw.sp, yx[:], yxT[:, t0:t0 + TT].rearrange("(k p) t -> p k t", p=128), t_y, reads=[T["yxT"]], writes=[t_y])
                    ws = WStream()
                    pa = [fw.psum("pa", [128, 512], F32) for _ in range(6)]
                    t_pa = [fw.tok("pa") for _ in range(6)]
                    NG = 5
                    gt = [fw.sbuf("gt", [128, 3, TT], BF16) for _ in range(NG)]
                    t_gt = [fw.tok("gt", dma=True) for _ in range(NG)]
                    ut = fw.sbuf("ut", [128, 4, TT], F32)
                    t_ut = [fw.tok("ut") for _ in range(4)]
                    u2 = fw.sbuf("u2", [128, 512], F32)
                    t_u2 = fw.tok("u2")
                    ust = [fw.sbuf("ust", [128, TT], BF16) for _ in range(2)]
                    t_ust = [fw.tok("ust", dma=True) for _ in range(2)]
                    cn5 = {"pa": 0, "gt": 0, "us": 0}
                    gslot = {}

                    def npa():
                        p = cn5["pa"] % 6
                        cn5["pa"] += 1
                        return p

                    def hA(dg):
                        def h(wtb, t_wtb):
                            for m in range(4):
                                dc = dg * 4 + m
                                gsl = cn5["gt"] % NG
                                cn5["gt"] += 1
                                gslot[dc] = gsl
                                fw.dma(fw.sp, gt[gsl][:], mgT[:, dc * 128:(dc + 1) * 128, t0:t0 + TT].rearrange("i p t -> p i t"),
                                       t_gt[gsl], reads=[T["mgT"]], writes=[t_gt[gsl]])
                                for th in range(2):
                                    sl = slice(th * 512, (th + 1) * 512)
                                    pA = npa()
                                    for kc in range(16):
                                        fw.op(fw.pe, lambda: nc.tensor.matmul(pa[pA][:], lhsT=wtb[:, kc, m * 128:(m + 1) * 128], rhs=ya[:, kc, sl],
                                                                              start=(kc == 0), stop=(kc == 15)),
                                              reads=[t_wtb, t_y], writes=[t_pa[pA]])
                                    pX = npa()
                                    for kc in range(4):
                                        fw.op(fw.pe, lambda: nc.tensor.matmul(pa[pX][:], lhsT=wtb[:, 16 + kc, m * 128:(m + 1) * 128], rhs=yx[:, kc, sl],
                                                                              start=(kc == 0), stop=(kc == 3)),
                                              reads=[t_wtb, t_y], writes=[t_pa[pX]])
                                    fw.op(fw.dve, lambda: nc.vector.tensor_tensor(out=ut[:, m, sl], in0=pa[pA][:], in1=gt[gsl][:, 0, sl], op=ALU.mult),
                                          reads=[t_pa[pA], t_gt[gsl]], writes=[t_ut[m]])
                                    fw.op(fw.dve, lambda: nc.vector.tensor_tensor(out=u2[:], in0=pa[pX][:], in1=gt[gsl][:, 2, sl], op=ALU.mult),
                                          reads=[t_pa[pX], t_gt[gsl]], writes=[t_u2])
                                    fw.op(fw.dve, lambda: nc.vector.tensor_tensor(out=ut[:, m, sl], in0=ut[:, m, sl], in1=u2[:], op=ALU.add),
                                          reads=[t_ut[m], t_u2], writes=[t_ut[m]])
                        return h

                    def hB(dg):
                        def h(wtb, t_wtb):
                            for m in range(4):
                                dc = dg * 4 + m
                                gsl = gslot[dc]
                                us = cn5["us"] % 2
                                cn5["us"] += 1
                                for th in range(2):
                                    sl = slice(th * 512, (th + 1) * 512)
                                    pB = npa()
                                    for kc in range(16):
                                        fw.op(fw.pe, lambda: nc.tensor.matmul(pa[pB][:], lhsT=wtb[:, kc, m * 128:(m + 1) * 128], rhs=yb[:, kc, sl],
                                                                              start=(kc == 0), stop=(kc == 15)),
                                              reads=[t_wtb, t_y], writes=[t_pa[pB]])
                                    fw.op(fw.dve, lambda: nc.vector.tensor_tensor(out=u2[:], in0=pa[pB][:], in1=gt[gsl][:, 1, sl], op=ALU.mult),
                                          reads=[t_pa[pB], t_gt[gsl]], writes=[t_u2])
                                    fw.op(fw.dve, lambda: nc.vector.tensor_tensor(out=ust[us][:, sl], in0=ut[:, m, sl], in1=u2[:], op=ALU.add),
                                          reads=[t_ut[m], t_u2], writes=[t_ust[us]])
                                fw.dma(fw.sp, uT_s[dc * 128:(dc + 1) * 128, t0:t0 + TT], ust[us][:], t_ust[us],
                                       reads=[t_ust[us]], writes=[T["uT_s"]])
                        return h

                    groups = []
                    for dg in range(8):
                        c0 = dg * 512
                        groups.append(([(wua_in[l][:, c0:c0 + 512], 0, 16, 0, 512), (wux_in[l][:, c0:c0 + 512], 16, 4, 0, 512)], hA(dg)))
                        groups.append(([(wub_in[l][:, c0:c0 + 512], 0, 16, 0, 512)], hB(dg)))
                    ws.run(groups)

            for tq in range(S // TT):
                t0 = tq * TT
                with fw.scope():
                    uT = fw.sbuf("uT", [128, KC, TT], BF16)
                    t_uT = fw.tok("uT", dma=True, multi=True)
                    for k4 in range(0, KC, 8):
                        fw.dma(fw.sp, uT[:, k4:k4 + 8, :], uT_s[k4 * 128:(k4 + 8) * 128, t0:t0 + TT].rearrange("(k p) t -> p k t", p=128),
                               t_uT, reads=[T["uT_s"]], writes=[t_uT])
                    ws = WStream()
                    pa = [fw.psum("pa", [128, 512], F32) for _ in range(4)]
                    t_pa = [fw.tok("pa") for _ in range(4)]
                    xres = [fw.sbuf("xres", [128, TT], F32) for _ in range(2)]
                    t_xres = [fw.tok("xres", dma=True) for _ in range(2)]
                    xo = [fw.sbuf("xo", [128, TT], F32) for _ in range(2)]
                    t_xo = [fw.tok("xo", dma=True) for _ in range(2)]
                    cn6 = {"pa": 0, "x": 0}

                    def hO(dg):
                        def h(wtb, t_wtb):
                            for m in range(4):
                                dc = dg * 4 + m
                                xb_ = cn6["x"] % 2
                                cn6["x"] += 1
                                fw.dma(fw.sp, xres[xb_][:], xT_s[dc, :, t0:t0 + TT], t_xres[xb_], reads=[T["xT_s"]], writes=[t_xres[xb_]])
                                for th in range(2):
                                    sl = slice(th * 512, (th + 1) * 512)
                                    pD = cn6["pa"] % 4
                                    cn6["pa"] += 1
                                    for kc in range(KC):
                                        fw.op(fw.pe, lambda: nc.tensor.matmul(pa[pD][:], lhsT=wtb[:, kc, m * 128:(m + 1) * 128], rhs=uT[:, kc, sl],
                                                                              start=(kc == 0), stop=(kc == KC - 1)),
                                              reads=[t_wtb, t_uT], writes=[t_pa[pD]])
                                    fw.op(fw.dve, lambda: nc.vector.tensor_tensor(out=xo[xb_][:, sl], in0=pa[pD][:], in1=xres[xb_][:, sl], op=ALU.add),
                                          reads=[t_pa[pD], t_xres[xb_]], writes=[t_xo[xb_]])
                                fw.dma(fw.sp, xT_s[dc, :, t0:t0 + TT], xo[xb_][:], t_xo[xb_], reads=[t_xo[xb_]], writes=[T["xT_s"]])
                        return h

                    ws.run([([(wo_in[l][:, dg * 512:dg * 512 + 512], 0, KC, 0, 512)], hO(dg)) for dg in range(8)])

    norm_phase(xT_s, T["xT_s"], fgT_in, S, dst_dram=out, dst_tok=T["out"])
    fw.barrier()
    n_instr = fw.n_instr
    fw.close()
    return nc, n_instr


def _consts():
    bf = ml_dtypes.bfloat16
    half = DH // 2
    inv_freq = (np.float32(10000.0) ** (-np.arange(half, dtype=np.float32) / np.float32(half))).astype(np.float32)
    ang = np.arange(S, dtype=np.float32)[None, :] * inv_freq[:, None]
    cos = np.cos(ang).astype(np.float32)
    sin = np.sin(ang).astype(np.float32)
    c = {}
    c["c_cos"] = np.ascontiguousarray(np.concatenate([cos, cos], 0))
    c["c_sin"] = np.ascontiguousarray(np.concatenate([-sin, sin], 0))
    c["c_ident"] = np.eye(128, dtype=np.float32).astype(bf)
    k = np.arange(128)[:, None]
    q = np.arange(128)[None, :]
    c["c_caus"] = np.ascontiguousarray(np.tile(np.where(k <= q, 0.0, NEGB).astype(np.float32).astype(bf), (1, 4)))
    c["c_wedge"] = np.ascontiguousarray(np.tile(np.where(k > q, 0.0, NEGB).astype(np.float32).astype(bf), (1, 4)))
    n = np.arange(128)[:, None, None]
    i = np.arange(16)[None, :, None]
    qq = np.arange(128)[None, None, :]
    t = 128 * i + qq
    c["c_cbias"] = np.ascontiguousarray(np.tile(np.where((16 * n + 31 <= t) & (n < 127), 0.0, NEGB).astype(np.float32).astype(bf), (1, 1, 4)))
    j = np.arange(128)[:, None, None]
    kt = np.arange(16)[None, :, None]
    key = np.arange(128)[None, None, :]
    c["c_expand"] = np.where((j == 2 * kt + key // 64) & (j < 32), -NEGB, 0.0).astype(np.float32).astype(bf)
    qv = np.arange(128)[:, None, None]
    iv = np.arange(16)[None, :, None]
    jv = np.arange(32)[None, None, :]
    tt = 128 * iv + qv
    cur = tt // 64
    future = jv > cur
    forced = (jv == 0) | (jv == cur) | (jv == cur - 1)
    c["c_selc"] = np.where(future, -10.0, np.where(forced, 10.0, 0.0)).astype(np.float32)
    nn = np.arange(128)[:, None]
    jj = np.arange(32)[None, :]
    ovl = ((16 * nn < 64 * jj + 64) & (16 * nn + 32 > 64 * jj) & (nn < 127))
    c["c_ovl"] = ovl.astype(np.float32).astype(bf)
    return c


def _prep_shared(inp, depth):
    f = lambda a: np.ascontiguousarray(np.asarray(a, dtype=np.float32))
    sh = {}
    sh["gT"] = f(np.asarray(inp["norm_g"])[:depth].reshape(depth, KC, 128).transpose(0, 2, 1))
    sh["mgnT"] = f(np.asarray(inp["mem_norm_g"])[:depth].reshape(depth, KC, 128).transpose(0, 2, 1))
    sh["fgT"] = f(np.asarray(inp["final_g"]).reshape(KC, 128).T)
    sh["w_in"] = f(np.asarray(inp["w_in"])[:depth])
    sh["cw"] = f(np.asarray(inp["conv_w"])[:depth].reshape(depth, 3, 16, 128).transpose(0, 3, 2, 1))
    sh["cb"] = f(np.asarray(inp["conv_b"])[:depth].reshape(depth, 16, 128).transpose(0, 2, 1))
    sh["posT"] = f(np.asarray(inp["cmp_pos"])[:depth].transpose(0, 1, 3, 2))
    sh["w1"] = f(np.asarray(inp["cmp_w1"])[:depth])
    sh["w2"] = f(np.asarray(inp["cmp_w2"])[:depth])
    sh["wmk"] = f(np.asarray(inp["w_mem_kv"])[:depth])
    sh["wua"] = f(np.asarray(inp["w_up_a"])[:depth])
    sh["wub"] = f(np.asarray(inp["w_up_b"])[:depth])
    sh["wux"] = f(np.asarray(inp["w_up_x"])[:depth])
    sh["wo"] = f(np.asarray(inp["w_out"])[:depth])
    sh.update(_consts())
    return sh


_CACHE = {}


def run(inp, depth=DEPTH, dbg=False, ncores=8, trace=False):
    key = (depth, dbg)
    if key not in _CACHE:
        _CACHE[key] = build(depth, dbg)
    nc, _ = _CACHE[key]
    sh = _prep_shared(inp, depth)
    x = np.asarray(inp["x"], dtype=np.float32)
    mem = np.asarray(inp["mem"], dtype=np.float32)
    B = x.shape[0]
    in_maps = []
    for c in range(ncores):
        b = c % B
        m = dict(sh)
        m["xT"] = np.ascontiguousarray(x[b].T).reshape(KC, 128, S)
        m["memT"] = np.ascontiguousarray(mem[b].T).reshape(KC, 128, MEM)
        in_maps.append(m)
    res = run_bass_kernel_spmd(nc, in_maps, core_ids=list(range(ncores)), trace=trace)
    return res


def kernel(**inputs):
    res = run(inputs)
    B = np.asarray(inputs["x"]).shape[0]
    outs = []
    for b in range(B):
        oT = np.asarray(res.results[b]["outT"]).reshape(D, S)
        outs.append(np.ascontiguousarray(oT.T))
    return np.stack(outs, 0).astype(np.float32)
```

```python
import numpy as np
import ml_dtypes
import concourse.bass as bass
import concourse.mybir as mybir
from concourse.bass_utils import run_bass_kernel_spmd

F32 = mybir.dt.float32
BF16 = mybir.dt.bfloat16
AF = mybir.ActivationFunctionType
ALU = mybir.AluOpType

D = 4096
KC = 32
S = 2048
DEPTH = 4
MEM = 256
DH = 128
INW = 28720
TT = 1024
EPS = 1e-6
SCALE = DH ** -0.5
NEGB = -30000.0
O_AH, O_AB, O_AC, O_AZ, O_Q, O_KV, O_NG, O_NZ, O_XQ, O_XZ, O_MG = (
    0, 2048, 4096, 6144, 8192, 10240, 13312, 13360, 15408, 15920, 16432)


class Tok:
    __slots__ = ("w", "r", "multi", "dsem", "dcount", "name")

    def __init__(self, name="", multi=False):
        self.w = {}
        self.r = {}
        self.multi = multi
        self.dsem = None
        self.dcount = 0
        self.name = name


class Eng:
    def __init__(self, name, h, sem):
        self.name = name
        self.h = h
        self.sem = sem
        self.count = 0
        self.known = {}


class Scope:
    def __init__(self, fw):
        self.fw = fw

    def __enter__(self):
        self.mark = len(self.fw._ctx)
        self.fw._scope_toks.append([])
        return self

    def __exit__(self, *a):
        self.fw.barrier()
        while len(self.fw._ctx) > self.mark:
            self.fw._ctx.pop().__exit__(None, None, None)
        for t in self.fw._scope_toks.pop():
            self.fw._sem_pool.append((t.dsem, t.dcount))
        return False


class FW:
    def __init__(self, nc):
        self.nc = nc
        self._ctx = []
        self._sem_ctx = []
        self._sem_pool = []
        self._scope_toks = []
        self.all_handles = {}
        self.pe = self._mk("pe", nc.tensor)
        self.act = self._mk("act", nc.scalar)
        self.dve = self._mk("dve", nc.vector)
        self.pool = self._mk("pool", nc.gpsimd)
        self.sp = self._mk("sp", nc.sync)
        self.engs = [self.pe, self.act, self.dve, self.pool, self.sp]
        self.n_instr = 0
        self._uid = 0

    def _enter(self, cm):
        v = cm.__enter__()
        self._ctx.append(cm)
        return v

    def close(self):
        while self._ctx:
            self._ctx.pop().__exit__(None, None, None)
        while self._sem_ctx:
            self._sem_ctx.pop().__exit__(None, None, None)

    def scope(self):
        return Scope(self)

    def _mk(self, name, h):
        cm = self.nc.semaphore("p_" + name)
        sem = cm.__enter__()
        self._sem_ctx.append(cm)
        return Eng(name, h, sem)

    def _nm(self, name):
        self._uid += 1
        return f"{name}_{self._uid}"

    def sbuf(self, name, shape, dt):
        return self._enter(self.nc.sbuf_tensor(self._nm(name), shape, dt))

    def psum(self, name, shape, dt):
        return self._enter(self.nc.psum_tensor(self._nm(name), shape, dt))

    def tok(self, name="", dma=False, multi=False):
        t = Tok(name, multi)
        if dma:
            if self._sem_pool:
                t.dsem, t.dcount = self._sem_pool.pop()
            else:
                cm = self.nc.semaphore(self._nm("d_" + name))
                t.dsem = cm.__enter__()
                self._sem_ctx.append(cm)
                t.dcount = 0
            if self._scope_toks:
                self._scope_toks[-1].append(t)
        return t

    def _needs(self, reads, writes):
        needs = {}

        def add(d):
            for s, v in d.items():
                if needs.get(s, 0) < v:
                    needs[s] = v
        for t in reads:
            add(t.w)
        for t in writes:
            add(t.r)
            if not t.multi:
                add(t.w)
        return needs

    def _emit_waits(self, eng, needs):
        for s, v in needs.items():
            if eng.name == "pe" and s is eng.sem:
                continue
            if eng.known.get(s, 0) < v:
                eng.h.wait_ge(s, v)
                eng.known[s] = v

    def _commit(self, handle, reads, writes):
        s, v = handle
        if self.all_handles.get(s, 0) < v:
            self.all_handles[s] = v
        for t in writes:
            if t.multi:
                if t.r:
                    t.w = {}
                    t.r = {}
                if t.w.get(s, 0) < v:
                    t.w[s] = v
            else:
                t.w = {s: v}
                t.r = {}
        for t in reads:
            if t.r.get(s, 0) < v:
                t.r[s] = v

    def op(self, eng, fn, reads=(), writes=(), sig=True):
        self._emit_waits(eng, self._needs(reads, writes))
        ins = fn()
        if sig:
            eng.count += 1
            ins.then_inc(eng.sem, 1)
            self._commit((eng.sem, eng.count), reads, writes)
        else:
            assert eng.name == "pe"
            self._commit((eng.sem, eng.count + 1), reads, writes)
        self.n_instr += 1
        return ins

    def dma(self, q, out, in_, semtok, reads=(), writes=(), **kw):
        self._emit_waits(q, self._needs(reads, writes))
        ins = q.h.dma_start(out=out, in_=in_, **kw)
        semtok.dcount += 16
        ins.then_inc(semtok.dsem, 16)
        self._commit((semtok.dsem, semtok.dcount), reads, writes)
        self.n_instr += 1
        return ins

    def coll(self, kind, src, dst, rg, semtok, reads=(), writes=()):
        q = self.pool
        self._emit_waits(q, self._needs(reads, writes))
        ins = self.nc.gpsimd.collective_compute(kind, ALU.bypass, replica_groups=rg, ins=[src.opt()], outs=[dst.opt()])
        semtok.dcount += 16
        ins.then_inc(semtok.dsem, 16)
        self._commit((semtok.dsem, semtok.dcount), reads, writes)
        self.n_instr += 1
        return ins

    def barrier(self):
        for e in self.engs:
            self._emit_waits(e, dict(self.all_handles))


def build(depth=DEPTH, dbg=False):
    nc = bass.Bass("TRN2", target_bir_lowering=False)
    fw = FW(nc)

    def din(name, shape, dt=F32):
        return nc.dram_tensor(name, list(shape), dt, kind="ExternalInput").ap()

    def dscr(name, shape, dt):
        kind = "ExternalOutput" if dbg else "Internal"
        return nc.dram_tensor(name, list(shape), dt, kind=kind).ap()

    xT_in = din("xT", [KC, 128, S])
    memT_in = din("memT", [KC, 128, MEM])
    gT_in = din("gT", [depth, 128, KC])
    mgT_in = din("mgnT", [depth, 128, KC])
    fgT_in = din("fgT", [128, KC])
    w_in = din("w_in", [depth, D, INW])
    cw_in = din("cw", [depth, 128, 16, 3])
    cb_in = din("cb", [depth, 128, 16])
    posT_in = din("posT", [depth, 2, 128, 32])
    w1_in = din("w1", [depth, 2, D, 128])
    w2_in = din("w2", [depth, 2, 128, 128])
    wmk_in = din("wmk", [depth, D, 1024])
    wua_in = din("wua", [depth, 2048, D])
    wub_in = din("wub", [depth, 2048, D])
    wux_in = din("wux", [depth, 512, D])
    wo_in = din("wo", [depth, D, D])
    cos_in = din("c_cos", [128, S])
    sin_in = din("c_sin", [128, S])
    ident_in = din("c_ident", [128, 128], BF16)
    caus_in = din("c_caus", [128, 512], BF16)
    wedge_in = din("c_wedge", [128, 512], BF16)
    cbias_in = din("c_cbias", [128, 16, 512], BF16)
    expand_in = din("c_expand", [128, 16, 128], BF16)
    selc_in = din("c_selc", [128, 16, 32])
    ovl_in = din("c_ovl", [128, 32], BF16)
    out = nc.dram_tensor("outT", [KC, 128, S], F32, kind="ExternalOutput").ap()

    xT_s = dscr("xT_s", [KC, 128, S], F32)
    yaT = dscr("yaT", [2048, S], BF16)
    qpT = dscr("qpT", [16, 128, S], BF16)
    qrT = dscr("qrT", [16, 128, S], BF16)
    kcmpT = dscr("kcmpT", [4, 128, S], BF16)
    vcmpT = dscr("vcmpT", [4, 128, S], BF16)
    kslcT = dscr("kslcT", [4, 128, S], BF16)
    kwinT = dscr("kwinT", [4, 128, S], BF16)
    vslc = dscr("vslc", [S, 512], BF16)
    vwin = dscr("vwin", [S, 512], BF16)
    ngs = dscr("ngs", [S, 48], F32)
    nzT = dscr("nzT", [2048, S], BF16)
    xqT = dscr("xqT", [4, 128, S], BF16)
    xzT = dscr("xzT", [512, S], BF16)
    mgT = dscr("mgT", [3, D, S], BF16)
    ybT = dscr("ybT", [2048, S], BF16)
    yxT = dscr("yxT", [512, S], BF16)
    uT_s = dscr("uT_s", [D, S], BF16)

    T = {n: fw.tok(n, multi=True) for n in
         ["xT_s", "yaT", "qpT", "qrT", "kcmpT", "vcmpT", "kslcT", "kwinT", "vslc", "vwin", "ngs", "nzT",
          "xqT", "xzT", "mgT", "ybT", "yxT", "uT_s", "out"]}
    t_ext = fw.tok("ext", multi=True)

    ident = fw.sbuf("ident", [128, 128], BF16)
    caus = fw.sbuf("caus", [128, 512], BF16)
    wedge = fw.sbuf("wedge", [128, 512], BF16)
    ones = fw.sbuf("ones", [128, 128], BF16)
    t_const = fw.tok("const", dma=True)
    for dst, src in ((ident, ident_in), (caus, caus_in), (wedge, wedge_in)):
        fw.dma(fw.sp, dst[:], src, t_const, writes=[t_const])
    t_ones = fw.tok("ones")
    fw.op(fw.dve, lambda: nc.vector.memset(ones[:], 1.0), writes=[t_ones])
    t_cp = fw.tok("cp", dma=True)
    for k in range(0, KC, 8):
        fw.dma(fw.sp, xT_s[k:k + 8], xT_in[k:k + 8], t_cp, reads=[t_ext], writes=[T["xT_s"]])
    fw.barrier()

    def norm_phase(src, src_tok, g_dram, ntok, dst_hT=None, t_hT=None, hoff=0, dst_dram=None, dst_tok=None):
        with fw.scope():
            gsb = fw.sbuf("gsb", [128, KC], F32)
            t_g = fw.tok("g", dma=True)
            fw.dma(fw.sp, gsb[:], g_dram, t_g, reads=[t_ext], writes=[t_g])
            xs = [fw.sbuf("xs", [128, KC, 128], F32) for _ in range(2)]
            t_xs = [fw.tok("xs", dma=True) for _ in range(2)]
            sq = fw.sbuf("sq", [128, KC, 128], BF16)
            t_sq = fw.tok("sq")
            psn = fw.psum("psn", [128, 512], F32)
            t_psn = fw.tok("psn")
            sd = fw.sbuf("sd", [128, 128], F32)
            t_sd = fw.tok("sd")
            rstd = [fw.sbuf("rstd", [128, 128], F32) for _ in range(2)]
            t_rstd = [fw.tok("rstd") for _ in range(2)]
            if dst_dram is not None:
                ob = [fw.sbuf("ob", [128, KC, 128], F32) for _ in range(2)]
                t_ob = [fw.tok("ob", dma=True, multi=True) for _ in range(2)]
            for i in range(ntok // 128):
                b = i % 2
                t0 = i * 128
                fw.dma(fw.sp, xs[b][:], src[:, :, t0:t0 + 128].rearrange("k p t -> p k t"), t_xs[b],
                       reads=[src_tok], writes=[t_xs[b]])
                fw.op(fw.act, lambda: nc.scalar.activation(out=sq[:], in_=xs[b][:], func=AF.Square),
                      reads=[t_xs[b]], writes=[t_sq])
                for kc in range(KC):
                    fw.op(fw.pe, lambda: nc.tensor.matmul(psn[:, 0:128], lhsT=ones[:], rhs=sq[:, kc, :],
                                                          start=(kc == 0), stop=(kc == KC - 1)),
                          reads=[t_sq, t_ones], writes=[t_psn])
                fw.op(fw.dve, lambda: nc.vector.tensor_scalar(out=sd[:], in0=psn[:, 0:128], scalar1=1.0 / D,
                                                              scalar2=EPS, op0=ALU.mult, op1=ALU.add),
                      reads=[t_psn], writes=[t_sd])
                fw.op(fw.act, lambda: nc.scalar.activation(out=sd[:], in_=sd[:], func=AF.Sqrt),
                      reads=[t_sd], writes=[t_sd])
                fw.op(fw.dve, lambda: nc.vector.reciprocal(out=rstd[b][:], in_=sd[:]),
                      reads=[t_sd], writes=[t_rstd[b]])
                for kc in range(KC):
                    if dst_dram is None:
                        o_ap, o_tok = dst_hT[:, kc, hoff + t0:hoff + t0 + 128], t_hT
                    else:
                        o_ap, o_tok = ob[b][:, kc, :], t_ob[b]
                    fw.op(fw.dve, lambda: nc.vector.scalar_tensor_tensor(
                        out=o_ap, in0=xs[b][:, kc, :], scalar=gsb[:, kc:kc + 1], in1=rstd[b][:],
                        op0=ALU.mult, op1=ALU.mult), reads=[t_xs[b], t_rstd[b], t_g], writes=[o_tok])
                if dst_dram is not None:
                    fw.dma(fw.sp, dst_dram[:, :, t0:t0 + 128].rearrange("k p t -> p k t"), ob[b][:], t_ob[b],
                           reads=[t_ob[b]], writes=[dst_tok])

    class WStream:
        def __init__(self):
            self.wt = [fw.sbuf("wt", [128, KC, 512], BF16) for _ in range(2)]
            self.t_wt = [fw.tok("wt", dma=True) for _ in range(2)]

        def run(self, groups):
            def load(gi):
                b = gi % 2
                for (src, kc0, nkc, col0, ncols) in groups[gi][0]:
                    fw.dma(fw.pool, self.wt[b][:, kc0:kc0 + nkc, col0:col0 + ncols],
                           src.rearrange("(k p) c -> p k c", p=128), self.t_wt[b],
                           reads=[t_ext], writes=[self.t_wt[b]])
            load(0)
            for gi in range(len(groups)):
                if gi + 1 < len(groups):
                    load(gi + 1)
                groups[gi][1](self.wt[gi % 2], self.t_wt[gi % 2])

    for l in range(depth):
        with fw.scope():
            halo = fw.sbuf("halo", [128, 16, 2], F32)
            t_halo = fw.tok("halo")
            fw.op(fw.dve, lambda: nc.vector.memset(halo[:], 0.0), writes=[t_halo])
            mK = fw.sbuf("mK", [128, 4, MEM], BF16)
            t_mK = fw.tok("mK")
            mV = fw.sbuf("mV", [128, 2, 4, 129], BF16)
            t_mV = fw.tok("mV")
            fw.op(fw.dve, lambda: nc.vector.memset(mV[:], 1.0), writes=[t_mV])

            with fw.scope():
                mhT = fw.sbuf("mhT", [128, KC, MEM], BF16)
                t_mhT = fw.tok("mhT", multi=True)
                norm_phase(memT_in, t_ext, mgT_in[l], MEM, dst_hT=mhT, t_hT=t_mhT)
                ws = WStream()
                pa = [fw.psum("pa", [128, 512], F32) for _ in range(2)]
                t_pa = [fw.tok("pa") for _ in range(2)]

                def hK(wtb, t_wtb):
                    for hx in range(4):
                        p = hx % 2
                        for kc in range(KC):
                            fw.op(fw.pe, lambda: nc.tensor.matmul(pa[p][:, 0:MEM], lhsT=wtb[:, kc, hx * 128:(hx + 1) * 128],
                                                                  rhs=mhT[:, kc, :], start=(kc == 0), stop=(kc == KC - 1)),
                                  reads=[t_wtb, t_mhT], writes=[t_pa[p]])
                        fw.op(fw.act, lambda: nc.scalar.copy(out=mK[:, hx, :], in_=pa[p][:, 0:MEM]),
                              reads=[t_pa[p]], writes=[t_mK])

                def hV(wtb, t_wtb):
                    for mt in range(2):
                        p = mt % 2
                        for kc in range(KC):
                            fw.op(fw.pe, lambda: nc.tensor.matmul(pa[p][:], lhsT=mhT[:, kc, mt * 128:(mt + 1) * 128],
                                                                  rhs=wtb[:, kc, :], start=(kc == 0), stop=(kc == KC - 1)),
                                  reads=[t_wtb, t_mhT], writes=[t_pa[p]])
                        fw.op(fw.act, lambda: nc.scalar.copy(out=mV[:, mt, :, 0:128],
                                                             in_=pa[p][:].rearrange("p (h d) -> p h d", h=4)),
                              reads=[t_pa[p]], writes=[t_mV])
                ws.run([([(wmk_in[l][:, 0:512], 0, KC, 0, 512)], hK),
                        ([(wmk_in[l][:, 512:1024], 0, KC, 0, 512)], hV)])

            for Tt in range(S // TT):
                tk0 = Tt * TT
                with fw.scope():
                    hT = fw.sbuf("hT", [128, KC, TT], BF16)
                    t_hT = fw.tok("hT", multi=True)
                    norm_phase(xT_s[:, :, tk0:tk0 + TT], T["xT_s"], gT_in[l], TT, dst_hT=hT, t_hT=t_hT)
                    ws = WStream()
                    pa = [fw.psum("pa", [128, 512], F32) for _ in range(4)]
                    t_pa = [fw.tok("pa") for _ in range(4)]
                    stg = [fw.sbuf("stg", [128, TT], BF16) for _ in range(4)]
                    t_stg = [fw.tok("stg", dma=True) for _ in range(4)]
                    cnt = {"pa": 0, "stg": 0, "ng": 0}
                    cosb = fw.sbuf("cosb", [128, TT], F32)
                    sinb = fw.sbuf("sinb", [128, TT], F32)
                    cwb = fw.sbuf("cwb", [128, 16, 3], F32)
                    cbb = fw.sbuf("cbb", [128, 16], F32)
                    t_tab = fw.tok("tab", dma=True)
                    fw.dma(fw.sp, cosb[:], cos_in[:, tk0:tk0 + TT], t_tab, reads=[t_ext], writes=[t_tab])
                    fw.dma(fw.sp, sinb[:], sin_in[:, tk0:tk0 + TT], t_tab, reads=[t_ext], writes=[t_tab])
                    fw.dma(fw.sp, cwb[:], cw_in[l], t_tab, reads=[t_ext], writes=[t_tab])
                    fw.dma(fw.sp, cbb[:], cb_in[l], t_tab, reads=[t_ext], writes=[t_tab])
                    x32 = fw.sbuf("x32", [128, 512], F32)
                    xsw = fw.sbuf("xsw", [128, 512], F32)
                    r1 = fw.sbuf("r1", [128, 512], F32)
                    r2 = fw.sbuf("r2", [128, 512], F32)
                    t_x32, t_xsw, t_r1, t_r2 = fw.tok(), fw.tok(), fw.tok(), fw.tok()
                    ah = fw.sbuf("ah", [128, 512], F32)
                    aB = fw.sbuf("aB", [128, 512], F32)
                    sz = fw.sbuf("sz", [128, 512], F32)
                    yv = fw.sbuf("yv", [128, 512], F32)
                    ub = fw.sbuf("ub", [128, 514], F32)
                    t_ah, t_aB, t_sz, t_yv, t_ub = fw.tok(), fw.tok(), fw.tok(), fw.tok(), fw.tok()
                    ngst = [fw.sbuf("ngst", [128, 48], F32) for _ in range(2)]
                    t_ngst = [fw.tok("ngst", dma=True) for _ in range(2)]

                    def accum_fm(wtb, t_wtb, m, th):
                        p = cnt["pa"] % 4
                        cnt["pa"] += 1
                        for kc in range(KC):
                            fw.op(fw.pe, lambda: nc.tensor.matmul(pa[p][:], lhsT=wtb[:, kc, m * 128:(m + 1) * 128],
                                                                  rhs=hT[:, kc, th * 512:(th + 1) * 512],
                                                                  start=(kc == 0), stop=(kc == KC - 1)),
                                  reads=[t_wtb, t_hT], writes=[t_pa[p]], sig=(kc == KC - 1))
                        return p

                    def next_stg():
                        s_ = cnt["stg"] % 4
                        cnt["stg"] += 1
                        return s_

                    def rope_evac(p, th, sdst):
                        fw.op(fw.act, lambda: nc.scalar.copy(out=x32[:], in_=pa[p][:]), reads=[t_pa[p]], writes=[t_x32])
                        fw.op(fw.act, lambda: nc.scalar.copy(out=xsw[64:128, :], in_=pa[p][0:64, :]),
                              reads=[t_pa[p]], writes=[t_xsw])
                        fw.op(fw.dve, lambda: nc.vector.tensor_copy(out=xsw[0:64, :], in_=pa[p][64:128, :]),
                              reads=[t_pa[p]], writes=[t_xsw])
                        fw.op(fw.dve, lambda: nc.vector.tensor_tensor(out=r1[:], in0=x32[:], in1=cosb[:, th * 512:(th + 1) * 512],
                                                                      op=ALU.mult), reads=[t_x32, t_tab], writes=[t_r1])
                        fw.op(fw.dve, lambda: nc.vector.tensor_tensor(out=r2[:], in0=xsw[:], in1=sinb[:, th * 512:(th + 1) * 512],
                                                                      op=ALU.mult), reads=[t_xsw, t_tab], writes=[t_r2])
                        fw.op(fw.dve, lambda: nc.vector.tensor_tensor(out=stg[sdst][:, th * 512:(th + 1) * 512], in0=r1[:],
                                                                      in1=r2[:], op=ALU.add),
                              reads=[t_r1, t_r2], writes=[t_stg[sdst]])

                    def fm_handler(chunks):
                        def h(wtb, t_wtb):
                            for m, ch in enumerate(chunks):
                                kind = ch[0]
                                s1 = next_stg()
                                s2 = next_stg() if kind == "qrope" else None
                                for th in range(2):
                                    p = accum_fm(wtb, t_wtb, m, th)
                                    sl = slice(th * 512, (th + 1) * 512)
                                    if kind in ("copy", "silu", "sigmoid"):
                                        func = {"copy": AF.Copy, "silu": AF.Silu, "sigmoid": AF.Sigmoid}[kind]
                                        fw.op(fw.act, lambda: nc.scalar.activation(out=stg[s1][:, sl], in_=pa[p][:], func=func),
                                              reads=[t_pa[p]], writes=[t_stg[s1]])
                                    elif kind == "rope":
                                        rope_evac(p, th, s1)
                                    elif kind == "qrope":
                                        fw.op(fw.act, lambda: nc.scalar.copy(out=stg[s2][:, sl], in_=pa[p][:]),
                                              reads=[t_pa[p]], writes=[t_stg[s2]])
                                        rope_evac(p, th, s1)
                                fw.dma(fw.sp, ch[1][:, tk0:tk0 + TT], stg[s1][:], t_stg[s1], reads=[t_stg[s1]], writes=[ch[2]])
                                if kind == "qrope":
                                    fw.dma(fw.sp, ch[3][:, tk0:tk0 + TT], stg[s2][:], t_stg[s2], reads=[t_stg[s2]], writes=[ch[4]])
                        return h

                    def conv_handler(cb):
                        def h(wtb, t_wtb):
                            s1 = next_stg()
                            for th in range(2):
                                sl = slice(th * 512, (th + 1) * 512)
                                if th == 0:
                                    fw.op(fw.dve, lambda: nc.vector.tensor_copy(out=ub[:, 0:2], in_=halo[:, cb, :]),
                                          reads=[t_halo], writes=[t_ub])
                                else:
                                    fw.op(fw.dve, lambda: nc.vector.tensor_copy(out=ub[:, 0:2], in_=ub[:, 512:514]),
                                          reads=[t_ub], writes=[t_ub])
                                p = accum_fm(wtb, t_wtb, 0, th)
                                fw.op(fw.act, lambda: nc.scalar.copy(out=ah[:], in_=pa[p][:]), reads=[t_pa[p]], writes=[t_ah])
                                p = accum_fm(wtb, t_wtb, 1, th)
                                fw.op(fw.act, lambda: nc.scalar.copy(out=aB[:], in_=pa[p][:]), reads=[t_pa[p]], writes=[t_aB])
                                p = accum_fm(wtb, t_wtb, 2, th)
                                fw.op(fw.dve, lambda: nc.vector.tensor_tensor(out=ub[:, 2:514], in0=pa[p][:], in1=ah[:], op=ALU.mult),
                                      reads=[t_pa[p], t_ah], writes=[t_ub])
                                p = accum_fm(wtb, t_wtb, 3, th)
                                fw.op(fw.act, lambda: nc.scalar.activation(out=sz[:], in_=pa[p][:], func=AF.Silu),
                                      reads=[t_pa[p]], writes=[t_sz])
                                fw.op(fw.dve, lambda: nc.vector.tensor_scalar(out=yv[:], in0=ub[:, 2:514], scalar1=cwb[:, cb, 2:3],
                                                                              scalar2=cbb[:, cb:cb + 1], op0=ALU.mult, op1=ALU.add),
                                      reads=[t_ub, t_tab], writes=[t_yv])
                                fw.op(fw.dve, lambda: nc.vector.scalar_tensor_tensor(out=yv[:], in0=ub[:, 1:513], scalar=cwb[:, cb, 1:2],
                                                                                     in1=yv[:], op0=ALU.mult, op1=ALU.add),
                                      reads=[t_ub, t_tab, t_yv], writes=[t_yv])
                                fw.op(fw.dve, lambda: nc.vector.scalar_tensor_tensor(out=yv[:], in0=ub[:, 0:512], scalar=cwb[:, cb, 0:1],
                                                                                     in1=yv[:], op0=ALU.mult, op1=ALU.add),
                                      reads=[t_ub, t_tab, t_yv], writes=[t_yv])
                                fw.op(fw.dve, lambda: nc.vector.tensor_tensor(out=yv[:], in0=yv[:], in1=aB[:], op=ALU.mult),
                                      reads=[t_yv, t_aB], writes=[t_yv])
                                fw.op(fw.dve, lambda: nc.vector.tensor_tensor(out=stg[s1][:, sl], in0=yv[:], in1=sz[:], op=ALU.mult),
                                      reads=[t_yv, t_sz], writes=[t_stg[s1]])
                            fw.op(fw.dve, lambda: nc.vector.tensor_copy(out=halo[:, cb, :], in_=ub[:, 512:514]),
                                  reads=[t_ub], writes=[t_halo])
                            fw.dma(fw.sp, yaT[cb * 128:(cb + 1) * 128, tk0:tk0 + TT], stg[s1][:], t_stg[s1],
                                   reads=[t_stg[s1]], writes=[T["yaT"]])
                        return h

                    def tm_handler(ncols, dst, dst_tok, kind):
                        def h(wtb, t_wtb):
                            for tb in range(TT // 128):
                                p = cnt["pa"] % 4
                                cnt["pa"] += 1
                                for kc in range(KC):
                                    fw.op(fw.pe, lambda: nc.tensor.matmul(pa[p][:, 0:ncols], lhsT=hT[:, kc, tb * 128:(tb + 1) * 128],
                                                                          rhs=wtb[:, kc, 0:ncols], start=(kc == 0), stop=(kc == KC - 1)),
                                          reads=[t_wtb, t_hT], writes=[t_pa[p]], sig=(kc == KC - 1))
                                r0 = tk0 + tb * 128
                                if kind == "v":
                                    s1 = next_stg()
                                    fw.op(fw.act, lambda: nc.scalar.copy(out=stg[s1][:, 0:512], in_=pa[p][:]),
                                          reads=[t_pa[p]], writes=[t_stg[s1]])
                                    fw.dma(fw.sp, dst[r0:r0 + 128, :], stg[s1][:, 0:512], t_stg[s1], reads=[t_stg[s1]], writes=[dst_tok])
                                else:
                                    n_ = cnt["ng"] % 2
                                    cnt["ng"] += 1
                                    fw.op(fw.act, lambda: nc.scalar.activation(out=ngst[n_][:], in_=pa[p][:, 0:48], func=AF.Sigmoid),
                                          reads=[t_pa[p]], writes=[t_ngst[n_]])
                                    fw.dma(fw.sp, dst[r0:r0 + 128, :], ngst[n_][:], t_ngst[n_], reads=[t_ngst[n_]], writes=[dst_tok])
                        return h

                    W = w_in[l]
                    groups = []
                    for cb in range(16):
                        segs = [(W[:, o + cb * 128:o + cb * 128 + 128], 0, KC, j * 128, 128)
                                for j, o in enumerate((O_AH, O_AB, O_AC, O_AZ))]
                        groups.append((segs, conv_handler(cb)))
                    for hg in range(4):
                        segs = [(W[:, O_Q + hg * 512:O_Q + hg * 512 + 512], 0, KC, 0, 512)]
                        groups.append((segs, fm_handler([("qrope", qrT[hg * 4 + r], T["qrT"], qpT[hg * 4 + r], T["qpT"])
                                                         for r in range(4)])))
                    kvo = lambda br, kvi: O_KV + (br * 2 + kvi) * 512
                    for (br, dstT, nm) in ((1, kslcT, "kslcT"), (2, kwinT, "kwinT")):
                        groups.append(([(W[:, kvo(br, 0):kvo(br, 0) + 512], 0, KC, 0, 512)],
                                       fm_handler([("rope", dstT[g], T[nm]) for g in range(4)])))
                    for (kvi, dstT, nm) in ((0, kcmpT, "kcmpT"), (1, vcmpT, "vcmpT")):
                        groups.append(([(W[:, kvo(0, kvi):kvo(0, kvi) + 512], 0, KC, 0, 512)],
                                       fm_handler([("copy", dstT[g], T[nm]) for g in range(4)])))
                    for (br, dst, nm) in ((1, vslc, "vslc"), (2, vwin, "vwin")):
                        groups.append(([(W[:, kvo(br, 1):kvo(br, 1) + 512], 0, KC, 0, 512)], tm_handler(512, dst, T[nm], "v")))
                    groups.append(([(W[:, O_NG:O_NG + 48], 0, KC, 0, 48)], tm_handler(48, ngs, T["ngs"], "g")))
                    for k in range(4):
                        groups.append(([(W[:, O_NZ + k * 512:O_NZ + k * 512 + 512], 0, KC, 0, 512)],
                                       fm_handler([("silu", nzT[(k * 4 + r) * 128:(k * 4 + r + 1) * 128], T["nzT"]) for r in range(4)])))
                    groups.append(([(W[:, O_XQ:O_XQ + 512], 0, KC, 0, 512)],
                                   fm_handler([("copy", xqT[r], T["xqT"]) for r in range(4)])))
                    groups.append(([(W[:, O_XZ:O_XZ + 512], 0, KC, 0, 512)],
                                   fm_handler([("silu", xzT[r * 128:(r + 1) * 128], T["xzT"]) for r in range(4)])))
                    for k in range(24):
                        gi_, dc0 = divmod(k * 4, 32)
                        groups.append(([(W[:, O_MG + k * 512:O_MG + k * 512 + 512], 0, KC, 0, 512)],
                                       fm_handler([("sigmoid", mgT[gi_][(dc0 + r) * 128:(dc0 + r + 1) * 128], T["mgT"])
                                                   for r in range(4)])))
                    ws.run(groups)

            with fw.scope():
                kcA = fw.sbuf("kcA", [128, 4, 128], BF16)
                vcA = fw.sbuf("vcA", [128, 4, 161], BF16)
                t_kcA, t_vcA = fw.tok("kcA"), fw.tok("vcA", dma=True)
                fw.op(fw.dve, lambda: nc.vector.memset(kcA[:], 0.0), writes=[t_kcA])
                fw.op(fw.dve, lambda: nc.vector.memset(vcA[:], 1.0), writes=[t_vcA])
                for g in range(4):
                    fw.dma(fw.sp, vcA[:, g, 129:161], ovl_in, t_vcA, reads=[t_ext], writes=[t_vcA])
                cbias = fw.sbuf("cbias", [128, 16, 512], BF16)
                expand = fw.sbuf("expand", [128, 16, 128], BF16)
                selc = fw.sbuf("selc", [128, 16, 32], F32)
                t_c3 = fw.tok("c3", dma=True)
                fw.dma(fw.sp, cbias[:], cbias_in, t_c3, reads=[t_ext], writes=[t_c3])
                fw.dma(fw.sp, expand[:], expand_in, t_c3, reads=[t_ext], writes=[t_c3])
                fw.dma(fw.sp, selc[:], selc_in, t_c3, reads=[t_ext], writes=[t_c3])

                with fw.scope():
                    w1b = fw.sbuf("w1b", [128, 32, 128], BF16)
                    w2b = fw.sbuf("w2b", [128, 128], BF16)
                    posb = fw.sbuf("posb", [128, 32], BF16)
                    t_w1 = fw.tok("w1", dma=True)
                    kcs = fw.sbuf("kcs", [128, S], BF16)
                    t_kcs = fw.tok("kcs", dma=True)
                    pp = fw.psum("pp", [128, 512], F32)
                    t_pp = fw.tok("pp")
                    pq = fw.psum("pq", [128, 512], F32)
                    t_pq = fw.tok("pq")
                    pbias = fw.sbuf("pbias", [128, 1], F32)
                    t_pbias = fw.tok("pbias")
                    hid = fw.sbuf("hid", [128, 128], BF16)
                    t_hid = fw.tok("hid")
                    fw.op(fw.dve, lambda: nc.vector.memset(hid[:], 0.0), writes=[t_hid])
                    for c in range(2):
                        fw.dma(fw.pool, w1b[:], w1_in[l, c].rearrange("(l d) o -> d l o", d=128), t_w1, reads=[t_ext], writes=[t_w1])
                        fw.dma(fw.pool, w2b[:], w2_in[l, c], t_w1, reads=[t_ext], writes=[t_w1])
                        fw.dma(fw.pool, posb[:], posT_in[l, c], t_w1, reads=[t_ext], writes=[t_w1])
                        for ll in range(32):
                            fw.op(fw.pe, lambda: nc.tensor.matmul(pq[:, 0:1], lhsT=w1b[:, ll, :], rhs=posb[:, ll:ll + 1],
                                                                  start=(ll == 0), stop=(ll == 31)),
                                  reads=[t_w1], writes=[t_pq])
                        fw.op(fw.act, lambda: nc.scalar.copy(out=pbias[:], in_=pq[:, 0:1]), reads=[t_pq], writes=[t_pbias])
                        srcT = kcmpT if c == 0 else vcmpT
                        for g in range(4):
                            fw.dma(fw.sp, kcs[:], srcT[g], t_kcs, reads=[T["kcmpT"], T["vcmpT"]], writes=[t_kcs])
                            for ll in range(32):
                                fw.op(fw.pe, lambda: nc.tensor.matmul(pp[:, 0:127], lhsT=w1b[:, ll, :],
                                                                      rhs=kcs[:, ll:ll + 16 * 126 + 1:16],
                                                                      start=(ll == 0), stop=(ll == 31)),
                                      reads=[t_w1, t_kcs], writes=[t_pp])
                            fw.op(fw.act, lambda: nc.scalar.activation(out=hid[:, 0:127], in_=pp[:, 0:127], func=AF.Silu,
                                                                       bias=pbias[:, 0:1]),
                                  reads=[t_pp, t_pbias], writes=[t_hid])
                            if c == 0:
                                fw.op(fw.pe, lambda: nc.tensor.matmul(pq[:, 0:127], lhsT=w2b[:], rhs=hid[:, 0:127], start=True, stop=True),
                                      reads=[t_w1, t_hid], writes=[t_pq])
                                fw.op(fw.act, lambda: nc.scalar.copy(out=kcA[:, g, 0:127], in_=pq[:, 0:127]),
                                      reads=[t_pq], writes=[t_kcA])
                            else:
                                fw.op(fw.pe, lambda: nc.tensor.matmul(pq[:, 0:128], lhsT=hid[:, 0:128], rhs=w2b[:], start=True, stop=True),
                                      reads=[t_w1, t_hid], writes=[t_pq])
                                fw.op(fw.act, lambda: nc.scalar.copy(out=vcA[:, g, 0:128], in_=pq[:, 0:128]),
                                      reads=[t_pq], writes=[t_vcA])

                ks2 = [fw.sbuf("ks", [128, S], BF16) for _ in range(2)]
                kw2 = [fw.sbuf("kw", [128, S], BF16) for _ in range(2)]
                vsA2 = [fw.sbuf("vsA", [128, 16, 129], BF16) for _ in range(2)]
                vwA2 = [fw.sbuf("vwA", [128, 16, 129], BF16) for _ in range(2)]
                t_kv2 = [fw.tok("kv", dma=True, multi=True) for _ in range(2)]
                for p_ in range(2):
                    fw.op(fw.dve, lambda: nc.vector.memset(vsA2[p_][:], 1.0), writes=[t_kv2[p_]])
                    fw.op(fw.dve, lambda: nc.vector.memset(vwA2[p_][:], 1.0), writes=[t_kv2[p_]])
                qp = [fw.sbuf("qp", [128, 4, 128], BF16) for _ in range(2)]
                qr = [fw.sbuf("qr", [128, 4, 128], BF16) for _ in range(2)]
                gs = [fw.sbuf("gs", [128, 48], F32) for _ in range(2)]
                nz = [fw.sbuf("nz", [128, 4, 128], BF16) for _ in range(2)]
                t_q = [fw.tok("q", dma=True, multi=True) for _ in range(2)]
                NE = 24
                Eb = fw.sbuf("Eb", [128, NE, 512], BF16)
                t_E = [fw.tok("E") for _ in range(NE)]
                pss = [fw.psum("pss", [128, 512], F32) for _ in range(3)]
                t_pss = [fw.tok("pss") for _ in range(3)]
                po = [fw.psum("po", [128, 512], F32) for _ in range(4)]
                t_po = [fw.tok("po") for _ in range(4)]
                pst = fw.psum("pst", [128, 1024], BF16)
                t_pst = fw.tok("pst")
                den = fw.sbuf("den", [128, 4], F32)
                rden = fw.sbuf("rden", [128, 4], F32)
                cc = fw.sbuf("cc", [128, 4], F32)
                imp = fw.sbuf("imp", [128, 32], F32)
                imp3 = fw.sbuf("imp3", [128, 32], F32)
                m1 = fw.sbuf("m1", [128, 8], F32)
                m2 = fw.sbuf("m2", [128, 8], F32)
                selb = fw.sbuf("selb", [128, 128], BF16)
                selbT = fw.sbuf("selbT", [128, 512], BF16)
                oacc = fw.sbuf("oacc", [128, 4, 128], F32)
                obf = fw.sbuf("obf", [128, 4, 128], BF16)
                yst = [fw.sbuf("yst", [128, 4, 128], BF16) for _ in range(2)]
                t_yst = [fw.tok("yst", dma=True) for _ in range(2)]
                t_den, t_rden, t_cc, t_imp, t_imp3, t_m1, t_m2 = (fw.tok() for _ in range(7))
                t_selb, t_selbT, t_oacc, t_obf = (fw.tok() for _ in range(4))
                fw.op(fw.dve, lambda: nc.vector.memset(selb[:], 0.0), writes=[t_selb])
                cn = {"e": 0, "s": 0, "po": 0, "it": 0}

                def po_ap(pair, r, c0, c1, w):
                    bank = po[pair * 2 + r // 2]
                    off = (r % 2) * w
                    return bank[:, off + c0:off + c1], t_po[pair * 2 + r // 2]

                def score_tile(lhsT, lhs_toks, rhs, rhs_toks, biases):
                    sb_ = cn["s"] % 3
                    cn["s"] += 1
                    e_ = cn["e"] % NE
                    cn["e"] += 1
                    fw.op(fw.pe, lambda: nc.tensor.matmul(pss[sb_][:], lhsT=lhsT, rhs=rhs, start=True, stop=(len(biases) == 0)),
                          reads=lhs_toks + rhs_toks, writes=[t_pss[sb_]])
                    nb = len(biases)
                    for bi, (bl, br_, btoks) in enumerate(biases):
                        fw.op(fw.pe, lambda: nc.tensor.matmul(pss[sb_][:], lhsT=bl, rhs=br_, start=False, stop=(bi == nb - 1)),
                              reads=btoks, writes=[t_pss[sb_]])
                    fw.op(fw.act, lambda: nc.scalar.activation(out=Eb[:, e_, :], in_=pss[sb_][:], func=AF.Exp, scale=SCALE),
                          reads=[t_pss[sb_]], writes=[t_E[e_]])
                    return e_

                def pv_and_combine(eslots, vA, v_toks, vsel, width, br, g, b, first, last):
                    pair = cn["po"] % 2
                    cn["po"] += 1
                    n = len(eslots)
                    for r in range(4):
                        oap, otok = po_ap(pair, r, 0, width, width)
                        for j, (e_, kt) in enumerate(eslots):
                            fw.op(fw.pe, lambda: nc.tensor.matmul(oap, lhsT=Eb[:, e_, r * 128:(r + 1) * 128], rhs=vsel(kt),
                                                                  start=(j == 0), stop=(j == n - 1)),
                                  reads=[t_E[e_]] + v_toks, writes=[otok])
                    for r in range(4):
                        dap, otok = po_ap(pair, r, 128, 129, width)
                        fw.op(fw.dve, lambda: nc.vector.tensor_scalar_max(out=den[:, r:r + 1], in0=dap, scalar1=1e-30),
                              reads=[otok], writes=[t_den])
                    fw.op(fw.dve, lambda: nc.vector.reciprocal(out=rden[:], in_=den[:]), reads=[t_den], writes=[t_rden])
                    if br == 0:
                        for r in range(4):
                            iap, otok = po_ap(pair, r, 129, 161, width)
                            if r == 0:
                                fw.op(fw.dve, lambda: nc.vector.tensor_scalar_mul(out=imp[:], in0=iap, scalar1=rden[:, 0:1]),
                                      reads=[otok, t_rden], writes=[t_imp])
                            else:
                                fw.op(fw.dve, lambda: nc.vector.scalar_tensor_tensor(out=imp[:], in0=iap, scalar=rden[:, r:r + 1],
                                                                                     in1=imp[:], op0=ALU.mult, op1=ALU.add),
                                      reads=[otok, t_rden, t_imp], writes=[t_imp])
                    gview = gs[b][:, 12 * g:12 * g + 12].rearrange("p (r c) -> p r c", c=3)[:, :, br]
                    fw.op(fw.dve, lambda: nc.vector.tensor_tensor(out=cc[:], in0=rden[:], in1=gview, op=ALU.mult),
                          reads=[t_rden, t_q[b]], writes=[t_cc])
                    for r in range(4):
                        vap, otok = po_ap(pair, r, 0, 128, width)
                        if first:
                            fw.op(fw.dve, lambda: nc.vector.tensor_scalar_mul(out=oacc[:, r, :], in0=vap, scalar1=cc[:, r:r + 1]),
                                  reads=[otok, t_cc], writes=[t_oacc])
                        else:
                            dst_ap, dst_t = (obf[:, r, :], t_obf) if last else (oacc[:, r, :], t_oacc)
                            fw.op(fw.dve, lambda: nc.vector.scalar_tensor_tensor(out=dst_ap, in0=vap, scalar=cc[:, r:r + 1],
                                                                                 in1=oacc[:, r, :], op0=ALU.mult, op1=ALU.add),
                                  reads=[otok, t_cc, t_oacc], writes=[dst_t])

                def load_kv(g):
                    p_ = g % 2
                    fw.dma(fw.sp, ks2[p_][:], kslcT[g], t_kv2[p_], reads=[T["kslcT"]], writes=[t_kv2[p_]])
                    fw.dma(fw.sp, kw2[p_][:], kwinT[g], t_kv2[p_], reads=[T["kwinT"]], writes=[t_kv2[p_]])
                    fw.dma(fw.sp, vsA2[p_][:, :, 0:128], vslc[:, g * 128:(g + 1) * 128].rearrange("(k p) d -> p k d", p=128), t_kv2[p_],
                           reads=[T["vslc"]], writes=[t_kv2[p_]])
                    fw.dma(fw.sp, vwA2[p_][:, :, 0:128], vwin[:, g * 128:(g + 1) * 128].rearrange("(k p) d -> p k d", p=128), t_kv2[p_],
                           reads=[T["vwin"]], writes=[t_kv2[p_]])

                def load_q(g, i, b):
                    q0 = i * 128
                    fw.dma(fw.sp, qp[b][:], qpT[4 * g:4 * g + 4, :, q0:q0 + 128].rearrange("r p t -> p r t"), t_q[b],
                           reads=[T["qpT"]], writes=[t_q[b]])
                    fw.dma(fw.sp, qr[b][:], qrT[4 * g:4 * g + 4, :, q0:q0 + 128].rearrange("r p t -> p r t"), t_q[b],
                           reads=[T["qrT"]], writes=[t_q[b]])
                    fw.dma(fw.sp, gs[b][:], ngs[q0:q0 + 128, :], t_q[b], reads=[T["ngs"]], writes=[t_q[b]])
                    fw.dma(fw.sp, nz[b][:], nzT[g * 512:(g + 1) * 512, q0:q0 + 128].rearrange("(r p) t -> p r t", p=128), t_q[b],
                           reads=[T["nzT"]], writes=[t_q[b]])

                iters = [(g_, i_) for g_ in range(4) for i_ in range(16)]
                NIT = len(iters)

                def stage_A(n_it):
                    if True:
                        g, i = iters[n_it]
                        b = n_it % 2
                        q0 = i * 128
                        ks, kw, vsA, vwA, t_kv = ks2[g % 2], kw2[g % 2], vsA2[g % 2], vwA2[g % 2], t_kv2[g % 2]
                        qpf = qp[b][:].rearrange("p r t -> p (r t)")
                        qrf = qr[b][:].rearrange("p r t -> p (r t)")
                        e_ = score_tile(kcA[:, g, :], [t_kcA], qpf, [t_q[b]], [(ident[:], cbias[:, i, :], [t_const, t_c3])])
                        pv_and_combine([(e_, 0)], vcA, [t_vcA], lambda kt: vcA[:, g, :], 161, 0, g, b, True, False)
                        sel_bias = []
                        if i >= 8:
                            fw.op(fw.dve, lambda: nc.vector.tensor_tensor(out=imp[:], in0=imp[:], in1=selc[:, i, :], op=ALU.add),
                                  reads=[t_imp, t_c3], writes=[t_imp])
                            fw.op(fw.dve, lambda: nc.vector.max(out=m1[:], in_=imp[:]), reads=[t_imp], writes=[t_m1])
                            fw.op(fw.dve, lambda: nc.vector.match_replace(out=imp3[:], in_to_replace=m1[:], in_values=imp[:],
                                                                          imm_value=-1e9),
                                  reads=[t_imp, t_m1], writes=[t_imp3])
                            fw.op(fw.dve, lambda: nc.vector.max(out=m2[:], in_=imp3[:]), reads=[t_imp3], writes=[t_m2])
                            fw.op(fw.dve, lambda: nc.vector.tensor_scalar(out=selb[:, 0:32], in0=imp[:], scalar1=m2[:, 7:8], scalar2=-1.0,
                                                                          op0=ALU.is_ge, op1=ALU.add),
                                  reads=[t_imp, t_m2], writes=[t_selb])
                        es = []
                        for kt in range(max(0, i - 4), i + 1):
                            biases = []
                            if kt == i:
                                biases.append((ident[:], caus[:], [t_const]))
                            if kt == i - 4:
                                biases.append((ident[:], wedge[:], [t_const]))
                            es.append((score_tile(kw[:, kt * 128:(kt + 1) * 128], [t_kv], qrf, [t_q[b]], biases), kt))
                        pv_and_combine(es, vwA, [t_kv], lambda kt: vwA[:, kt, :], 129, 2, g, b, False, False)

                def stage_B(n_it):
                    if True:
                        g, i = iters[n_it]
                        b = n_it % 2
                        q0 = i * 128
                        ks, kw, vsA, vwA, t_kv = ks2[g % 2], kw2[g % 2], vsA2[g % 2], vwA2[g % 2], t_kv2[g % 2]
                        qpf = qp[b][:].rearrange("p r t -> p (r t)")
                        qrf = qr[b][:].rearrange("p r t -> p (r t)")
                        if i >= 8:
                            for r in range(4):
                                fw.op(fw.pe, lambda: nc.tensor.transpose(pst[:, r * 128:(r + 1) * 128], selb[:], ident[:]),
                                      reads=[t_selb, t_const], writes=[t_pst])
                            fw.op(fw.act, lambda: nc.scalar.copy(out=selbT[:], in_=pst[:, 0:512]), reads=[t_pst], writes=[t_selbT])
                        es = []
                        for kt in range(i + 1):
                            biases = []
                            if i >= 8:
                                biases.append((expand[:, kt, :], selbT[:], [t_c3, t_selbT]))
                            if kt == i:
                                biases.append((ident[:], caus[:], [t_const]))
                            es.append((score_tile(ks[:, kt * 128:(kt + 1) * 128], [t_kv], qrf, [t_q[b]], biases), kt))
                        pv_and_combine(es, vsA, [t_kv], lambda kt: vsA[:, kt, :], 129, 1, g, b, False, True)

                def stage_C(n_it):
                    if True:
                        g, i = iters[n_it]
                        b = n_it % 2
                        q0 = i * 128
                        ks, kw, vsA, vwA, t_kv = ks2[g % 2], kw2[g % 2], vsA2[g % 2], vwA2[g % 2], t_kv2[g % 2]
                        qpf = qp[b][:].rearrange("p r t -> p (r t)")
                        qrf = qr[b][:].rearrange("p r t -> p (r t)")
                        for r in range(4):
                            fw.op(fw.pe, lambda: nc.tensor.transpose(pst[:, r * 128:(r + 1) * 128], obf[:, r, :], ident[:]),
                                  reads=[t_obf, t_const], writes=[t_pst])
                        fw.op(fw.dve, lambda: nc.vector.tensor_tensor(out=yst[b][:].rearrange("p r t -> p (r t)"), in0=pst[:, 0:512],
                                                                      in1=nz[b][:].rearrange("p r t -> p (r t)"), op=ALU.mult),
                              reads=[t_pst, t_q[b]], writes=[t_yst[b]])
                        fw.dma(fw.sp, ybT[g * 512:(g + 1) * 512, q0:q0 + 128].rearrange("(r p) t -> p r t", p=128), yst[b][:], t_yst[b],
                               reads=[t_yst[b]], writes=[T["ybT"]])

                load_kv(0)
                load_q(0, 0, 0)
                stage_A(0)
                if NIT > 1:
                    load_q(iters[1][0], iters[1][1], 1)
                for n_it in range(NIT):
                    g, i = iters[n_it]
                    if i == 0 and g + 1 < 4:
                        load_kv(g + 1)
                    stage_B(n_it)
                    if n_it + 1 < NIT:
                        stage_A(n_it + 1)
                    stage_C(n_it)
                    if n_it + 2 < NIT:
                        load_q(iters[n_it + 2][0], iters[n_it + 2][1], n_it % 2)

            with fw.scope():
                xq = [fw.sbuf("xq", [128, 512], BF16) for _ in range(2)]
                xz = [fw.sbuf("xz", [128, 512], BF16) for _ in range(2)]
                t_xq = [fw.tok("xq", dma=True, multi=True) for _ in range(2)]
                Em = [fw.sbuf("Em", [128, 512], BF16) for _ in range(4)]
                t_Em = [fw.tok("Em") for _ in range(4)]
                pss = [fw.psum("pss", [128, 512], F32) for _ in range(2)]
                t_pss = [fw.tok("pss") for _ in range(2)]
                po = [fw.psum("po", [128, 512], F32) for _ in range(4)]
                t_po = [fw.tok("po") for _ in range(4)]
                pst = fw.psum("pst", [128, 1024], BF16)
                t_pst = fw.tok("pst")
                den = fw.sbuf("den", [128, 4], F32)
                rden = fw.sbuf("rden", [128, 4], F32)
                obf = fw.sbuf("obf", [128, 4, 128], BF16)
                yst = [fw.sbuf("yst", [128, 512], BF16) for _ in range(2)]
                t_yst = [fw.tok("yst", dma=True) for _ in range(2)]
                t_den, t_rden, t_obf = fw.tok(), fw.tok(), fw.tok()
                def load_x(tq_, hx_, b_):
                    fw.dma(fw.sp, xq[b_][:], xqT[hx_][:, tq_ * 512:tq_ * 512 + 512], t_xq[b_], reads=[T["xqT"]], writes=[t_xq[b_]])
                    fw.dma(fw.sp, xz[b_][:], xzT[hx_ * 128:(hx_ + 1) * 128, tq_ * 512:tq_ * 512 + 512], t_xq[b_],
                           reads=[T["xzT"]], writes=[t_xq[b_]])

                iters4 = [(tq_, hx_) for tq_ in range(4) for hx_ in range(4)]
                load_x(0, 0, 0)
                for it, (tq, hx) in enumerate(iters4):
                    if True:
                        t0 = tq * 512
                        b = it % 2
                        if it + 1 < len(iters4):
                            load_x(iters4[it + 1][0], iters4[it + 1][1], (it + 1) % 2)
                        for mt in range(2):
                            fw.op(fw.pe, lambda: nc.tensor.matmul(pss[mt][:], lhsT=mK[:, hx, mt * 128:(mt + 1) * 128], rhs=xq[b][:],
                                                                  start=True, stop=True),
                                  reads=[t_mK, t_xq[b]], writes=[t_pss[mt]])
                            e_ = b * 2 + mt
                            fw.op(fw.act, lambda: nc.scalar.activation(out=Em[e_][:], in_=pss[mt][:], func=AF.Exp, scale=SCALE),
                                  reads=[t_pss[mt]], writes=[t_Em[e_]])
                        pair = b
                        for qs in range(4):
                            bank = pair * 2 + qs // 2
                            off = (qs % 2) * 129
                            for mt in range(2):
                                e_ = b * 2 + mt
                                fw.op(fw.pe, lambda: nc.tensor.matmul(po[bank][:, off:off + 129], lhsT=Em[e_][:, qs * 128:(qs + 1) * 128],
                                                                      rhs=mV[:, mt, hx, :], start=(mt == 0), stop=(mt == 1)),
                                      reads=[t_Em[e_], t_mV], writes=[t_po[bank]])
                        for qs in range(4):
                            bank = pair * 2 + qs // 2
                            off = (qs % 2) * 129
                            fw.op(fw.dve, lambda: nc.vector.tensor_scalar_max(out=den[:, qs:qs + 1], in0=po[bank][:, off + 128:off + 129],
                                                                              scalar1=1e-30), reads=[t_po[bank]], writes=[t_den])
                        fw.op(fw.dve, lambda: nc.vector.reciprocal(out=rden[:], in_=den[:]), reads=[t_den], writes=[t_rden])
                        for qs in range(4):
                            bank = pair * 2 + qs // 2
                            off = (qs % 2) * 129
                            fw.op(fw.dve, lambda: nc.vector.tensor_scalar_mul(out=obf[:, qs, :], in0=po[bank][:, off:off + 128],
                                                                              scalar1=rden[:, qs:qs + 1]),
                                  reads=[t_po[bank], t_rden], writes=[t_obf])
                        for qs in range(4):
                            fw.op(fw.pe, lambda: nc.tensor.transpose(pst[:, qs * 128:(qs + 1) * 128], obf[:, qs, :], ident[:]),
                                  reads=[t_obf, t_const], writes=[t_pst])
                        fw.op(fw.dve, lambda: nc.vector.tensor_tensor(out=yst[b][:], in0=pst[:, 0:512], in1=xz[b][:], op=ALU.mult),
                              reads=[t_pst, t_xq[b]], writes=[t_yst[b]])
                        fw.dma(fw.sp, yxT[hx * 128:(hx + 1) * 128, t0:t0 + 512], yst[b][:], t_yst[b], reads=[t_yst[b]], writes=[T["yxT"]])

            for tq in range(S // TT):
                t0 = tq * TT
                with fw.scope():
                    ya = fw.sbuf("ya", [128, 16, TT], BF16)
                    yb = fw.sbuf("yb", [128, 16, TT], BF16)
                    yx = fw.sbuf("yx", [128, 4, TT], BF16)
                    t_y = fw.tok("y", dma=True, multi=True)
                    fw.dma(fw.sp, ya[:], yaT[:, t0:t0 + TT].rearrange("(k p) t -> p k t", p=128), t_y, reads=[T["yaT"]], writes=[t_y])
                    fw.dma(fw.sp, yb[:], ybT[:, t0:t0 + TT].rearrange("(k p) t -> p k t", p=128), t_y, reads=[T["ybT"]], writes=[t_y])
                    fw.dma(fw.sp, yx[:], yxT[:, t0:t0 + TT].rearrange("(k p) t -> p k t", p=128), t_y, reads=[T["yxT"]], writes=[t_y])
                    ws = WStream()
                    pa = [fw.psum("pa", [128, 512], F32) for _ in range(6)]
                    t_pa = [fw.tok("pa") for _ in range(6)]
                    NG = 5
                    gt = [fw.sbuf("gt", [128, 3, TT], BF16) for _ in range(NG)]
                    t_gt = [fw.tok("gt", dma=True) for _ in range(NG)]
                    ut = fw.sbuf("ut", [128, 4, TT], F32)
                    t_ut = [fw.tok("ut") for _ in range(4)]
                    u2 = fw.sbuf("u2", [128, 512], F32)
                    t_u2 = fw.tok("u2")
                    ust = [fw.sbuf("ust", [128, TT], BF16) for _ in range(2)]
                    t_ust = [fw.tok("ust", dma=True) for _ in range(2)]
                    cn5 = {"pa": 0, "gt": 0, "us": 0}
                    gslot = {}

                    def npa():
                        p = cn5["pa"] % 6
                        cn5["pa"] += 1
                        return p

                    def hA(dg):
                        def h(wtb, t_wtb):
                            for m in range(4):
                                dc = dg * 4 + m
                                gsl = cn5["gt"] % NG
                                cn5["gt"] += 1
                                gslot[dc] = gsl
                                fw.dma(fw.sp, gt[gsl][:], mgT[:, dc * 128:(dc + 1) * 128, t0:t0 + TT].rearrange("i p t -> p i t"),
                                       t_gt[gsl], reads=[T["mgT"]], writes=[t_gt[gsl]])
                                for th in range(2):
                                    sl = slice(th * 512, (th + 1) * 512)
                                    pA = npa()
                                    for kc in range(16):
                                        fw.op(fw.pe, lambda: nc.tensor.matmul(pa[pA][:], lhsT=wtb[:, kc, m * 128:(m + 1) * 128], rhs=ya[:, kc, sl],
                                                                              start=(kc == 0), stop=(kc == 15)),
                                              reads=[t_wtb, t_y], writes=[t_pa[pA]], sig=(kc == 15))
                                    pX = npa()
                                    for kc in range(4):
                                        fw.op(fw.pe, lambda: nc.tensor.matmul(pa[pX][:], lhsT=wtb[:, 16 + kc, m * 128:(m + 1) * 128], rhs=yx[:, kc, sl],
                                                                              start=(kc == 0), stop=(kc == 3)),
                                              reads=[t_wtb, t_y], writes=[t_pa[pX]], sig=(kc == 3))
                                    fw.op(fw.dve, lambda: nc.vector.tensor_tensor(out=ut[:, m, sl], in0=pa[pA][:], in1=gt[gsl][:, 0, sl], op=ALU.mult),
                                          reads=[t_pa[pA], t_gt[gsl]], writes=[t_ut[m]])
                                    fw.op(fw.dve, lambda: nc.vector.tensor_tensor(out=u2[:], in0=pa[pX][:], in1=gt[gsl][:, 2, sl], op=ALU.mult),
                                          reads=[t_pa[pX], t_gt[gsl]], writes=[t_u2])
                                    fw.op(fw.dve, lambda: nc.vector.tensor_tensor(out=ut[:, m, sl], in0=ut[:, m, sl], in1=u2[:], op=ALU.add),
                                          reads=[t_ut[m], t_u2], writes=[t_ut[m]])
                        return h

                    def hB(dg):
                        def h(wtb, t_wtb):
                            for m in range(4):
                                dc = dg * 4 + m
                                gsl = gslot[dc]
                                us = cn5["us"] % 2
                                cn5["us"] += 1
                                for th in range(2):
                                    sl = slice(th * 512, (th + 1) * 512)
                                    pB = npa()
                                    for kc in range(16):
                                        fw.op(fw.pe, lambda: nc.tensor.matmul(pa[pB][:], lhsT=wtb[:, kc, m * 128:(m + 1) * 128], rhs=yb[:, kc, sl],
                                                                              start=(kc == 0), stop=(kc == 15)),
                                              reads=[t_wtb, t_y], writes=[t_pa[pB]], sig=(kc == 15))
                                    fw.op(fw.dve, lambda: nc.vector.tensor_tensor(out=u2[:], in0=pa[pB][:], in1=gt[gsl][:, 1, sl], op=ALU.mult),
                                          reads=[t_pa[pB], t_gt[gsl]], writes=[t_u2])
                                    fw.op(fw.dve, lambda: nc.vector.tensor_tensor(out=ust[us][:, sl], in0=ut[:, m, sl], in1=u2[:], op=ALU.add),
                                          reads=[t_ut[m], t_u2], writes=[t_ust[us]])
                                fw.dma(fw.sp, uT_s[dc * 128:(dc + 1) * 128, t0:t0 + TT], ust[us][:], t_ust[us],
                                       reads=[t_ust[us]], writes=[T["uT_s"]])
                        return h

                    groups = []
                    for dg in range(8):
                        c0 = dg * 512
                        groups.append(([(wua_in[l][:, c0:c0 + 512], 0, 16, 0, 512), (wux_in[l][:, c0:c0 + 512], 16, 4, 0, 512)], hA(dg)))
                        groups.append(([(wub_in[l][:, c0:c0 + 512], 0, 16, 0, 512)], hB(dg)))
                    ws.run(groups)

            for tq in range(S // TT):
                t0 = tq * TT
                with fw.scope():
                    uT = fw.sbuf("uT", [128, KC, TT], BF16)
                    t_uT = fw.tok("uT", dma=True, multi=True)
                    for k4 in range(0, KC, 8):
                        fw.dma(fw.sp, uT[:, k4:k4 + 8, :], uT_s[k4 * 128:(k4 + 8) * 128, t0:t0 + TT].rearrange("(k p) t -> p k t", p=128),
                               t_uT, reads=[T["uT_s"]], writes=[t_uT])
                    ws = WStream()
                    pa = [fw.psum("pa", [128, 512], F32) for _ in range(4)]
                    t_pa = [fw.tok("pa") for _ in range(4)]
                    xres = [fw.sbuf("xres", [128, TT], F32) for _ in range(2)]
                    t_xres = [fw.tok("xres", dma=True) for _ in range(2)]
                    xo = [fw.sbuf("xo", [128, TT], F32) for _ in range(2)]
                    t_xo = [fw.tok("xo", dma=True) for _ in range(2)]
                    cn6 = {"pa": 0, "x": 0}

                    def hO(dg):
                        def h(wtb, t_wtb):
                            for m in range(4):
                                dc = dg * 4 + m
                                xb_ = cn6["x"] % 2
                                cn6["x"] += 1
                                fw.dma(fw.sp, xres[xb_][:], xT_s[dc, :, t0:t0 + TT], t_xres[xb_], reads=[T["xT_s"]], writes=[t_xres[xb_]])
                                for th in range(2):
                                    sl = slice(th * 512, (th + 1) * 512)
                                    pD = cn6["pa"] % 4
                                    cn6["pa"] += 1
                                    for kc in range(KC):
                                        fw.op(fw.pe, lambda: nc.tensor.matmul(pa[pD][:], lhsT=wtb[:, kc, m * 128:(m + 1) * 128], rhs=uT[:, kc, sl],
                                                                              start=(kc == 0), stop=(kc == KC - 1)),
                                              reads=[t_wtb, t_uT], writes=[t_pa[pD]], sig=(kc == KC - 1))
                                    fw.op(fw.dve, lambda: nc.vector.tensor_tensor(out=xo[xb_][:, sl], in0=pa[pD][:], in1=xres[xb_][:, sl], op=ALU.add),
                                          reads=[t_pa[pD], t_xres[xb_]], writes=[t_xo[xb_]])
                                fw.dma(fw.sp, xT_s[dc, :, t0:t0 + TT], xo[xb_][:], t_xo[xb_], reads=[t_xo[xb_]], writes=[T["xT_s"]])
                        return h

                    ws.run([([(wo_in[l][:, dg * 512:dg * 512 + 512], 0, KC, 0, 512)], hO(dg)) for dg in range(8)])

    norm_phase(xT_s, T["xT_s"], fgT_in, S, dst_dram=out, dst_tok=T["out"])
    fw.barrier()
    n_instr = fw.n_instr
    fw.close()
    return nc, n_instr


def _consts():
    bf = ml_dtypes.bfloat16
    half = DH // 2
    inv_freq = (np.float32(10000.0) ** (-np.arange(half, dtype=np.float32) / np.float32(half))).astype(np.float32)
    ang = np.arange(S, dtype=np.float32)[None, :] * inv_freq[:, None]
    cos = np.cos(ang).astype(np.float32)
    sin = np.sin(ang).astype(np.float32)
    c = {}
    c["c_cos"] = np.ascontiguousarray(np.concatenate([cos, cos], 0))
    c["c_sin"] = np.ascontiguousarray(np.concatenate([-sin, sin], 0))
    c["c_ident"] = np.eye(128, dtype=np.float32).astype(bf)
    k = np.arange(128)[:, None]
    q = np.arange(128)[None, :]
    c["c_caus"] = np.ascontiguousarray(np.tile(np.where(k <= q, 0.0, NEGB).astype(np.float32).astype(bf), (1, 4)))
    c["c_wedge"] = np.ascontiguousarray(np.tile(np.where(k > q, 0.0, NEGB).astype(np.float32).astype(bf), (1, 4)))
    n = np.arange(128)[:, None, None]
    i = np.arange(16)[None, :, None]
    qq = np.arange(128)[None, None, :]
    t = 128 * i + qq
    c["c_cbias"] = np.ascontiguousarray(np.tile(np.where((16 * n + 31 <= t) & (n < 127), 0.0, NEGB).astype(np.float32).astype(bf), (1, 1, 4)))
    j = np.arange(128)[:, None, None]
    kt = np.arange(16)[None, :, None]
    key = np.arange(128)[None, None, :]
    c["c_expand"] = np.where((j == 2 * kt + key // 64) & (j < 32), -NEGB, 0.0).astype(np.float32).astype(bf)
    qv = np.arange(128)[:, None, None]
    iv = np.arange(16)[None, :, None]
    jv = np.arange(32)[None, None, :]
    tt = 128 * iv + qv
    cur = tt // 64
    future = jv > cur
    forced = (jv == 0) | (jv == cur) | (jv == cur - 1)
    c["c_selc"] = np.where(future, -10.0, np.where(forced, 10.0, 0.0)).astype(np.float32)
    nn = np.arange(128)[:, None]
    jj = np.arange(32)[None, :]
    ovl = ((16 * nn < 64 * jj + 64) & (16 * nn + 32 > 64 * jj) & (nn < 127))
    c["c_ovl"] = ovl.astype(np.float32).astype(bf)
    return c


def _prep_shared(inp, depth):
    f = lambda a: np.ascontiguousarray(np.asarray(a, dtype=np.float32))
    sh = {}
    sh["gT"] = f(np.asarray(inp["norm_g"])[:depth].reshape(depth, KC, 128).transpose(0, 2, 1))
    sh["mgnT"] = f(np.asarray(inp["mem_norm_g"])[:depth].reshape(depth, KC, 128).transpose(0, 2, 1))
    sh["fgT"] = f(np.asarray(inp["final_g"]).reshape(KC, 128).T)
    sh["w_in"] = f(np.asarray(inp["w_in"])[:depth])
    sh["cw"] = f(np.asarray(inp["conv_w"])[:depth].reshape(depth, 3, 16, 128).transpose(0, 3, 2, 1))
    sh["cb"] = f(np.asarray(inp["conv_b"])[:depth].reshape(depth, 16, 128).transpose(0, 2, 1))
    sh["posT"] = f(np.asarray(inp["cmp_pos"])[:depth].transpose(0, 1, 3, 2))
    sh["w1"] = f(np.asarray(inp["cmp_w1"])[:depth])
    sh["w2"] = f(np.asarray(inp["cmp_w2"])[:depth])
    sh["wmk"] = f(np.asarray(inp["w_mem_kv"])[:depth])
    sh["wua"] = f(np.asarray(inp["w_up_a"])[:depth])
    sh["wub"] = f(np.asarray(inp["w_up_b"])[:depth])
    sh["wux"] = f(np.asarray(inp["w_up_x"])[:depth])
    sh["wo"] = f(np.asarray(inp["w_out"])[:depth])
    sh.update(_consts())
    return sh


_CACHE = {}


def run(inp, depth=DEPTH, dbg=False, ncores=8, trace=False):
    key = (depth, dbg)
    if key not in _CACHE:
        _CACHE[key] = build(depth, dbg)
    nc, _ = _CACHE[key]
    sh = _prep_shared(inp, depth)
    x = np.asarray(inp["x"], dtype=np.float32)
    mem = np.asarray(inp["mem"], dtype=np.float32)
    B = x.shape[0]
    in_maps = []
    for c in range(ncores):
        b = c % B
        m = dict(sh)
        m["xT"] = np.ascontiguousarray(x[b].T).reshape(KC, 128, S)
        m["memT"] = np.ascontiguousarray(mem[b].T).reshape(KC, 128, MEM)
        in_maps.append(m)
    res = run_bass_kernel_spmd(nc, in_maps, core_ids=list(range(ncores)), trace=trace)
    return res


def kernel(**inputs):
    res = run(inputs)
    B = np.asarray(inputs["x"]).shape[0]
    outs = []
    for b in range(B):
        oT = np.asarray(res.results[b]["outT"]).reshape(D, S)
        outs.append(np.ascontiguousarray(oT.T))
    return np.stack(outs, 0).astype(np.float32)
```
